# Optimizing a Trainium2 kernel written in Bass

```python
import math
import jax, jax.numpy as jnp
from jax import lax
import numpy as np

D_MODEL = 2048
BATCH = 4
SEQ = 4096
DEPTH = 4

D_MIX = D_MODEL
D_FF = 2 * D_MODEL
NORM_EPS = 1e-6
GATED_NORM_EPS = 1e-5
Q_BLOCK = 128
ROPE_THETA = 500000.0
ROT_FRACTION = 4
FOX_HEADS = 4
FOX_HEAD_DIM = 128
FGATE_BIAS_INIT = 3.0
SSM_D_INNER = D_MIX // 2
SSM_HEAD_DIM = 64
SSM_HEADS = SSM_D_INNER // SSM_HEAD_DIM
SSM_GROUPS = 2
SSM_STATE = 128
SSM_CONV = 4
SSM_CHUNK = 128
SSM_CONV_CH = SSM_D_INNER + 2 * SSM_GROUPS * SSM_STATE
DSA_HEADS = 4
DSA_HEAD_DIM = 128
DSA_Q_LORA = 512
IDX_HEADS = 8
IDX_HEAD_DIM = 64
TOPK_MAX = 256
SPLIT_SIZES = (
    FOX_HEADS * FOX_HEAD_DIM,
    FOX_HEADS * FOX_HEAD_DIM,
    FOX_HEADS * FOX_HEAD_DIM,
    FOX_HEADS,
    SSM_D_INNER,
    SSM_CONV_CH,
    SSM_HEADS,
    DSA_Q_LORA,
    DSA_HEAD_DIM,
    DSA_HEAD_DIM,
    IDX_HEAD_DIM,
    IDX_HEADS,
)
N_IN = (3 * FOX_HEADS * FOX_HEAD_DIM + FOX_HEADS + SSM_D_INNER + SSM_CONV_CH + SSM_HEADS
        + DSA_Q_LORA + 2 * DSA_HEAD_DIM + IDX_HEAD_DIM + IDX_HEADS)

kernel_name = 'hybrid_fox_ssd_dsa_macaron_trunk'


def rms_norm(x, g, eps=NORM_EPS):
    xf = x.astype(jnp.float32)
    y = xf * lax.rsqrt(jnp.mean(xf * xf, axis=-1, keepdims=True) + eps)
    return (y * g.astype(jnp.float32)).astype(x.dtype)


def swiglu(h, w_gate, w_up, w_down):
    return (jax.nn.silu(h @ w_gate) * (h @ w_up)) @ w_down


def rope_cos_sin(seq_len, rot_dim):
    inv = ROPE_THETA ** (-jnp.arange(0, rot_dim, 2, dtype=jnp.float32) / rot_dim)
    ang = jnp.arange(seq_len, dtype=jnp.float32)[:, None] * inv[None, :]
    return jnp.cos(ang), jnp.sin(ang)


def apply_partial_rope(x, cos, sin):
    half = cos.shape[-1]
    rot = 2 * half
    xf = x.astype(jnp.float32)
    x1, x2 = xf[..., :half], xf[..., half:rot]
    c, s = cos[None, :, None, :], sin[None, :, None, :]
    out = jnp.concatenate([x1 * c - x2 * s, x2 * c + x1 * s, xf[..., rot:]], axis=-1)
    return out.astype(x.dtype)


def to_blocks(a, nb):
    return jnp.moveaxis(a.reshape(a.shape[0], nb, Q_BLOCK, *a.shape[2:]), 1, 0)


def from_blocks(a):
    a = jnp.moveaxis(a, 0, 1)
    return a.reshape(a.shape[0], a.shape[1] * a.shape[2], *a.shape[3:])


def fox_attention(q, k, v, log_f):
    B, S, H, D = q.shape
    nb = S // Q_BLOCK
    scale = D ** -0.5
    c = jnp.cumsum(log_f, axis=1).transpose(0, 2, 1)
    kpos = jnp.arange(S)

    def block(args):
        qb, cb, tb = args
        logits = (jnp.einsum('bqhd,bkhd->bhqk', qb, k).astype(jnp.float32) * scale
                  + cb[:, :, :, None] - c[:, :, None, :])
        causal = kpos[None, :] <= tb[:, None]
        logits = jnp.where(causal[None, None], logits, -jnp.inf)
        p = jax.nn.softmax(logits, axis=-1)
        return jnp.einsum('bhqk,bkhd->bqhd', p.astype(v.dtype), v)

    cb = c.reshape(B, H, nb, Q_BLOCK).transpose(2, 0, 1, 3)
    out = lax.map(block, (to_blocks(q, nb), cb, kpos.reshape(nb, Q_BLOCK)))
    return from_blocks(out)


def causal_depthwise_conv(x, w, b):
    C = x.shape[-1]
    y = lax.conv_general_dilated(x, w[:, None, :], window_strides=(1,),
                                 padding=[(w.shape[0] - 1, 0)],
                                 dimension_numbers=('NWC', 'WIO', 'NWC'),
                                 feature_group_count=C)
    return y + b


def ssd_chunked(x, a, b, c):
    Bsz, S, H, P = x.shape
    Lc = SSM_CHUNK
    nc = S // Lc
    rep = H // b.shape[2]
    N = b.shape[-1]
    b = jnp.repeat(b, rep, axis=2).reshape(Bsz, nc, Lc, H, N)
    c = jnp.repeat(c, rep, axis=2).reshape(Bsz, nc, Lc, H, N)
    x = x.reshape(Bsz, nc, Lc, H, P)
    a = a.reshape(Bsz, nc, Lc, H).transpose(0, 3, 1, 2)
    a_cs = jnp.cumsum(a, axis=-1)
    tril = jnp.tril(jnp.ones((Lc, Lc), dtype=bool))
    seg = jnp.exp(jnp.where(tril, a_cs[..., :, None] - a_cs[..., None, :], -jnp.inf))
    scores = jnp.einsum('bclhn,bcshn->bhcls', c, b) * seg
    y_diag = jnp.einsum('bhcls,bcshp->bclhp', scores, x)
    decay_states = jnp.exp(a_cs[..., -1:] - a_cs)
    states = jnp.einsum('bclhn,bhcl,bclhp->bchpn', b, decay_states, x)
    chunk_decay = jnp.exp(a_cs[..., -1])

    def step(h, inp):
        s_c, d_c = inp
        return h * d_c[:, :, None, None] + s_c, h

    h0 = jnp.zeros((Bsz, H, P, N), jnp.float32)
    _, h_in = lax.scan(step, h0, (jnp.moveaxis(states, 1, 0), jnp.moveaxis(chunk_decay, 2, 0)))
    h_in = jnp.moveaxis(h_in, 0, 1)
    y_off = jnp.einsum('bclhn,bchpn,bhcl->bclhp', c, h_in, jnp.exp(a_cs))
    return (y_diag + y_off).reshape(Bsz, S, H, P)


def gated_group_rmsnorm(y, z, g):
    B, S, Dn = y.shape
    u = (y.astype(jnp.float32) * jax.nn.silu(z.astype(jnp.float32))).reshape(B, S, SSM_GROUPS, Dn // SSM_GROUPS)
    u = u * lax.rsqrt(jnp.mean(u * u, axis=-1, keepdims=True) + GATED_NORM_EPS)
    return (u.reshape(B, S, Dn) * g.astype(jnp.float32)).astype(z.dtype)


def mamba2_mixer(z, xbc, dt_raw, conv_w, conv_b, dt_bias, a_log, d_skip, norm_g):
    B, S, _ = xbc.shape
    xbc = jax.nn.silu(causal_depthwise_conv(xbc, conv_w, conv_b))
    xs, bm, cm = jnp.split(xbc, [SSM_D_INNER, SSM_D_INNER + SSM_GROUPS * SSM_STATE], axis=-1)
    xs = xs.reshape(B, S, SSM_HEADS, SSM_HEAD_DIM).astype(jnp.float32)
    bm = bm.reshape(B, S, SSM_GROUPS, SSM_STATE).astype(jnp.float32)
    cm = cm.reshape(B, S, SSM_GROUPS, SSM_STATE).astype(jnp.float32)
    dt = jax.nn.softplus(dt_raw.astype(jnp.float32) + dt_bias.astype(jnp.float32))
    a = -jnp.exp(a_log.astype(jnp.float32))
    y = ssd_chunked(xs * dt[..., None], dt * a, bm, cm)
    y = y + xs * d_skip.astype(jnp.float32)[:, None]
    return gated_group_rmsnorm(y.reshape(B, S, SSM_D_INNER), z, norm_g)


def dsa_attention(q, k, v, q_idx, k_idx, w_idx, topk):
    B, S, H, D = q.shape
    nb = S // Q_BLOCK
    scale = D ** -0.5
    w_scale = (IDX_HEADS ** -0.5) * (IDX_HEAD_DIM ** -0.5)
    kpos = jnp.arange(S)
    k_idx_f = k_idx.astype(jnp.float32)

    def block(args):
        qb, qib, wb, tb = args
        sc = jnp.einsum('bqhe,bke->bqhk', qib.astype(jnp.float32), k_idx_f)
        idx_score = jnp.einsum('bqh,bqhk->bqk', wb.astype(jnp.float32) * w_scale, jax.nn.relu(sc))
        causal = kpos[None, :] <= tb[:, None]
        idx_score = jnp.where(causal[None], idx_score, -jnp.inf)
        _, sel = lax.top_k(idx_score, topk)
        k_sel = jax.vmap(lambda kb, ib: kb[ib])(k, sel)
        v_sel = jax.vmap(lambda vb, ib: vb[ib])(v, sel)
        logits = jnp.einsum('bqhd,bqkd->bhqk', qb, k_sel).astype(jnp.float32) * scale
        valid = sel <= tb[None, :, None]
        logits = jnp.where(valid[:, None], logits, -jnp.inf)
        p = jax.nn.softmax(logits, axis=-1)
        return jnp.einsum('bhqk,bqkd->bqhd', p.astype(v.dtype), v_sel)

    out = lax.map(block, (to_blocks(q, nb), to_blocks(q_idx, nb), to_blocks(w_idx, nb),
                          kpos.reshape(nb, Q_BLOCK)))
    return from_blocks(out)


def setup_inputs(seed: int = 0) -> dict:
    key = jax.random.key(seed)
    ks = iter(jax.random.split(key, 40))
    L = DEPTH

    def nrm(shape, scale):
        return jax.random.normal(next(ks), shape, jnp.float32) * scale

    def gain(shape):
        return 1.0 + nrm(shape, 0.02)

    dt0 = jnp.exp(jax.random.uniform(next(ks), (L, SSM_HEADS), jnp.float32,
                                     minval=math.log(1e-3), maxval=math.log(1e-1)))
    dt_bias = dt0 + jnp.log(-jnp.expm1(-dt0))
    a_log = jnp.log(jax.random.uniform(next(ks), (L, SSM_HEADS), jnp.float32, minval=1.0, maxval=16.0))
    return {
        'x': nrm((BATCH, SEQ, D_MODEL), 1.0),
        'ffn1_norm': gain((L, D_MODEL)),
        'ffn1_w_gate': nrm((L, D_MODEL, D_FF), D_MODEL ** -0.5),
        'ffn1_w_up': nrm((L, D_MODEL, D_FF), D_MODEL ** -0.5),
        'ffn1_w_down': nrm((L, D_FF, D_MODEL), D_FF ** -0.5),
        'mix_norm': gain((L, D_MODEL)),
        'w_in': nrm((L, D_MODEL, N_IN), D_MODEL ** -0.5),
        'fox_fgate_b': FGATE_BIAS_INIT + nrm((L, FOX_HEADS), 0.5),
        'fox_q_norm': gain((L, FOX_HEAD_DIM)),
        'fox_k_norm': gain((L, FOX_HEAD_DIM)),
        'ssm_conv_w': nrm((L, SSM_CONV, SSM_CONV_CH), SSM_CONV ** -0.5),
        'ssm_conv_b': nrm((L, SSM_CONV_CH), 0.02),
        'ssm_dt_bias': dt_bias,
        'ssm_a_log': a_log,
        'ssm_d': gain((L, SSM_HEADS)),
        'ssm_norm': gain((L, SSM_D_INNER)),
        'dsa_cq_norm': gain((L, DSA_Q_LORA)),
        'dsa_w_uq': nrm((L, DSA_Q_LORA, DSA_HEADS * DSA_HEAD_DIM), DSA_Q_LORA ** -0.5),
        'dsa_w_uq_idx': nrm((L, DSA_Q_LORA, IDX_HEADS * IDX_HEAD_DIM), DSA_Q_LORA ** -0.5),
        'dsa_q_norm': gain((L, DSA_HEAD_DIM)),
        'dsa_k_norm': gain((L, DSA_HEAD_DIM)),
        'dsa_kidx_norm': gain((L, IDX_HEAD_DIM)),
        'w_out': nrm((L, D_MIX, D_MODEL), 0.5 * D_MIX ** -0.5),
        'ffn2_norm': gain((L, D_MODEL)),
        'ffn2_w_gate': nrm((L, D_MODEL, D_FF), D_MODEL ** -0.5),
        'ffn2_w_up': nrm((L, D_MODEL, D_FF), D_MODEL ** -0.5),
        'ffn2_w_down': nrm((L, D_FF, D_MODEL), D_FF ** -0.5),
    }


def reference(x, ffn1_norm, ffn1_w_gate, ffn1_w_up, ffn1_w_down, mix_norm, w_in,
              fox_fgate_b, fox_q_norm, fox_k_norm, ssm_conv_w, ssm_conv_b, ssm_dt_bias,
              ssm_a_log, ssm_d, ssm_norm, dsa_cq_norm, dsa_w_uq, dsa_w_uq_idx, dsa_q_norm,
              dsa_k_norm, dsa_kidx_norm, w_out, ffn2_norm, ffn2_w_gate, ffn2_w_up, ffn2_w_down):
    Bsz, S, _ = x.shape
    topk = min(TOPK_MAX, S // 4)
    offs = np.cumsum(np.array(SPLIT_SIZES))[:-1].tolist()
    cos_a, sin_a = rope_cos_sin(S, DSA_HEAD_DIM // ROT_FRACTION)
    cos_i, sin_i = rope_cos_sin(S, IDX_HEAD_DIM // ROT_FRACTION)

    for l in range(DEPTH):
        x = x + 0.5 * swiglu(rms_norm(x, ffn1_norm[l]), ffn1_w_gate[l], ffn1_w_up[l], ffn1_w_down[l])

        h = rms_norm(x, mix_norm[l])
        proj = h @ w_in[l]
        (f_q, f_k, f_v, f_g, m_z, m_xbc, m_dt,
         d_cq, d_k, d_v, d_ki, d_wi) = jnp.split(proj, offs, axis=-1)

        fq = rms_norm(f_q.reshape(Bsz, S, FOX_HEADS, FOX_HEAD_DIM), fox_q_norm[l])
        fk = rms_norm(f_k.reshape(Bsz, S, FOX_HEADS, FOX_HEAD_DIM), fox_k_norm[l])
        fv = f_v.reshape(Bsz, S, FOX_HEADS, FOX_HEAD_DIM)
        log_f = jax.nn.log_sigmoid(f_g.astype(jnp.float32) + fox_fgate_b[l].astype(jnp.float32))
        fox_o = fox_attention(fq, fk, fv, log_f)

        ssm_o = mamba2_mixer(m_z, m_xbc, m_dt, ssm_conv_w[l], ssm_conv_b[l], ssm_dt_bias[l],
                             ssm_a_log[l], ssm_d[l], ssm_norm[l])

        cq = rms_norm(d_cq, dsa_cq_norm[l])
        dq = rms_norm((cq @ dsa_w_uq[l]).reshape(Bsz, S, DSA_HEADS, DSA_HEAD_DIM), dsa_q_norm[l])
        dq = apply_partial_rope(dq, cos_a, sin_a)
        dk = apply_partial_rope(rms_norm(d_k[:, :, None, :], dsa_k_norm[l]), cos_a, sin_a)[:, :, 0]
        qi = apply_partial_rope((cq @ dsa_w_uq_idx[l]).reshape(Bsz, S, IDX_HEADS, IDX_HEAD_DIM), cos_i, sin_i)
        ki = apply_partial_rope(rms_norm(d_ki[:, :, None, :], dsa_kidx_norm[l]), cos_i, sin_i)[:, :, 0]
        dsa_o = dsa_attention(dq, dk, d_v, qi, ki, d_wi, topk)

        mixed = jnp.concatenate([fox_o.reshape(Bsz, S, -1), ssm_o, dsa_o.reshape(Bsz, S, -1)], axis=-1)
        x = x + mixed @ w_out[l]

        x = x + 0.5 * swiglu(rms_norm(x, ffn2_norm[l]), ffn2_w_gate[l], ffn2_w_up[l], ffn2_w_down[l])
    return x
```

```python
import numpy as np
from contextlib import ExitStack
import concourse.bass as bass
import concourse.mybir as mybir
from concourse.bass_utils import run_bass_kernel_spmd

F32 = mybir.dt.float32
BF16 = mybir.dt.bfloat16
AF = mybir.ActivationFunctionType
ALU = mybir.AluOpType
AX = mybir.AxisListType

ENGINES = ['tensor', 'vector', 'scalar', 'gpsimd', 'sync']
_SFX = ['']
_DIN = {}


def _sbt(nc, name, shape, dt):
    return nc.sbuf_tensor(name + _SFX[0], shape, dt)


def _pst(nc, name, shape, dt):
    return nc.psum_tensor(name + _SFX[0], shape, dt)


def _din(nc, name, shape):
    if name not in _DIN:
        _DIN[name] = nc.dram_tensor(name, shape, F32, kind="ExternalInput").ap()
    return _DIN[name]
INORDER_SAFE = {'tensor'}
SEM_LIMIT = 30000
NDMASEM = 6

D_MODEL = 2048
D_FF = 4096
N_IN = 4956
N_INP = 4992
NCORE = 8
TOK = 2048
TT = 1024


class Buf:
    __slots__ = ('name', 'w', 'r', 'rd')

    def __init__(self, name=''):
        self.name = name
        self.w = None
        self.r = {}
        self.rd = []


class Op:
    __slots__ = ('eng', 'fn', 'deps', 'dma', 'ms', 'ev')

    def __init__(self, eng, fn, dma):
        self.eng = eng
        self.fn = fn
        self.dma = dma
        self.deps = []
        self.ms = False
        self.ev = None


class Prog:
    def __init__(self, nc):
        self.nc = nc
        self.ops = {e: [] for e in ENGINES}
        self.dmaops = {e: [] for e in ENGINES}
        self.nbuf = 0
        self.clear_sems = False
        self.GS = None

    def buf(self, name=''):
        self.nbuf += 1
        return Buf(name or f'b{self.nbuf}')

    def add(self, eng, fn, reads=(), writes=(), dma=False):
        op = Op(eng, fn, dma)
        deps = {}

        def need(d, war=False):
            if d is None or d is op:
                return
            if (not d.dma) and (not dma) and d.eng == eng:
                if eng in INORDER_SAFE:
                    return
            deps[id(d)] = d

        for b in reads:
            need(b.w)
        for b in writes:
            need(b.w)
            for r in b.r.values():
                need(r, war=True)
            for r in b.rd:
                need(r)
        if dma:
            lst = self.dmaops[eng]
            if len(lst) >= NDMASEM:
                need(lst[len(lst) - NDMASEM])
            lst.append(op)
        op.deps = list(deps.values())
        for b in reads:
            if dma:
                b.rd.append(op)
            else:
                b.r[eng] = op
        for b in writes:
            b.w = op
            b.r = {}
            b.rd = []
        self.ops[eng].append(op)
        return op

    def finish(self, reads=()):
        op = self.add('sync', None, reads=reads)
        have = {id(d) for d in op.deps}
        for e in ENGINES:
            for d in self.dmaops[e][-NDMASEM:]:
                if id(d) not in have:
                    op.deps.append(d)
            if e != 'sync':
                last = [o for o in self.ops[e] if not o.dma and o.fn is not None]
                if last and id(last[-1]) not in have:
                    op.deps.append(last[-1])
        return op

    def mm(self, out, lhsT, rhs, start=True, stop=True, reads=(), writes=(), **kw):
        return self.add('tensor', lambda e: e.matmul(out, lhsT, rhs, start=start, stop=stop, **kw),
                        reads, writes)

    def dma(self, q, out, in_, reads=(), writes=(), **kw):
        return self.add(q, lambda e: e.dma_start(out=out, in_=in_, **kw), reads, writes, dma=True)

    def V(self, fn, reads=(), writes=()):
        return self.add('vector', fn, reads, writes)

    def S(self, fn, reads=(), writes=()):
        return self.add('scalar', fn, reads, writes)

    def G(self, fn, reads=(), writes=()):
        return self.add('gpsimd', fn, reads, writes)

    def emit(self, stack):
        nc = self.nc
        for e in ENGINES:
            for op in self.ops[e]:
                for d in op.deps:
                    if not d.dma:
                        d.ms = True
        G = self.GS if self.GS is not None else {'sems': {}, 'cnt': {e: 0 for e in ENGINES}, 'dcnt': {e: 0 for e in ENGINES}, 'stack': stack}
        semcache = G['sems']

        def getsem(name):
            if name not in semcache:
                semcache[name] = G['stack'].enter_context(nc.semaphore(name))
            return semcache[name]

        DLIM = SEM_LIMIT // 16
        for e in ENGINES:
            cnt = G['cnt'][e]
            for op in self.ops[e]:
                if op.dma:
                    continue
                if op.ms:
                    ep, v = divmod(cnt, SEM_LIMIT)
                    op.ev = (getsem(f'p_{e}_{ep}'), v + 1)
                    cnt += 1
            G['cnt'][e] = cnt
            base = G['dcnt'][e]
            for i, op in enumerate(self.dmaops[e]):
                gi = base + i
                ep, v = divmod(gi // NDMASEM, DLIM)
                op.ev = (getsem(f'd_{e}_{gi % NDMASEM}_{ep}'), 16 * (v + 1))
            nd = base + len(self.dmaops[e])
            G['dcnt'][e] = nd
        block = stack.enter_context(nc.Block())
        prog = self

        def run(e, eng):
            known = {}
            for op in prog.ops[e]:
                w = {}
                for d in op.deps:
                    s, v = d.ev
                    k = id(s)
                    if known.get(k, 0) >= v:
                        continue
                    if k not in w or w[k][1] < v:
                        w[k] = (s, v)
                for k, (s, v) in w.items():
                    eng.wait_ge(s, v)
                    known[k] = v
                if op.fn is None:
                    continue
                inst = op.fn(eng)
                if op.dma:
                    inst.then_inc(op.ev[0], 16)
                elif op.ms:
                    inst.then_inc(op.ev[0], 1)

        if self.ops['sync']:
            @block.sync
            def _(eng):
                run('sync', eng)
        if self.ops['tensor']:
            @block.tensor
            def _(eng):
                run('tensor', eng)
        if self.ops['vector']:
            @block.vector
            def _(eng):
                run('vector', eng)
        if self.ops['scalar']:
            @block.scalar
            def _(eng):
                run('scalar', eng)
        if self.ops['gpsimd']:
            @block.gpsimd
            def _(eng):
                run('gpsimd', eng)


class Pool:
    def __init__(self, P, st, nc, name, n, shape, dtype, psum=False):
        self.tiles = []
        self.bufs = []
        self.i = 0
        for i in range(n):
            if psum:
                t = st.enter_context(_pst(nc, f'{name}{i}', shape, dtype))
            else:
                t = st.enter_context(_sbt(nc, f'{name}{i}', shape, dtype))
            self.tiles.append(t)
            self.bufs.append(P.buf(f'{name}{i}'))

    def next(self):
        i = self.i % len(self.tiles)
        self.i += 1
        return self.tiles[i], self.bufs[i]


class TCtx:
    pass


def t_setup(nc, P, st):
    C = TCtx()
    C.nc, C.P = nc, P
    C.hT = st.enter_context(_sbt(nc, 'hT', [128, 16, TT], BF16))
    C.hT_b = [P.buf(f'hT{k}') for k in range(16)]
    C.actT = st.enter_context(_sbt(nc, 'actT', [128, 32, TT], BF16))
    C.act_b = {(f, t): P.buf(f'act{f}_{t}') for f in range(32) for t in range(TT // 512)}
    C.wslots = Pool(P, st, nc, 'wsl', 3, [128, 8192], BF16)
    C.xs = Pool(P, st, nc, 'xs', 3, [128, TT], F32)
    C.sq = Pool(P, st, nc, 'sq', 2, [128, TT], F32)
    C.xo = Pool(P, st, nc, 'xo', 3, [128, 512], F32)
    C.xr = Pool(P, st, nc, 'xr', 3, [128, 512], F32)
    C.sg = Pool(P, st, nc, 'sg', 3, [128, 512], F32)
    C.rstd = st.enter_context(_sbt(nc, 'rstd', [128, TT], F32))
    C.rstd_b = P.buf('rstd')
    C.ones = st.enter_context(_sbt(nc, 'onesf', [128, 128], F32))
    C.ones_b = P.buf('ones')
    C.ps = Pool(P, st, nc, 'ps', 8, [128, 512], F32, psum=True)
    P.V(lambda e: e.memset(C.ones[:], 1.0 / D_MODEL), writes=[C.ones_b])
    C.eps_t = st.enter_context(_sbt(nc, 'eps_t', [128, 1], F32))
    P.V(lambda e: e.memset(C.eps_t[:], 1e-6), writes=[C.ones_b])
    C.cp = 0
    return C


def t_norm(C, x_dram, xbufs, t0, g_tile, g_buf, eps=1e-6):
    P = C.P
    nt = TT // 512
    pss = [C.ps.next() for _ in range(nt)]
    for c in range(16):
        xs, xsb = C.xs.next()
        P.dma('sync', xs[:], x_dram[c * 128:(c + 1) * 128, t0:t0 + TT], reads=[xbufs[c]], writes=[xsb])
        sq, sqb = C.sq.next()
        P.S(lambda e, sq=sq, xs=xs: e.activation(sq[:], xs[:], AF.Square), reads=[xsb], writes=[sqb])
        for t in range(nt):
            ps, psb = pss[t]
            P.mm(ps[:], C.ones[:], sq[:, t * 512:(t + 1) * 512], start=(c == 0), stop=(c == 15),
                 reads=[C.ones_b, sqb], writes=[psb])
    for t in range(nt):
        ps, psb = pss[t]
        sg, sgb = C.sg.next()
        P.S(lambda e, ps=ps, sg=sg: e.activation(sg[:], ps[:], AF.Ln, bias=C.eps_t[:, 0:1]), reads=[psb, C.ones_b], writes=[sgb])
        P.S(lambda e, sg=sg, t=t: e.activation(C.rstd[:, t * 512:(t + 1) * 512], sg[:], AF.Exp, scale=-0.5),
            reads=[sgb], writes=[C.rstd_b])
    for c in range(16):
        xs, xsb = C.xs.next()
        P.dma('sync', xs[:], x_dram[c * 128:(c + 1) * 128, t0:t0 + TT], reads=[xbufs[c]], writes=[xsb])
        P.V(lambda e, xs=xs, c=c: e.scalar_tensor_tensor(C.hT[:, c, :], xs[:], g_tile[:, c:c + 1], C.rstd[:],
                                                         ALU.mult, ALU.mult),
            reads=[xsb, C.rstd_b, g_buf], writes=[C.hT_b[c]])


def t_loadT(C, src_dram, t0):
    P = C.P
    for c in range(16):
        P.dma('gpsimd', C.hT[:, c, :], src_dram[c * 128:(c + 1) * 128, t0:t0 + TT], writes=[C.hT_b[c]])


def t_wjob(C, Ws, nk, c0, cw, CB):
    P = C.P
    slot, _ = C.wslots.next()
    idx = (C.wslots.i - 1) % len(C.wslots.tiles)
    if not hasattr(C, 'wpb'):
        C.wpb = {}
    RG = 1024

    def reg(col):
        key = (idx, col // RG)
        if key not in C.wpb:
            C.wpb[key] = P.buf(f'w{key}')
        return C.wpb[key]
    views = []
    KP = 4
    for wi, W in enumerate(Ws):
        base = wi * nk * CB
        v = slot[:, base:base + nk * CB].rearrange("p (k c) -> p k c", c=CB)
        views.append(v)
        for k0 in range(0, nk, KP):
            a, b = base + k0 * CB, base + (k0 + KP) * CB
            wr = [reg(c) for c in range(a, b, RG)]
            src = W[k0 * 128:(k0 + KP) * 128, c0:c0 + cw].rearrange("(k p) c -> p k c", p=128)
            P.dma('gpsimd', v[:, k0:k0 + KP, :cw], src, writes=wr)

    class PB:
        def __init__(self, wi):
            self.wi = wi

        def __getitem__(self, kk):
            return None
    rb = lambda wi, k: reg(wi * nk * CB + k * CB)
    return views, rb, KP


def t_ffn_gu(C, wg, wu):
    P = C.P
    nt = TT // 512
    CB = 256
    for cb in range(D_FF // CB):
        views, pbufs, KP = t_wjob(C, [wg, wu], 16, cb * CB, CB, CB)
        for m in range(CB // 128):
            f = cb * (CB // 128) + m
            for t in range(nt):
                pg, pgb = C.ps.next()
                pu, pub = C.ps.next()
                for wi, (ps, psb) in enumerate([(pg, pgb), (pu, pub)]):
                    for k in range(16):
                        P.mm(ps[:], views[wi][:, k, m * 128:(m + 1) * 128], C.hT[:, k, t * 512:(t + 1) * 512],
                             start=(k == 0), stop=(k == 15),
                             reads=[pbufs(wi, k), C.hT_b[k]], writes=[psb])
                sg, sgb = C.sg.next()
                P.S(lambda e, sg=sg, pg=pg: e.activation(sg[:], pg[:], AF.Silu), reads=[pgb], writes=[sgb])
                P.V(lambda e, sg=sg, pu=pu, f=f, t=t: e.tensor_tensor(
                    C.actT[:, f, t * 512:(t + 1) * 512], sg[:], pu[:], ALU.mult),
                    reads=[sgb, pub], writes=[C.act_b[(f, t)]])


def t_mm_resid(C, W, nk, rhs, rhs_bufs, x_dram, xbufs, o_dram, obufs, t0, fac):
    P = C.P
    nt = TT // 512
    CB = 8192 // nk
    for cb in range(D_MODEL // CB):
        views, pbufs, KP = t_wjob(C, [W], nk, cb * CB, CB, CB)
        for m in range(CB // 128):
            c = cb * (CB // 128) + m
            for t in range(nt):
                ps, psb = C.ps.next()
                for k in range(nk):
                    P.mm(ps[:], views[0][:, k, m * 128:(m + 1) * 128], rhs[:, k, t * 512:(t + 1) * 512],
                         start=(k == 0), stop=(k == nk - 1),
                         reads=[pbufs(0, k), rhs_bufs(k, t)], writes=[psb])
                xr, xrb = C.xr.next()
                sl = slice(t0 + t * 512, t0 + (t + 1) * 512)
                P.dma('sync', xr[:], x_dram[c * 128:(c + 1) * 128, sl], reads=[xbufs[c]], writes=[xrb])
                xo, xob = C.xo.next()
                P.V(lambda e, xo=xo, ps=ps, xr=xr: e.scalar_tensor_tensor(xo[:], ps[:], fac, xr[:], ALU.mult, ALU.add),
                    reads=[psb, xrb], writes=[xob])
                P.dma('sync', o_dram[c * 128:(c + 1) * 128, sl], xo[:], reads=[xob], writes=[obufs[c]])


def t_proj(C, W, ncols, o_dram, obufs, t0, c_lo=0, c_hi=None):
    P = C.P
    nt = TT // 512
    CB = 512
    c_hi = ncols if c_hi is None else c_hi
    for col0 in range(c_lo, c_hi, CB):
        cw = min(CB, c_hi - col0)
        views, pbufs, KP = t_wjob(C, [W], 16, col0, cw, CB)
        for m in range((cw + 127) // 128):
            msz = min(128, cw - m * 128)
            r0 = col0 + m * 128
            for t in range(nt):
                ps, psb = C.ps.next()
                for k in range(16):
                    P.mm(ps[:msz, :], views[0][:, k, m * 128:m * 128 + msz], C.hT[:, k, t * 512:(t + 1) * 512],
                         start=(k == 0), stop=(k == 15),
                         reads=[pbufs(0, k), C.hT_b[k]], writes=[psb])
                xo, xob = C.xo.next()
                if C.cp % 2 == 0:
                    P.S(lambda e, xo=xo, ps=ps, msz=msz: e.copy(xo[:msz, :], ps[:msz, :]), reads=[psb], writes=[xob])
                else:
                    P.V(lambda e, xo=xo, ps=ps, msz=msz: e.tensor_copy(xo[:msz, :], ps[:msz, :]), reads=[psb], writes=[xob])
                C.cp += 1
                sl = slice(t0 + t * 512, t0 + (t + 1) * 512)
                ob = obufs.setdefault(r0, P.buf())
                P.dma('sync', o_dram[r0:r0 + msz, sl], xo[:msz, :], reads=[xob], writes=[ob])


def t_proj_tm(C, W, c_lo, c_hi, tm_off, o_dram, obufs, t0):
    P = C.P
    CB = 512
    for col0 in range(c_lo, c_hi, CB):
        cw = min(CB, c_hi - col0)
        views, pbufs, KP = t_wjob(C, [W], 16, col0, cw, CB)
        for tc in range(TT // 128):
            ps, psb = C.ps.next()
            for k in range(16):
                P.mm(ps[:, :cw], C.hT[:, k, tc * 128:(tc + 1) * 128], views[0][:, k, :cw],
                     start=(k == 0), stop=(k == 15), reads=[pbufs(0, k), C.hT_b[k]], writes=[psb])
            xo, xob = C.xo.next()
            if C.cp % 2 == 0:
                P.S(lambda e, xo=xo, ps=ps, cw=cw: e.copy(xo[:, :cw], ps[:, :cw]), reads=[psb], writes=[xob])
            else:
                P.V(lambda e, xo=xo, ps=ps, cw=cw: e.tensor_copy(xo[:, :cw], ps[:, :cw]), reads=[psb], writes=[xob])
            C.cp += 1
            ob = obufs.setdefault((col0, tc), P.buf())
            o0 = tm_off + (col0 - c_lo)
            P.dma('sync', o_dram[t0 + tc * 128:t0 + (tc + 1) * 128, o0:o0 + cw], xo[:, :cw], reads=[xob], writes=[ob])


def build_T(mode):
    _SFX[0] = ''
    nc = bass.Bass("TRN2", target_bir_lowering=False)
    di = lambda n, s: nc.dram_tensor(n, s, F32, kind="ExternalInput").ap()
    do = lambda n, s: nc.dram_tensor(n, s, F32, kind="ExternalOutput").ap()
    with ExitStack() as st:
        P = Prog(nc)
        C = t_setup(nc, P, st)
        gt = st.enter_context(_sbt(nc, 'gains', [128, 48], F32))
        gb = P.buf('gains')
        hb = lambda n: [P.buf(f'{n}{c}') for c in range(16)]
        obs = []
        if 'C' in mode:
            x1 = di('x1T', [D_MODEL, TOK])
            mix = di('mixT', [D_MODEL, TOK])
            w_out = di('w_out', [D_MODEL, D_MODEL])
            g2 = di('g2', [128, 16])
            wg2, wu2, wd2 = di('wg2', [D_MODEL, D_FF]), di('wu2', [D_MODEL, D_FF]), di('wd2', [D_FF, D_MODEL])
            x2 = do('x2T', [D_MODEL, TOK])
            x3 = do('x3T', [D_MODEL, TOK])
            P.dma('sync', gt[:, 32:48], g2, writes=[gb])
            x1b, x2b, x3b = hb('x1'), hb('x2'), hb('x3')
            for t0 in range(0, TOK, TT):
                t_loadT(C, mix, t0)
                t_mm_resid(C, w_out, 16, C.hT, lambda k, t: C.hT_b[k], x1, x1b, x2, x2b, t0, 1.0)
                t_norm(C, x2, x2b, t0, gt[:, 32:48], gb)
                t_ffn_gu(C, wg2, wu2)
                t_mm_resid(C, wd2, 32, C.actT, lambda k, t: C.act_b[(k, t)], x2, x2b, x3, x3b, t0, 0.5)
            xin, xinb = x3, x3b
            obs += x3b
        if 'A' in mode:
            if 'C' not in mode:
                xin = di('xT', [D_MODEL, TOK])
                xinb = hb('xin')
            g1 = di('g1', [128, 16])
            gm = di('gm', [128, 16])
            wg1, wu1, wd1 = di('wg1', [D_MODEL, D_FF]), di('wu1', [D_MODEL, D_FF]), di('wd1', [D_FF, D_MODEL])
            w_in = di('w_in', [D_MODEL, N_IN])
            xo1 = do('x1oT', [D_MODEL, TOK])
            proj = do('projT', [N_INP, TOK])
            P.dma('sync', gt[:, 0:16], g1, writes=[gb])
            P.dma('sync', gt[:, 16:32], gm, writes=[gb])
            xo1b = hb('xo1')
            pjb = {}
            for t0 in range(0, TOK, TT):
                t_norm(C, xin, xinb, t0, gt[:, 0:16], gb)
                t_ffn_gu(C, wg1, wu1)
                t_mm_resid(C, wd1, 32, C.actT, lambda k, t: C.act_b[(k, t)], xin, xinb, xo1, xo1b, t0, 0.5)
                t_norm(C, xo1, xo1b, t0, gt[:, 16:32], gb)
                t_proj(C, w_in, N_IN, proj, pjb, t0)
            obs += xo1b + list(pjb.values())
        P.finish(obs)
        P.emit(st)
    return nc


S = 4096
NCH = S // 128
SSD_STOP = 0
NEG = -30000.0


class MCtx:
    pass


PER_J = {'d_cosaq', 'd_sinaq', 'd_cosiq', 'd_siniq', 'd_maskT', 'd_maskQ'}
GLOBAL_IN = {'d_cosak', 'd_sinak', 'd_cosik', 'd_sinik'}


def m_setup(nc, P, st, npsA=4, src=None, dst=None, lj=None, nt512=5):
    C = MCtx()
    C.nc, C.P, C.st = nc, P, st
    C.n = 0
    C.src, C.dst, C.lj = src or {}, dst or {}, lj
    C.fused = lj is not None

    def sb(name, shape, dt=F32):
        t = st.enter_context(_sbt(nc, name, shape, dt))
        return t, P.buf(name)
    C.sb = sb
    C.in_names = []

    def di(n, s):
        if n in C.src:
            return C.src[n]
        if C.fused and not (n.startswith('c_') or n in GLOBAL_IN):
            n = n + (f'_j{lj[1]}' if n in PER_J else f'_l{lj[0]}_j{lj[1]}')
        C.in_names.append(n)
        return _din(nc, n, s)
    C.di = di

    def do(n, s):
        if n in C.dst:
            return C.dst[n]
        return nc.dram_tensor(n, s, F32, kind="ExternalOutput").ap()
    C.do = do
    C.ident, C.ident_b = sb('ident', [128, 128])
    C.tri, C.tri_b = sb('tri', [128, 128])
    C.maskT, C.maskT_b = sb('maskT', [128, 128], BF16)
    C.identb, C.identb_b = sb('identb', [128, 128], BF16)
    C.su32, C.su32_b = sb('su32', [32, 32])
    C.ones, C.ones_b = sb('ones', [128, 128])
    C.onesb, C.onesb_b = sb('onesb', [128, 128], BF16)
    C.one_c, C.one_cb = sb('one_c', [128, 4])
    c_ident, c_tri, c_mask, c_su = di('c_ident', [128, 128]), di('c_tri', [128, 128]), di('c_maskT', [128, 128]), di('c_su32', [32, 32])
    P.dma('sync', C.ident[:], c_ident, writes=[C.ident_b])
    P.dma('sync', C.tri[:], c_tri, writes=[C.tri_b])
    P.dma('sync', C.su32[:], c_su, writes=[C.su32_b])
    P.dma('gpsimd', C.maskT[:], c_mask, writes=[C.maskT_b])
    P.dma('gpsimd', C.identb[:], c_ident, writes=[C.identb_b])
    P.V(lambda e: e.memset(C.ones[:], 1.0), writes=[C.ones_b])
    P.V(lambda e: e.memset(C.onesb[:], 1.0), writes=[C.onesb_b])
    P.V(lambda e: e.memset(C.one_c[:], 1.0), writes=[C.one_cb])
    C.psA = Pool(P, st, nc, 'psA', npsA, [128, 512], F32, psum=True)
    C.psO = Pool(P, st, nc, 'psO', 2, [128, 512], F32, psum=True)
    C.psD = Pool(P, st, nc, 'psD', 2, [128, 512], F32, psum=True)
    C.t512 = Pool(P, st, nc, 't512_', nt512, [128, 512], F32)
    C.u512 = Pool(P, st, nc, 'u512_', 4, [128, 512], F32)
    C.outs = []
    return C


def fm_rmsnorm(C, src, srcb, npart, n, gcol, gb, inv_d, eps, out, outb, ones=None):
    P = C.P
    sq, sqb = C.t512.next()
    P.S(lambda e: e.activation(sq[:npart, :n], src, AF.Square), reads=[srcb], writes=[sqb])
    ps, psb = C.psA.next()
    om, omb = ones if ones is not None else (C.ones, C.ones_b)
    P.mm(ps[:npart, :n], om[:npart, :npart], sq[:npart, :n], reads=[omb, sqb], writes=[psb])
    sr, srb = C.t512.next()
    P.S(lambda e: e.activation(sr[:npart, :n], ps[:npart, :n], AF.Ln, bias=C.epsc[:npart, 0:1] if eps == 1e-6 else C.epsc[:npart, 1:2],
                               scale=inv_d), reads=[psb, C.epsc_b], writes=[srb])
    rs, rsb = C.t512.next()
    P.S(lambda e: e.activation(rs[:npart, :n], sr[:npart, :n], AF.Exp, scale=-0.5), reads=[srb], writes=[rsb])
    P.V(lambda e: e.scalar_tensor_tensor(out, src, gcol, rs[:npart, :n], ALU.mult, ALU.mult),
        reads=[srcb, rsb, gb], writes=[outb])


def m_consts2(C):
    C.epsc, C.epsc_b = C.sb('epsc', [128, 2])
    C.P.V(lambda e: e.memset(C.epsc[:, 0:1], 1e-6), writes=[C.epsc_b])
    C.P.V(lambda e: e.memset(C.epsc[:, 1:2], 1e-5), writes=[C.epsc_b])
    C.k255, C.k255_b = C.sb('k255', [128, 1])
    C.P.V(lambda e: e.memset(C.k255[:], TOPK - 0.5), writes=[C.k255_b])


def m_fox(C, nh=2):
    from itertools import zip_longest
    P, nc, sb, di = C.P, C.nc, C.sb, C.di
    fq, fk = di('fq', [nh, 128, S]), di('fk', [nh, 128, S])
    fv = di('fv', [S, nh * 128])
    fgc, fgr = di('fgc', [nh, 128, 32]), di('fgr', [nh, 32, 128])
    fnb = di('fnb', [128, nh])
    fg = di('fgains', [128, 2])
    foxT = C.do('foxT', [nh * 128, S])
    v, vb = sb('fvb', [128, NCH, nh * 128], BF16)
    P.dma('gpsimd', v[:], fv.rearrange("(j p) d -> p j d", p=128), writes=[vb])
    gn, gnb = sb('fgn', [128, 2 + nh])
    P.dma('sync', gn[:, 0:2], fg, writes=[gnb])
    P.dma('sync', gn[:, 2:2 + nh], fnb, writes=[gnb])
    gs, gsb = sb('fgs', [128, 2 + nh])
    P.V(lambda e: e.tensor_scalar(gs[:, 0:1], gn[:, 0:1], 128 ** -0.5, None, ALU.mult), reads=[gnb], writes=[gsb])
    P.V(lambda e: e.tensor_scalar(gs[:, 2:2 + nh], gn[:, 2:2 + nh], -1.0, None, ALU.mult), reads=[gnb], writes=[gsb])
    qn, kn = sb('fqn', [128, nh, S], BF16)[0], sb('fkn', [128, nh, S], BF16)[0]
    qnb = {(h, T): P.buf() for h in range(nh) for T in range(8)}
    knb = {(h, T): P.buf() for h in range(nh) for T in range(8)}
    negC, negCb = sb('negC', [128, nh, 32])
    obufs = []
    for h in range(nh):
        xc, xcb = sb(f'fxc{h}', [128, 32])
        xr, xrb = sb(f'fxr{h}', [32, 128])
        P.dma('sync', xc[:], fgc[h], writes=[xcb], allow_slow_non_contiguous=True)
        P.dma('sync', xr[:], fgr[h], writes=[xrb])
        P.S(lambda e, xc=xc, h=h: e.activation(xc[:], xc[:], AF.Exp, bias=gs[:, 2 + h:3 + h], scale=-1.0), reads=[xcb, gsb], writes=[xcb])
        P.S(lambda e, xc=xc: e.activation(xc[:], xc[:], AF.Ln, bias=C.one_c[:, 0:1], scale=1.0), reads=[xcb, C.one_cb], writes=[xcb])
        P.S(lambda e, xr=xr, h=h: e.activation(xr[:], xr[:], AF.Exp, bias=gs[:32, 2 + h:3 + h], scale=-1.0), reads=[xrb, gsb], writes=[xrb])
        P.S(lambda e, xr=xr: e.activation(xr[:], xr[:], AF.Ln, bias=C.one_c[:32, 0:1], scale=1.0), reads=[xrb, C.one_cb], writes=[xrb])
        rs, rsb = sb(f'frs{h}', [32, 1])
        P.V(lambda e, rs=rs, xr=xr: e.reduce_sum(rs[:], xr[:], axis=AX.X), reads=[xrb], writes=[rsb])
        rm, rmb = sb(f'frm{h}', [32, 128])
        P.V(lambda e, rm=rm, rs=rs: e.tensor_scalar(rm[:], C.ones[:32, :], rs[:, 0:1], None, ALU.mult), reads=[rsb, C.ones_b], writes=[rmb])
        ps, psb = C.psA.next()
        P.mm(ps[:, :32], C.tri[:], xc[:], start=True, stop=False, reads=[C.tri_b, xcb], writes=[psb])
        P.mm(ps[:, :32], rm[:], C.su32[:], start=False, stop=True, reads=[rmb, C.su32_b], writes=[psb])
        P.V(lambda e, ps=ps, h=h: e.tensor_copy(negC[:, h, :], ps[:, :32]), reads=[psb], writes=[negCb])
        for T in range(8):
            for (src, dst, dstb, gcol) in ((fq, qn, qnb, gs[:, 0:1]), (fk, kn, knb, gn[:, 1:2])):
                raw, rawb = C.u512.next()
                P.dma('sync', raw[:], src[h, :, T * 512:(T + 1) * 512], writes=[rawb])
                fm_rmsnorm(C, raw[:], rawb, 128, 512, gcol, gsb if src is fq else gnb, 1.0 / 128, 1e-6,
                           dst[:, h, T * 512:(T + 1) * 512], dstb[(h, T)])
    def att(h):
        for T in range(8):
            dg, dgb = C.t512.next()
            for r in range(4):
                P.V(lambda e, dg=dg, r=r, h=h, T=T: e.tensor_scalar(dg[:, r * 128:(r + 1) * 128], C.ident[:],
                                                                   negC[:, h, 4 * T + r:4 * T + r + 1], None, ALU.mult),
                    reads=[C.ident_b, negCb], writes=[dgb])
            pcb, pcbb = C.psA.next()
            P.mm(pcb[:], C.ones[:], dg[:], reads=[C.ones_b, dgb], writes=[pcbb])
            ncb, ncbb = C.u512.next()
            P.S(lambda e, ncb=ncb, pcb=pcb: e.copy(ncb[:], pcb[:]), reads=[pcbb], writes=[ncbb])
            po, pob = C.psO.next()
            pd, pdb = C.psD.next()
            nj = 4 * T + 4
            LA = 2
            staged = {}

            def stage1(j, T=T, h=h, ncb=ncb, ncbb=ncbb):
                r = j - 4 * T
                c0 = 0 if r < 0 else r * 128
                n = 512 - c0
                tsl = slice(T * 512 + c0, (T + 1) * 512)
                pS, pSb = C.psA.next()
                P.mm(pS[:, :n], kn[:, h, j * 128:(j + 1) * 128], qn[:, h, tsl], start=True, stop=(r < 0),
                     reads=[knb[(h, j // 4)], qnb[(h, T)]], writes=[pSb])
                if r >= 0:
                    P.mm(pS[:, :128], C.identb[:], C.maskT[:], start=False, stop=True,
                         reads=[C.identb_b, C.maskT_b], writes=[pSb])
                L, Lb = C.t512.next()
                P.V(lambda e, L=L, pS=pS, n=n, c0=c0: e.tensor_tensor(L[:, :n], pS[:, :n], ncb[:, c0:], ALU.subtract),
                    reads=[pSb, ncbb], writes=[Lb])
                pT, pTb = C.pT.next()
                P.S(lambda e, pT=pT, L=L, n=n, j=j: e.activation(pT[:, :n], L[:, :n], AF.Exp, bias=negC[:, h, j:j + 1], scale=1.0),
                    reads=[Lb, negCb], writes=[pTb])
                staged[j] = (pT, pTb, c0, n)

            for j in range(min(LA, nj)):
                stage1(j)
                yield
            for j in range(nj):
                if j + LA < nj:
                    stage1(j + LA)
                pT, pTb, c0, n = staged.pop(j)
                P.mm(po[:, c0:], v[:, j, h * 128:(h + 1) * 128], pT[:, :n], start=(j == 0), stop=(j == nj - 1),
                     reads=[vb, pTb], writes=[pob])
                P.mm(pd[:, c0:], C.onesb[:], pT[:, :n], start=(j == 0), stop=(j == nj - 1),
                     reads=[C.onesb_b, pTb], writes=[pdb])
                yield
            rc, rcb = C.t512.next()
            P.S(lambda e, rc=rc, pd=pd: e.activation(rc[:], pd[:], AF.Ln), reads=[pdb], writes=[rcb])
            P.S(lambda e, rc=rc: e.activation(rc[:], rc[:], AF.Exp, scale=-1.0), reads=[rcb], writes=[rcb])
            o, ob = C.u512.next()
            P.V(lambda e, o=o, po=po, rc=rc: e.tensor_tensor(o[:], po[:], rc[:], ALU.mult), reads=[pob, rcb], writes=[ob])
            hb_ = P.buf()
            P.dma('sync', foxT[h * 128:(h + 1) * 128, T * 512:(T + 1) * 512], o[:], reads=[ob], writes=[hb_])
            obufs.append(hb_)
            yield

    for h0 in range(0, nh, 2):
        for _ in zip_longest(att(h0), att(h0 + 1)):
            pass
    C.outs += obufs


def build_M(parts):
    nc = bass.Bass("TRN2", target_bir_lowering=False)
    _DIN.clear()
    _SFX[0] = ''
    with ExitStack() as st:
        P = Prog(nc)
        C = m_setup(nc, P, st, 3 if 'dsa' in parts else 4, nt512=(1 if parts == ['ssd'] else (7 if parts == ['fox'] else 5)))
        m_consts2(C)
        C.pT = Pool(P, st, nc, 'pT_', 8 if parts == ['fox'] else 5, [128, 512], BF16)
        if 'dsa' in parts:
            C.psT = Pool(P, st, nc, 'psT', 1, [128, 512], BF16, psum=True)
        if 'fox' in parts:
            m_fox(C)
        if 'ssd' in parts:
            m_ssd(C)
        if 'dsa' in parts:
            m_dsa(C)
        P.finish(C.outs)
        P.emit(st)
    return nc, list(C.in_names)


def _rope_mat(n, half, bases):
    m = np.zeros((n, n), np.float32)
    for b in bases:
        for d in range(half):
            m[b + d + half, b + d] = -1.0
            m[b + d, b + d + half] = 1.0
    return m


def rope_tables(rot, nrow, reps):
    half = rot // 2
    inv = (500000.0 ** (-np.arange(0, rot, 2, dtype=np.float32) / rot)).astype(np.float32)
    ang = np.arange(S, dtype=np.float32)[:, None] * inv[None, :]
    cos, sin = np.cos(ang).astype(np.float32), np.sin(ang).astype(np.float32)
    cf = np.ones((nrow, S), np.float32)
    sf = np.zeros((nrow, S), np.float32)
    cf[:half] = cos.T
    cf[half:rot] = cos.T
    sf[:half] = sin.T
    sf[half:rot] = sin.T
    return np.tile(cf, (reps, 1)), np.tile(sf, (reps, 1))


def dsa_prep(j, d_cq, d_k, d_v, d_ki, d_wi, w_uq, w_uqi, g_cq, g_q, g_k, g_ki):
    tiles = dsa_slot_tiles(j)
    tok = np.concatenate([np.arange(t * 128, (t + 1) * 128) for t in tiles])
    ca, sa = rope_tables(32, 128, 1)
    ci, si = rope_tables(16, 64, 2)
    i = np.arange(128)
    tri = np.where(i[:, None] <= i[None, :], 0.0, NEG).astype(np.float32)
    full = np.full((128, 128), NEG, np.float32)
    zero = np.zeros((128, 128), np.float32)
    mT, mQ = [], []
    for par in range(2):
        smaller = (j == 0 and par == 0) or (j == 1 and par == 1)
        A, B = (tri, full) if smaller else (zero, tri)
        for m in (A, B):
            mT.append(np.tile(m, (1, 4)))
            mQ.append(np.ascontiguousarray(m.T))
    return {
        'd_cqT': np.ascontiguousarray(d_cq[tok].T), 'd_kT': np.ascontiguousarray(d_k.T), 'd_v': np.ascontiguousarray(d_v),
        'd_kiT2': np.ascontiguousarray(np.concatenate([d_ki.T, d_ki.T], 0)),
        'd_wi': np.ascontiguousarray(d_wi[tok].reshape(NSLOT, 128, 8).transpose(1, 0, 2).reshape(128, NSLOT * 8)),
        'd_wuq': np.ascontiguousarray(w_uq), 'd_wuqi': np.ascontiguousarray(w_uqi),
        'd_gcq': np.ascontiguousarray(g_cq.reshape(4, 128).T),
        'd_gmisc': np.ascontiguousarray(np.stack([g_q, g_k, np.concatenate([g_ki, g_ki])], 1)),
        'd_cosaq': np.ascontiguousarray(ca[:, tok]), 'd_sinaq': np.ascontiguousarray(sa[:, tok]), 'd_cosak': ca, 'd_sinak': sa,
        'd_cosiq': np.ascontiguousarray(ci[:, tok]), 'd_siniq': np.ascontiguousarray(si[:, tok]), 'd_cosik': ci, 'd_sinik': si,
        'd_maskT': np.stack(mT), 'd_maskQ': np.stack(mQ),
    }, tok


def m_constants():
    i = np.arange(128)
    return {
        'c_ident': np.eye(128, dtype=np.float32),
        'c_tri': (i[:, None] <= i[None, :]).astype(np.float32),
        'c_maskT': np.where(i[:, None] <= i[None, :], 0.0, NEG).astype(np.float32),
        'c_su32': (np.arange(32)[:, None] < np.arange(32)[None, :]).astype(np.float32),
        'c_negones': -np.ones((128, 128), np.float32),
        'c_rma': _rope_mat(128, 16, [0]),
        'c_rmi': _rope_mat(128, 8, [0, 64]),
        'c_blk64': np.kron(np.eye(2, dtype=np.float32), np.ones((64, 64), np.float32)),
        'c_pow2': np.ascontiguousarray(np.broadcast_to((0.5 ** np.arange(1, NBIS + 1)).astype(np.float32)[None, :], (128, NBIS))),
        'c_mask4': np.tile(np.where(i[:, None] <= i[None, :], 0.0, NEG).astype(np.float32), (1, 4)),
    }


def m_ssd(C):
    P, nc, sb, di = C.P, C.nc, C.sb, C.di
    xbc = di('s_xbcT', [768, S])
    cwt = di('s_convw', [128, 6, 4])
    cbt_ = di('s_convb', [128, 6])
    zt = di('s_z', [S, 512])
    dtr = di('s_dt', [128, 256])
    dtb = di('s_dtb', [128, 256])
    alg = di('s_alog', [128, 256])
    dsk = di('s_dsk', [128, 512])
    ngn = di('s_ng', [128, 512])
    c_neg1 = di('c_negones', [128, 128])
    c_mask4 = di('c_mask4', [128, 512])
    ssmo = C.do('ssmo', [S, 512])
    cw, cwb = sb('s_cw', [128, 6, 4])
    cb, cbb = sb('s_cb', [128, 6])
    P.dma('sync', cw[:], cwt, writes=[cwb])
    P.dma('sync', cb[:], cbt_, writes=[cbb])
    negones, negb = sb('s_negones', [128, 128])
    mask4, mask4b = sb('s_mask4', [128, 512])
    P.dma('sync', negones[:], c_neg1, writes=[negb])
    P.dma('sync', mask4[:], c_mask4, writes=[mask4b])
    dskt, dskb = sb('s_dskt', [128, 512])
    ngt, ngb = sb('s_ngt', [128, 512])
    P.dma('sync', dskt[:], dsk, writes=[dskb])
    P.dma('sync', ngt[:], ngn, writes=[ngb])
    if SSD_STOP == -1:
        return
    dt, dtB = sb('s_dtv', [128, 256])
    ea_, eab_ = sb('s_ealog', [128, 256])
    tb_, tbb_ = sb('s_dtbv', [128, 256])
    if C.fused:
        P.dma('sync', dt[:].rearrange("p (j h) -> p j h", h=8), dtr.rearrange("(j p) h -> p j h", p=128), writes=[dtB],
              allow_slow_non_contiguous=True)
    else:
        P.dma('sync', dt[:], dtr, writes=[dtB])
    P.dma('sync', tb_[:], dtb, writes=[tbb_])
    P.dma('sync', ea_[:], alg, writes=[eab_])
    P.V(lambda e: e.tensor_tensor(dt[:], dt[:], tb_[:], ALU.add), reads=[dtB, tbb_], writes=[dtB])
    P.S(lambda e: e.activation(dt[:], dt[:], AF.Exp), reads=[dtB], writes=[dtB])
    P.S(lambda e: e.activation(dt[:], dt[:], AF.Ln, bias=C.one_c[:, 0:1], scale=1.0), reads=[dtB, C.one_cb], writes=[dtB])
    P.S(lambda e: e.activation(ea_[:], ea_[:], AF.Exp), reads=[eab_], writes=[eab_])
    if SSD_STOP == -2:
        return
    dA, dAb = sb('s_dA', [128, 256])
    P.V(lambda e: e.scalar_tensor_tensor(dA[:], dt[:], -1.0, ea_[:], ALU.mult, ALU.mult), reads=[dtB, eab_], writes=[dAb])
    pcs, pcsb = C.psA.next()
    ptot, ptotb = C.psA.next()
    P.mm(pcs[:, :256], C.tri[:], dA[:], reads=[C.tri_b, dAb], writes=[pcsb])
    P.mm(ptot[:, :256], C.ones[:], dA[:], reads=[C.ones_b, dAb], writes=[ptotb])
    if SSD_STOP == -3:
        return
    acs, acsb = sb('s_acs', [128, 256])
    P.V(lambda e: e.tensor_copy(acs[:], pcs[:, :256]), reads=[pcsb], writes=[acsb])
    if SSD_STOP == -4:
        return
    eacs, eacsb = sb('s_eacs', [128, 256])
    P.S(lambda e: e.activation(eacs[:], acs[:], AF.Exp), reads=[acsb], writes=[eacsb])
    if SSD_STOP == -5:
        return
    cd, cdb = sb('s_cd', [128, 256])
    tots, totsb = sb('s_tots', [128, 256])
    P.V(lambda e: e.tensor_copy(tots[:], ptot[:, :256]), reads=[ptotb], writes=[totsb])
    P.S(lambda e: e.activation(cd[:], tots[:], AF.Exp), reads=[totsb], writes=[cdb])
    if SSD_STOP == -6:
        return
    dec, decb = sb('s_dec', [128, 256])
    P.V(lambda e: e.tensor_tensor(dec[:], tots[:], acs[:], ALU.subtract), reads=[totsb, acsb], writes=[decb])
    P.S(lambda e: e.activation(dec[:], dec[:], AF.Exp), reads=[decb], writes=[decb])
    if SSD_STOP == 1:
        return
    x_tm, x_tmb = sb('s_xtm', [128, NCH, 512])
    xtb = [P.buf() for _ in range(NCH)]
    B_tm, _ = sb('s_Btm', [128, NCH, 128], BF16)
    Btb = [P.buf() for _ in range(NCH)]
    BT, BTb = sb('s_BT', [128, S], BF16)
    CT, CTb = sb('s_CT', [128, S], BF16)
    raws = Pool(P, C.st, nc, 's_raw', 1, [128, S + 3], F32)
    accs = Pool(P, C.st, nc, 's_acc', 1, [128, S], F32)
    for i in range(1):
        P.V(lambda e, i=i: e.memset(raws.tiles[i][:, 0:3], 0.0), writes=[raws.bufs[i]])
    for c in range(6):
        raw, rawb = raws.next()
        P.dma('sync', raw[:, 3:], C.src['ssd_rows'](c) if C.fused else xbc[c * 128:(c + 1) * 128, :], writes=[rawb])
        acc, accb = accs.next()
        P.V(lambda e, acc=acc, raw=raw, c=c: e.tensor_scalar(acc[:], raw[:, 0:S], cw[:, c, 0:1], None, ALU.mult),
            reads=[rawb, cwb], writes=[accb])
        for k in range(1, 4):
            P.V(lambda e, acc=acc, raw=raw, c=c, k=k: e.scalar_tensor_tensor(acc[:], raw[:, k:k + S], cw[:, c, k:k + 1], acc[:],
                                                                             ALU.mult, ALU.add),
                reads=[rawb, cwb, accb], writes=[accb])
        if c == 4:
            P.S(lambda e, acc=acc, c=c: e.activation(BT[:], acc[:], AF.Silu, bias=cb[:, c:c + 1], scale=1.0), reads=[accb, cbb], writes=[BTb])
        if c == 5:
            P.S(lambda e, acc=acc, c=c: e.activation(CT[:], acc[:], AF.Silu, bias=cb[:, c:c + 1], scale=1.0), reads=[accb, cbb], writes=[CTb])
            continue
        P.S(lambda e, acc=acc, c=c: e.activation(acc[:], acc[:], AF.Silu, bias=cb[:, c:c + 1], scale=1.0), reads=[accb, cbb], writes=[accb])
        for g in range(NCH // 4):
            pt, ptb = C.psA.next()
            for jj in range(4):
                j = 4 * g + jj
                P.add('tensor', lambda e, pt=pt, acc=acc, jj=jj, j=j: e.transpose(pt[:, jj * 128:(jj + 1) * 128],
                                                                                   acc[:, j * 128:(j + 1) * 128], C.ident[:]),
                      reads=[accb, C.ident_b], writes=[ptb])
            if c < 4:
                dst = x_tm[:, 4 * g:4 * g + 4, c * 128:(c + 1) * 128]
                wb_ = xtb[4 * g:4 * g + 4]
            else:
                dst = B_tm[:, 4 * g:4 * g + 4, :]
                wb_ = Btb[4 * g:4 * g + 4]
            src = pt[:].rearrange("p (a b) -> p a b", a=4)
            if g % 2 == 0:
                P.V(lambda e, dst=dst, src=src: e.tensor_copy(dst, src), reads=[ptb], writes=wb_)
            else:
                P.S(lambda e, dst=dst, src=src: e.copy(dst, src), reads=[ptb], writes=wb_)
    if SSD_STOP == 2:
        return
    xDs = Pool(P, C.st, nc, 's_xD', 2, [128, 512], F32)
    Hf, Hfb = sb('s_Hf', [128, 512])
    Hbs = Pool(P, C.st, nc, 's_Hb', 2, [128, 512], BF16)
    Zs = Pool(P, C.st, nc, 's_Z', 2, [128, 8, 128], F32)
    Es = Pool(P, C.st, nc, 's_E', 2, [128, 1024], F32)
    SCs = Pool(P, C.st, nc, 's_SC', 2, [128, 8, 128], BF16)
    cbts = Pool(P, C.st, nc, 's_cbt', 2, [128, 128], F32)
    xdts = Pool(P, C.st, nc, 's_xdt', 2, [128, 512], BF16)
    xdds = Pool(P, C.st, nc, 's_xdd', 2, [128, 512], BF16)
    zts = Pool(P, C.st, nc, 's_zt', 2, [128, 512], F32)
    sss = Pool(P, C.st, nc, 's_ss', 2, [128, 2], F32)
    y0s = Pool(P, C.st, nc, 's_y0', 2, [128, 512], F32)
    psts = Pool(P, C.st, nc, 's_pst', 2, [128, 512], F32)
    ys = Pool(P, C.st, nc, 's_y', 2, [128, 512], F32)
    psE = [C.psO, C.psD]
    NJ = NCH if SSD_STOP == 0 else SSD_STOP - 2
    st1 = {}
    hb_prev = [None]

    def stage1(j):
        hs = slice(j * 8, (j + 1) * 8)
        xDj, xDjb = xDs.next()
        P.G(lambda e: e.tensor_tensor(xDj[:], x_tm[:, j, :], dskt[:], ALU.mult), reads=[xtb[j], dskb], writes=[xDjb])
        Z, Zb = Zs.next()
        P.V(lambda e: e.tensor_tensor(Z[:], C.tri[:].unsqueeze(1).to_broadcast([128, 8, 128]),
                                      dA[:, hs].unsqueeze(2).to_broadcast([128, 8, 128]), ALU.mult),
            reads=[C.tri_b, dAb], writes=[Zb])
        Zf = Z[:].rearrange("p h l -> p (h l)")
        E, Eb = Es.next()
        for half in range(2):
            pe, peb = psE[half].next()
            P.mm(pe[:], C.ones[:], Zf[:, half * 512:(half + 1) * 512], start=True, stop=False, reads=[C.ones_b, Zb], writes=[peb])
            for hh in range(4):
                h = half * 4 + hh
                P.mm(pe[:, hh * 128:(hh + 1) * 128], Z[:, h, :], negones[:], start=False, stop=False, reads=[Zb, negb], writes=[peb])
            P.mm(pe[:], C.ident[:], mask4[:], start=False, stop=True, reads=[C.ident_b, mask4b], writes=[peb])
            P.S(lambda e, pe=pe, half=half: e.activation(E[:, half * 512:(half + 1) * 512], pe[:], AF.Exp), reads=[peb], writes=[Eb])
        pcb_, pcbb_ = C.psA.next()
        P.mm(pcb_[:, :128], BT[:, j * 128:(j + 1) * 128], CT[:, j * 128:(j + 1) * 128], reads=[BTb, CTb], writes=[pcbb_])
        cbt, cbtb = cbts.next()
        P.V(lambda e: e.tensor_copy(cbt[:], pcb_[:, :128]), reads=[pcbb_], writes=[cbtb])
        SC, SCb = SCs.next()
        P.V(lambda e: e.tensor_tensor(SC[:], E[:].rearrange("p (h l) -> p h l", h=8),
                                      cbt[:].unsqueeze(1).to_broadcast([128, 8, 128]), ALU.mult),
            reads=[Eb, cbtb], writes=[SCb])
        xdt, xdtb = xdts.next()
        xdd, xddb = xdds.next()
        x3 = x_tm[:, j, :].rearrange("p (h d) -> p h d", h=8)
        P.V(lambda e: e.tensor_tensor(xdt[:].rearrange("p (h d) -> p h d", h=8), x3,
                                      dt[:, hs].unsqueeze(2).to_broadcast([128, 8, 64]), ALU.mult),
            reads=[xtb[j], dtB], writes=[xdtb])
        P.V(lambda e: e.tensor_tensor(xdd[:].rearrange("p (h d) -> p h d", h=8),
                                      xdt[:].rearrange("p (h d) -> p h d", h=8),
                                      dec[:, hs].unsqueeze(2).to_broadcast([128, 8, 64]), ALU.mult),
            reads=[xdtb, decb], writes=[xddb])
        py, pyb = C.psA.next()
        for h in range(8):
            P.mm(py[:, h * 64:(h + 1) * 64], SC[:, h, :], xdt[:, h * 64:(h + 1) * 64], reads=[SCb, xdtb], writes=[pyb])
        y0, y0b = y0s.next()
        P.V(lambda e: e.tensor_tensor(y0[:], py[:], xDj[:], ALU.add), reads=[pyb, xDjb], writes=[y0b])
        pstt = None
        if j < NCH - 1:
            pst, pstb = C.psA.next()
            P.mm(pst[:], B_tm[:, j, :], xdd[:], reads=[Btb[j], xddb], writes=[pstb])
            pss_, pssb = psts.next()
            P.S(lambda e: e.copy(pss_[:], pst[:]), reads=[pstb], writes=[pssb])
            pstt = (pss_, pssb)
        z, zb = zts.next()
        P.dma('sync', z[:], zt[j * 128:(j + 1) * 128, :], writes=[zb])
        P.S(lambda e: e.activation(z[:], z[:], AF.Silu), reads=[zb], writes=[zb])
        st1[j] = (y0, y0b, pstt, z, zb)

    def stage2(j):
        hs = slice(j * 8, (j + 1) * 8)
        y0, y0b, pstt, z, zb = st1.pop(j)
        y, yb = ys.next()
        if j > 0:
            Hbp, Hbpb = hb_prev[0]
            pyo, pyob = C.psA.next()
            P.mm(pyo[:], CT[:, j * 128:(j + 1) * 128], Hbp[:], reads=[CTb, Hbpb], writes=[pyob])
        if j < NCH - 1:
            pss_, pssb = pstt
            if j == 0:
                P.V(lambda e: e.tensor_copy(Hf[:], pss_[:]), reads=[pssb], writes=[Hfb])
            else:
                P.V(lambda e: e.tensor_tensor(Hf[:].rearrange("p (h d) -> p h d", h=8), Hf[:].rearrange("p (h d) -> p h d", h=8),
                                              cd[:, hs].unsqueeze(2).to_broadcast([128, 8, 64]), ALU.mult),
                    reads=[Hfb, cdb], writes=[Hfb])
                P.V(lambda e: e.tensor_tensor(Hf[:], Hf[:], pss_[:], ALU.add), reads=[Hfb, pssb], writes=[Hfb])
            Hbn, Hbnb = Hbs.next()
            P.S(lambda e: e.copy(Hbn[:], Hf[:]), reads=[Hfb], writes=[Hbnb])
            hb_prev[0] = (Hbn, Hbnb)
        if j > 0:
            P.V(lambda e: e.tensor_tensor(y[:].rearrange("p (h d) -> p h d", h=8),
                                          pyo[:].rearrange("p (h d) -> p h d", h=8),
                                          eacs[:, hs].unsqueeze(2).to_broadcast([128, 8, 64]), ALU.mult),
                reads=[pyob, eacsb], writes=[yb])
            P.V(lambda e: e.tensor_tensor(y[:], y[:], y0[:], ALU.add), reads=[yb, y0b], writes=[yb])
            ysrc, ysrcb = y, yb
        else:
            ysrc, ysrcb = y0, y0b
        P.V(lambda e: e.tensor_tensor(y[:], ysrc[:], z[:], ALU.mult), reads=[ysrcb, zb], writes=[yb])
        ss, ssb = sss.next()
        P.V(lambda e: e.memset(ss[:], 0.0), writes=[ssb])
        P.S(lambda e: e.activation(z[:], y[:], AF.Square, accum_out=ss[:, 0:1]), reads=[yb, zb, ssb], writes=[zb, ssb])
        P.S(lambda e: e.activation(ss[:, 1:2], ss[:, 0:1], AF.Ln, bias=C.epsc[:, 1:2], scale=1.0 / 512),
            reads=[ssb, C.epsc_b], writes=[ssb])
        P.S(lambda e: e.activation(ss[:, 1:2], ss[:, 1:2], AF.Exp, scale=-0.5), reads=[ssb], writes=[ssb])
        o, ob = C.u512.next()
        P.V(lambda e: e.scalar_tensor_tensor(o[:], y[:], ss[:, 1:2], ngt[:], ALU.mult, ALU.mult),
            reads=[yb, ssb, ngb], writes=[ob])
        hb_ = P.buf()
        if C.fused:
            ptr, ptrb = C.psA.next()
            for a in range(4):
                P.add('tensor', lambda e, a=a: e.transpose(ptr[:, a * 128:(a + 1) * 128], o[:, a * 128:(a + 1) * 128], C.ident[:]),
                      reads=[ob, C.ident_b], writes=[ptrb])
            oT, oTb = C.u512.next()
            P.S(lambda e: e.copy(oT[:], ptr[:]), reads=[ptrb], writes=[oTb])
            P.dma('sync', ssmo[:, j * 128:(j + 1) * 128].rearrange("(a p) t -> p a t", p=128),
                  oT[:].rearrange("p (a t) -> p a t", a=4), reads=[oTb], writes=[hb_])
        else:
            P.dma('sync', ssmo[j * 128:(j + 1) * 128, :], o[:], reads=[ob], writes=[hb_])
        C.outs.append(hb_)

    if NJ > 0:
        stage1(0)
    for j in range(NJ):
        if j + 1 < NJ:
            stage1(j + 1)
        stage2(j)


NSLOT = 16
NBIS = 16
TOPK = 256


def dsa_slot_tiles(j):
    out = []
    for m in range(8):
        out += [4 * m, 4 * m + 3] if j == 0 else [4 * m + 1, 4 * m + 2]
    return out


DSA_NCH = [(4 * (k // 2) + 2) if k % 2 == 0 else (4 * (k // 2) + 4) for k in range(NSLOT)]


def fm_rope(C, x, xb, cosd, sind, c0, n, rm, rmb, out, outb, out3=False):
    P = C.P
    ct, ctb = C.u512.next()
    st_, stb = C.u512.next()
    P.dma('sync', ct[:, :n], cosd[:, c0:c0 + n], writes=[ctb])
    P.dma('sync', st_[:, :n], sind[:, c0:c0 + n], writes=[stb])
    pr, prb = C.psA.next()
    P.mm(pr[:, :n], rm[:], x, reads=[rmb, xb], writes=[prb])
    P.V(lambda e: e.tensor_tensor(st_[:, :n], pr[:, :n], st_[:, :n], ALU.mult), reads=[prb, stb], writes=[stb])
    P.G(lambda e: e.tensor_tensor(ct[:, :n], x, ct[:, :n], ALU.mult), reads=[xb, ctb], writes=[ctb])
    if out3:
        P.V(lambda e: e.tensor_tensor(out, ct[:, :n].rearrange("p (a b) -> p a b", b=128), st_[:, :n].rearrange("p (a b) -> p a b", b=128), ALU.add),
            reads=[ctb, stb], writes=[outb])
    else:
        P.V(lambda e: e.tensor_tensor(out, ct[:, :n], st_[:, :n], ALU.add), reads=[ctb, stb], writes=[outb])


def m_dsa(C):
    P, nc, sb, di = C.P, C.nc, C.sb, C.di
    NQ = NSLOT * 128
    cq = di('d_cqT', [512, NQ])
    dk = di('d_kT', [128, S])
    dvt = di('d_v', [S, 128])
    kid = di('d_kiT2', [128, S])
    wid = di('d_wi', [128, NSLOT * 8])
    wuq = di('d_wuq', [512, 512])
    wuqi = di('d_wuqi', [512, 512])
    gcq = di('d_gcq', [128, 4])
    gmisc = di('d_gmisc', [128, 3])
    cosaq, sinaq = di('d_cosaq', [128, NQ]), di('d_sinaq', [128, NQ])
    cosak, sinak = di('d_cosak', [128, S]), di('d_sinak', [128, S])
    cosiq, siniq = di('d_cosiq', [128, NQ]), di('d_siniq', [128, NQ])
    cosik, sinik = di('d_cosik', [128, S]), di('d_sinik', [128, S])
    rma_d, rmi_d = di('c_rma', [128, 128]), di('c_rmi', [128, 128])
    blk_d = di('c_blk64', [128, 128])
    pow2_d = di('c_pow2', [128, NBIS])
    mT_d = di('d_maskT', [4, 128, 512])
    mQ_d = di('d_maskQ', [4, 128, 128])
    dsaT = C.do('dsaT', [512, NQ])
    rma, rmab = sb('d_rma', [128, 128])
    rmi, rmib = sb('d_rmi', [128, 128])
    blk, blkb = sb('d_blk', [128, 128])
    pow2, pow2b = sb('d_pow2', [128, NBIS])
    P.dma('sync', rma[:], rma_d, writes=[rmab])
    P.dma('sync', rmi[:], rmi_d, writes=[rmib])
    P.dma('sync', blk[:], blk_d, writes=[blkb])
    P.dma('sync', pow2[:], pow2_d, writes=[pow2b])
    mT, mTb = sb('d_mT', [128, 4, 512], BF16)
    mQ, mQb = sb('d_mQ', [128, 4, 128])
    for i in range(4):
        P.dma('gpsimd', mT[:, i, :], mT_d[i], writes=[mTb])
        P.dma('sync', mQ[:, i, :], mQ_d[i], writes=[mQb])
    gq, gqb = sb('d_gq', [128, 8])
    P.dma('sync', gq[:, 0:4], gcq, writes=[gqb])
    P.dma('sync', gq[:, 4:7], gmisc, writes=[gqb])
    gqs, gqsb = sb('d_gqs', [128, 1])
    P.V(lambda e: e.tensor_scalar(gqs[:], gq[:, 4:5], 128 ** -0.5, None, ALU.mult), reads=[gqb], writes=[gqsb])
    wi, wib = sb('d_wis', [128, NSLOT * 8])
    tiles_ = dsa_slot_tiles(C.lj[1]) if C.fused else None
    if C.fused:
        for k_ in range(NSLOT):
            P.dma('sync', wi[:, k_ * 8:(k_ + 1) * 8], wid[tiles_[k_] * 128:(tiles_[k_] + 1) * 128, :], writes=[wib])
    else:
        P.dma('sync', wi[:], wid, writes=[wib])
    P.V(lambda e: e.tensor_scalar(wi[:], wi[:], (8 ** -0.5) * (64 ** -0.5), None, ALU.mult), reads=[wib], writes=[wib])
    wq, wqb = sb('d_wq', [128, 4, 512], BF16)
    wqi, wqib = sb('d_wqi', [128, 4, 512], BF16)
    P.dma('gpsimd', wq[:], wuq.rearrange("(k p) c -> p k c", p=128), writes=[wqb])
    P.dma('gpsimd', wqi[:], wuqi.rearrange("(k p) c -> p k c", p=128), writes=[wqib])
    dv, dvb = sb('d_dv', [128, NCH, 128], BF16)
    P.dma('gpsimd', dv[:], dvt.rearrange("(j p) d -> p j d", p=128), writes=[dvb])
    dkT, _ = sb('d_dkT', [128, S], BF16)
    kiT, _ = sb('d_kiT', [128, S], BF16)
    dkb = [P.buf() for _ in range(8)]
    kib = [P.buf() for _ in range(8)]
    for T in range(8):
        sl = slice(T * 512, (T + 1) * 512)
        raw, rawb = C.u512.next()
        P.dma('sync', raw[:], dk[:, sl], writes=[rawb])
        nr, nrb = C.t512.next()
        fm_rmsnorm(C, raw[:], rawb, 128, 512, gq[:, 5:6], gqb, 1.0 / 128, 1e-6, nr[:], nrb)
        fm_rope(C, nr[:], nrb, cosak, sinak, T * 512, 512, rma, rmab, dkT[:, sl], dkb[T])
        raw, rawb = C.u512.next()
        if C.fused:
            P.dma('sync', raw[0:64, :], kid[:, sl], writes=[rawb])
            P.dma('sync', raw[64:128, :], kid[:, sl], writes=[rawb])
        else:
            P.dma('sync', raw[:], kid[:, sl], writes=[rawb])
        nr, nrb = C.t512.next()
        fm_rmsnorm(C, raw[:], rawb, 128, 512, gq[:, 6:7], gqb, 1.0 / 64, 1e-6, nr[:], nrb, ones=(blk, blkb))
        fm_rope(C, nr[:], nrb, cosik, sinik, T * 512, 512, rmi, rmib, kiT[:, sl], kib[T])
    cqn, _ = sb('d_cqn', [128, 4, NQ], BF16)
    cqnb = [P.buf() for _ in range(NQ // 512)]
    dqT, _ = sb('d_dqT', [128, NSLOT, 4, 128], BF16)
    dqb = [P.buf() for _ in range(NQ // 512)]
    qiT, _ = sb('d_qiT', [128, 4, NQ], BF16)
    qib = [P.buf() for _ in range(NQ // 512)]
    cqr = Pool(P, C.st, nc, 'd_cqr', 1, [128, 4, 512], F32)
    for T in range(NQ // 512):
        sl = slice(T * 512, (T + 1) * 512)
        raw, rawb = cqr.next()
        if C.fused:
            for q_ in range(4):
                tl = tiles_[4 * T + q_]
                P.dma('sync', raw[:, :, q_ * 128:(q_ + 1) * 128], cq[:, tl * 128:(tl + 1) * 128].rearrange("(k p) t -> p k t", p=128), writes=[rawb])
        else:
            P.dma('sync', raw[:], cq[:, sl].rearrange("(k p) t -> p k t", p=128), writes=[rawb])
        ps, psb = C.psA.next()
        for k in range(4):
            sq, sqb = C.t512.next()
            P.S(lambda e, sq=sq, raw=raw, k=k: e.activation(sq[:], raw[:, k, :], AF.Square), reads=[rawb], writes=[sqb])
            P.mm(ps[:], C.ones[:], sq[:], start=(k == 0), stop=(k == 3), reads=[C.ones_b, sqb], writes=[psb])
        sr, srb = C.t512.next()
        P.S(lambda e, sr=sr, ps=ps: e.activation(sr[:], ps[:], AF.Ln, bias=C.epsc[:, 0:1], scale=1.0 / 512), reads=[psb, C.epsc_b], writes=[srb])
        rs, rsb = C.t512.next()
        P.S(lambda e, rs=rs, sr=sr: e.activation(rs[:], sr[:], AF.Exp, scale=-0.5), reads=[srb], writes=[rsb])
        for k in range(4):
            P.V(lambda e, raw=raw, rs=rs, k=k, sl=sl: e.scalar_tensor_tensor(cqn[:, k, sl], raw[:, k, :], gq[:, k:k + 1], rs[:], ALU.mult, ALU.mult),
                reads=[rawb, rsb, gqb], writes=[cqnb[T]])
        for h in range(4):
            pq, pqb = C.psA.next()
            for k in range(4):
                P.mm(pq[:], wq[:, k, h * 128:(h + 1) * 128], cqn[:, k, sl], start=(k == 0), stop=(k == 3),
                     reads=[wqb, cqnb[T]], writes=[pqb])
            qf, qfb = C.u512.next()
            P.S(lambda e, qf=qf, pq=pq: e.copy(qf[:], pq[:]), reads=[pqb], writes=[qfb])
            nr, nrb = C.t512.next()
            fm_rmsnorm(C, qf[:], qfb, 128, 512, gqs[:, 0:1], gqsb, 1.0 / 128, 1e-6, nr[:], nrb)
            dst = dqT[:, 4 * T:4 * T + 4, h, :]
            fm_rope(C, nr[:], nrb, cosaq, sinaq, T * 512, 512, rma, rmab, dst, dqb[T], out3=True)
        for c in range(4):
            pq, pqb = C.psA.next()
            for k in range(4):
                P.mm(pq[:], wqi[:, k, c * 128:(c + 1) * 128], cqn[:, k, sl], start=(k == 0), stop=(k == 3),
                     reads=[wqib, cqnb[T]], writes=[pqb])
            qf, qfb = C.t512.next()
            P.S(lambda e, qf=qf, pq=pq: e.copy(qf[:], pq[:]), reads=[pqb], writes=[qfb])
            fm_rope(C, qf[:], qfb, cosiq, siniq, T * 512, 512, rmi, rmib, qiT[:, c, sl], qib[T])
    idxs = Pool(P, C.st, nc, 'd_idx', 2, [128, S], F32)
    sels = Pool(P, C.st, nc, 'd_sel', 2, [128, S], BF16)
    selTs = Pool(P, C.st, nc, 'd_selT', 2, [128, NCH, 128], BF16)
    rls = Pool(P, C.st, nc, 'd_rl', 3, [128, 512], F32)
    bst = Pool(P, C.st, nc, 'd_bs', 2, [128, 8], F32)
    Wt = Pool(P, C.st, nc, 'd_W', 2, [128, NBIS], F32)
    junks = Pool(P, C.st, nc, 'd_junk2_', 2, [128, S], BF16)

    def idx_phase(k):
        nch = DSA_NCH[k]
        n = nch * 128
        par = k % 2
        idx, idxb = idxs.next()
        for s0 in range(0, n, 512):
            sn = min(512, n - s0)
            for h in range(8):
                c, half = h // 2, h % 2
                pi, pib = C.psA.next()
                P.mm(pi[:, :sn], qiT[half * 64:(half + 1) * 64, c, k * 128:(k + 1) * 128],
                     kiT[half * 64:(half + 1) * 64, s0:s0 + sn], reads=[qib[k // 4], kib[s0 // 512]], writes=[pib])
                rl, rlb = rls.next()
                P.S(lambda e, rl=rl, pi=pi, sn=sn: e.activation(rl[:, :sn], pi[:, :sn], AF.Relu), reads=[pib], writes=[rlb])
                if h == 0:
                    P.V(lambda e, rl=rl, idx=idx, s0=s0, sn=sn, k=k, h=h: e.tensor_scalar(
                        idx[:, s0:s0 + sn], rl[:, :sn], wi[:, k * 8 + h:k * 8 + h + 1], None, ALU.mult),
                        reads=[rlb, wib], writes=[idxb])
                else:
                    P.V(lambda e, rl=rl, idx=idx, s0=s0, sn=sn, k=k, h=h: e.scalar_tensor_tensor(
                        idx[:, s0:s0 + sn], rl[:, :sn], wi[:, k * 8 + h:k * 8 + h + 1], idx[:, s0:s0 + sn], ALU.mult, ALU.add),
                        reads=[rlb, wib, idxb], writes=[idxb])
        bs, bsb = bst.next()
        W, Wb = Wt.next()
        jk, jkb = junks.next()
        return dict(k=k, nch=nch, n=n, par=par, idx=idx, idxb=idxb, bs=bs, bsb=bsb, W=W, Wb=Wb, jk=jk, jkb=jkb)

    def prep_ops(q):
        bs, bsb, idx, idxb, W, Wb, n, nch, par = q['bs'], q['bsb'], q['idx'], q['idxb'], q['W'], q['Wb'], q['n'], q['nch'], q['par']
        ops = [
            lambda: P.V(lambda e: e.tensor_reduce(bs[:, 0:1], idx[:, :n], AX.X, ALU.min), reads=[idxb], writes=[bsb]),
            lambda: P.V(lambda e: e.tensor_reduce(bs[:, 1:2], idx[:, :n], AX.X, ALU.max), reads=[idxb, bsb], writes=[bsb]),
            lambda: P.V(lambda e: e.tensor_tensor(bs[:, 1:2], bs[:, 1:2], bs[:, 0:1], ALU.subtract), reads=[bsb], writes=[bsb]),
            lambda: P.V(lambda e: e.tensor_scalar(W[:], pow2[:], bs[:, 1:2], None, ALU.mult), reads=[bsb, pow2b], writes=[Wb]),
        ]
        ops.append(lambda: P.V(lambda e: e.memset(bs[:, 5:6], float(n) - 2.0 * TOPK + 0.5), reads=[bsb], writes=[bsb]))
        for a_ in range(2):
            cc = nch - 2 + a_
            ops.append(lambda cc=cc, a_=a_: P.V(lambda e: e.tensor_tensor(idx[:, cc * 128:(cc + 1) * 128], idx[:, cc * 128:(cc + 1) * 128],
                                                                        mQ[:, par * 2 + a_, :], ALU.add),
                                                reads=[idxb, mQb], writes=[idxb]))
        return ops

    def bis_ops(q, it, on_act=False):
        bs, bsb, idx, idxb, W, Wb, n, jk, jkb = q['bs'], q['bsb'], q['idx'], q['idxb'], q['W'], q['Wb'], q['n'], q['jk'], q['jkb']
        if on_act:
            return [
                lambda: P.V(lambda e: e.tensor_tensor(bs[:, 2:3], bs[:, 0:1], W[:, it:it + 1], ALU.add), reads=[bsb, Wb], writes=[bsb]),
                lambda: P.V(lambda e: e.memset(bs[:, 3:4], 0.0), reads=[bsb], writes=[bsb]),
                lambda: P.S(lambda e: e.activation(jk[:, :n], idx[:, :n], AF.Sign, bias=bs[:, 2:3], scale=-1.0, accum_out=bs[:, 3:4]),
                            reads=[idxb, bsb, jkb], writes=[jkb, bsb]),
                lambda: P.V(lambda e: e.tensor_scalar(bs[:, 4:5], bs[:, 3:4], bs[:, 5:6], W[:, it:it + 1], ALU.is_le, ALU.mult),
                            reads=[bsb, Wb], writes=[bsb]),
                lambda: P.V(lambda e: e.tensor_tensor(bs[:, 0:1], bs[:, 0:1], bs[:, 4:5], ALU.add), reads=[bsb], writes=[bsb]),
            ]
        return [
            lambda: P.V(lambda e: e.tensor_tensor(bs[:, 2:3], bs[:, 0:1], W[:, it:it + 1], ALU.add), reads=[bsb, Wb], writes=[bsb]),
            lambda: P.V(lambda e: e.memset(bs[:, 3:4], 0.0), reads=[bsb], writes=[bsb]),
            lambda: P.V(lambda e: e.tensor_scalar(jk[:, :n], idx[:, :n], bs[:, 2:3], 0.0, ALU.is_ge, ALU.add, accum_out=bs[:, 3:4]),
                        reads=[idxb, bsb, jkb], writes=[jkb, bsb]),
            lambda: P.V(lambda e: e.tensor_scalar(bs[:, 4:5], bs[:, 3:4], C.k255[:, 0:1], W[:, it:it + 1], ALU.is_ge, ALU.mult),
                        reads=[bsb, Wb, C.k255_b], writes=[bsb]),
            lambda: P.V(lambda e: e.tensor_tensor(bs[:, 0:1], bs[:, 0:1], bs[:, 4:5], ALU.add), reads=[bsb], writes=[bsb]),
        ]

    def interleave(la, lb):
        for i in range(max(len(la), len(lb))):
            if i < len(la):
                la[i]()
            if i < len(lb):
                lb[i]()

    def post_phase(q):
        k, nch, n, par, idx, idxb, bs, bsb = q['k'], q['nch'], q['n'], q['par'], q['idx'], q['idxb'], q['bs'], q['bsb']
        sel, selb = sels.next()
        P.V(lambda e, sel=sel, idx=idx, bs=bs, n=n: e.tensor_scalar(sel[:, :n], idx[:, :n], bs[:, 0:1], None, ALU.is_ge),
            reads=[idxb, bsb], writes=[selb])
        selT, selTb = selTs.next()
        for g in range(0, nch, 4):
            gn = min(4, nch - g)
            pt, ptb = C.psT.next()
            for cc in range(gn):
                P.add('tensor', lambda e, pt=pt, sel=sel, cc=cc, g=g: e.transpose(pt[:, cc * 128:(cc + 1) * 128],
                                                                                 sel[:, (g + cc) * 128:(g + cc + 1) * 128], C.identb[:]),
                      reads=[selb, C.identb_b], writes=[ptb])
            src = pt[:, :gn * 128].rearrange("p (a b) -> p a b", a=gn)
            P.S(lambda e, selT=selT, src=src, g=g, gn=gn: e.copy(selT[:, g:g + gn, :], src), reads=[ptb], writes=[selTb])
        po, pob = C.psO.next()
        pd, pdb = C.psD.next()
        LA = 2
        staged = {}

        def stage1(c, k=k, nch=nch, par=par, selT=selT, selTb=selTb):
            pS, pSb = C.psA.next()
            last2 = c >= nch - 2
            P.mm(pS[:], dkT[:, c * 128:(c + 1) * 128], dqT[:, k, :, :].rearrange("p h t -> p (h t)"), start=True, stop=not last2,
                 reads=[dkb[c // 4], dqb[k // 4]], writes=[pSb])
            if last2:
                P.mm(pS[:], C.identb[:], mT[:, par * 2 + (c - (nch - 2)), :], start=False, stop=True,
                     reads=[C.identb_b, mTb], writes=[pSb])
            pT, pTb = C.pT.next()
            P.S(lambda e, pT=pT, pS=pS: e.activation(pT[:], pS[:], AF.Exp), reads=[pSb], writes=[pTb])
            P.V(lambda e, pT=pT, c=c: e.tensor_tensor(pT[:].rearrange("p (h t) -> p h t", h=4), pT[:].rearrange("p (h t) -> p h t", h=4),
                                                      selT[:, c, :].unsqueeze(1).to_broadcast([128, 4, 128]), ALU.mult),
                reads=[pTb, selTb], writes=[pTb])
            staged[c] = (pT, pTb)

        for c in range(min(LA, nch)):
            stage1(c)
        for c in range(nch):
            if c + LA < nch:
                stage1(c + LA)
            pT, pTb = staged.pop(c)
            P.mm(po[:], dv[:, c, :], pT[:], start=(c == 0), stop=(c == nch - 1), reads=[dvb, pTb], writes=[pob])
            P.mm(pd[:], C.onesb[:], pT[:], start=(c == 0), stop=(c == nch - 1), reads=[C.onesb_b, pTb], writes=[pdb])
        rc, rcb = C.t512.next()
        P.S(lambda e, rc=rc, pd=pd: e.activation(rc[:], pd[:], AF.Ln), reads=[pdb], writes=[rcb])
        P.S(lambda e, rc=rc: e.activation(rc[:], rc[:], AF.Exp, scale=-1.0), reads=[rcb], writes=[rcb])
        o, ob = C.u512.next()
        P.V(lambda e, o=o, po=po, rc=rc: e.tensor_tensor(o[:], po[:], rc[:], ALU.mult), reads=[pob, rcb], writes=[ob])
        hb_ = P.buf()
        kc = tiles_[k] if C.fused else k
        P.dma('sync', dsaT.rearrange("(h d) t -> d h t", d=128)[:, :, kc * 128:(kc + 1) * 128],
              o[:].rearrange("p (h t) -> p h t", h=4), reads=[ob], writes=[hb_])
        C.outs.append(hb_)

    for k0 in range(0, NSLOT, 2):
        qa, qb = idx_phase(k0), idx_phase(k0 + 1)
        interleave(prep_ops(qa), prep_ops(qb))
        for it in range(NBIS):
            interleave(bis_ops(qa, it), bis_ops(qb, it, on_act=True))
        for q in (qa, qb):
            post_phase(q)


_PROGS = {}
DEPTH = 4
OFF = dict(fq=0, fk=512, fv=1024, fg=1536, z=1540, xbc=2564, dt=4100, cq=4116, dk=4628, dv=4756, dki=4884, dwi=4948)


def _prog(key):
    if key not in _PROGS:
        if key in ('A', 'C', 'CA'):
            _PROGS[key] = (build_T(key), None)
        else:
            _PROGS[key] = build_M([key])
    return _PROGS[key]


def _run(key, in_maps):
    nc, names = _prog(key)
    if names is not None:
        in_maps = [{k: v for k, v in m.items() if k in names} for m in in_maps]
    in_maps = [{k: np.ascontiguousarray(v, dtype=np.float32) for k, v in m.items()} for m in in_maps]
    return run_bass_kernel_spmd(nc, in_maps, core_ids=list(range(NCORE))).results


def _gT(g):
    return np.ascontiguousarray(g.reshape(16, 128).T)


def _bc(v, n=128):
    return np.ascontiguousarray(np.broadcast_to(np.asarray(v, np.float32)[None, :], (n, len(v))))


def _mixer_inputs(PT, j, l, w):
    hs = [2 * j, 2 * j + 1]
    rows = lambda k, n: PT[OFF[k]:OFF[k] + n]
    fox = {}
    fox['fq'] = rows('fq', 512).reshape(4, 128, S)[hs]
    fox['fk'] = rows('fk', 512).reshape(4, 128, S)[hs]
    fox['fv'] = rows('fv', 512).reshape(4, 128, S)[hs].transpose(2, 0, 1).reshape(S, 256)
    fgl = rows('fg', 4)[hs]
    fox['fgc'] = fgl.reshape(2, 32, 128).transpose(0, 2, 1)
    fox['fgr'] = fgl.reshape(2, 32, 128)
    fox['fnb'] = _bc(w['fox_fgate_b'][l][hs])
    fox['fgains'] = np.stack([w['fox_q_norm'][l], w['fox_k_norm'][l]], 1)
    sel = np.concatenate([np.arange(512 * j, 512 * j + 512), 1024 + np.arange(128 * j, 128 * j + 128),
                          1280 + np.arange(128 * j, 128 * j + 128)])
    ssd = {}
    ssd['s_xbcT'] = rows('xbc', 1536)[sel]
    ssd['s_convw'] = w['ssm_conv_w'][l][:, sel].T.reshape(6, 128, 4).transpose(1, 0, 2)
    ssd['s_convb'] = w['ssm_conv_b'][l][sel].reshape(6, 128).T
    ssd['s_z'] = rows('z', 1024)[512 * j:512 * j + 512].T
    dtr = rows('dt', 16)[8 * j:8 * j + 8].T
    ssd['s_dt'] = dtr.reshape(32, 128, 8).transpose(1, 0, 2).reshape(128, 256)
    h8 = slice(8 * j, 8 * j + 8)
    ssd['s_dtb'] = np.broadcast_to(w['ssm_dt_bias'][l][h8][None, None, :], (128, 32, 8)).reshape(128, 256)
    ssd['s_alog'] = np.broadcast_to(w['ssm_a_log'][l][h8][None, None, :], (128, 32, 8)).reshape(128, 256)
    ssd['s_dsk'] = _bc(np.repeat(w['ssm_d'][l][h8], 64))
    ssd['s_ng'] = _bc(w['ssm_norm'][l][512 * j:512 * j + 512])
    dsa, tok = dsa_prep(j, rows('cq', 512).T, rows('dk', 128).T, rows('dv', 128).T, rows('dki', 64).T, rows('dwi', 8).T,
                        w['dsa_w_uq'][l], w['dsa_w_uq_idx'][l], w['dsa_cq_norm'][l], w['dsa_q_norm'][l],
                        w['dsa_k_norm'][l], w['dsa_kidx_norm'][l])
    return fox, ssd, dsa, tok


def kernel_unfused(**w):
    x = np.asarray(w['x'], np.float32)
    w = {k: np.asarray(v, np.float32) for k, v in w.items()}
    consts = m_constants()
    cores = [(b, h) for b in range(4) for h in range(2)]
    xT = [np.ascontiguousarray(x[b, h * TOK:(h + 1) * TOK].T) for b, h in cores]

    def a_inputs(l):
        return {'g1': _gT(w['ffn1_norm'][l]), 'gm': _gT(w['mix_norm'][l]), 'wg1': w['ffn1_w_gate'][l],
                'wu1': w['ffn1_w_up'][l], 'wd1': w['ffn1_w_down'][l], 'w_in': w['w_in'][l]}

    def c_inputs(l):
        return {'g2': _gT(w['ffn2_norm'][l]), 'wg2': w['ffn2_w_gate'][l], 'wu2': w['ffn2_w_up'][l],
                'wd2': w['ffn2_w_down'][l], 'w_out': w['w_out'][l]}

    res = _run('A', [dict(a_inputs(0), xT=xT[c]) for c in range(NCORE)])
    x1T = [r['x1oT'] for r in res]
    projT = [r['projT'] for r in res]
    out = None
    for l in range(DEPTH):
        fox_in, ssd_in, dsa_in, toks = [], [], [], []
        for b in range(4):
            PT = np.concatenate([projT[2 * b][:N_IN], projT[2 * b + 1][:N_IN]], axis=1)
            for j in range(2):
                f, s_, d, tok = _mixer_inputs(PT, j, l, w)
                fox_in.append(dict(consts, **f))
                ssd_in.append(dict(consts, **s_))
                dsa_in.append(dict(consts, **d))
                toks.append(tok)
        rf = _run('fox', fox_in)
        rs = _run('ssd', ssd_in)
        rd = _run('dsa', dsa_in)
        mixT = []
        for b in range(4):
            M = np.empty((D_MODEL, S), np.float32)
            for j in range(2):
                c = 2 * b + j
                M[256 * j:256 * j + 256] = rf[c]['foxT']
                M[512 + 512 * j:1024 + 512 * j] = rs[c]['ssmo'].T
                M[1536:2048, toks[c]] = rd[c]['dsaT']
            mixT += [np.ascontiguousarray(M[:, :TOK]), np.ascontiguousarray(M[:, TOK:])]
        if l < DEPTH - 1:
            res = _run('CA', [dict(c_inputs(l), **a_inputs(l + 1), x1T=x1T[c], mixT=mixT[c]) for c in range(NCORE)])
            x1T = [r['x1oT'] for r in res]
            projT = [r['projT'] for r in res]
        else:
            res = _run('C', [dict(c_inputs(l), x1T=x1T[c], mixT=mixT[c]) for c in range(NCORE)])
            out = np.empty_like(x)
            for c, (b, h) in enumerate(cores):
                out[b, h * TOK:(h + 1) * TOK] = res[c]['x3T'].T
    return out


FM_SEGS = [(0, 1024), (1536, 1540), (2564, 4100), (4116, 4756), (4884, 4948)]
TM_SEGS = [(1024, 1536, 0), (1540, 2564, 512), (4100, 4116, 1536), (4756, 4884, 1552), (4948, 4956, 1680)]
TMC = 1688


def build_fused(depth):
    nc = bass.Bass("TRN2", target_bir_lowering=False)
    _DIN.clear()
    di = lambda n, s: _din(nc, n, s)
    dint = lambda n, s: nc.dram_tensor(n, s, F32, kind="Internal").ap()
    xin = di('xT', [D_MODEL, S])
    outT = nc.dram_tensor('outT', [D_MODEL, S], F32, kind="ExternalOutput").ap()
    xa, x2, xb = dint('i_xa', [D_MODEL, S]), dint('i_x2', [D_MODEL, S]), dint('i_xb', [D_MODEL, S])
    pj, ptm, mix = dint('i_pj', [N_INP, S]), dint('i_ptm', [S, TMC]), dint('i_mix', [D_MODEL, S])
    names = []
    gst = ExitStack()
    GS = {'sems': {}, 'cnt': {e: 0 for e in ENGINES}, 'dcnt': {e: 0 for e in ENGINES}, 'stack': gst}
    for l in range(depth):
        xsrc = xin if l == 0 else xb
        xdst = outT if l == depth - 1 else xb
        _SFX[0] = f'_A{l}'
        with ExitStack() as st:
            P = Prog(nc)
            C = t_setup(nc, P, st)
            gt = st.enter_context(_sbt(nc, 'gains', [128, 48], F32))
            gb = P.buf('gains')
            g1, gm = di(f'g1_l{l}', [128, 16]), di(f'gm_l{l}', [128, 16])
            wg1, wu1, wd1 = di(f'wg1_l{l}', [D_MODEL, D_FF]), di(f'wu1_l{l}', [D_MODEL, D_FF]), di(f'wd1_l{l}', [D_FF, D_MODEL])
            w_in = di(f'w_in_l{l}', [D_MODEL, N_IN])
            P.dma('sync', gt[:, 0:16], g1, writes=[gb])
            P.dma('sync', gt[:, 16:32], gm, writes=[gb])
            xinb = [P.buf() for _ in range(16)]
            xab = [P.buf() for _ in range(16)]
            pjb, ptb = {}, {}
            for t0 in range(0, S, TT):
                t_norm(C, xsrc, xinb, t0, gt[:, 0:16], gb)
                t_ffn_gu(C, wg1, wu1)
                t_mm_resid(C, wd1, 32, C.actT, lambda k, t: C.act_b[(k, t)], xsrc, xinb, xa, xab, t0, 0.5)
                t_norm(C, xa, xab, t0, gt[:, 16:32], gb)
                for lo, hi in FM_SEGS:
                    t_proj(C, w_in, N_IN, pj, pjb, t0, lo, hi)
                for lo, hi, off in TM_SEGS:
                    t_proj_tm(C, w_in, lo, hi, off, ptm, ptb, t0)
            P.finish(xab + list(pjb.values()) + list(ptb.values()))
            P.GS = GS
            P.emit(st)
        for part in ('fox', 'ssd', 'dsa'):
            for j in range(2):
                _SFX[0] = f'_{part}{l}{j}'
                with ExitStack() as st:
                    P = Prog(nc)
                    if part == 'fox':
                        if j == 1:
                            continue
                        src = {'fq': pj[OFF['fq']:OFF['fq'] + 512, :].rearrange("(h d) s -> h d s", d=128),
                               'fk': pj[OFF['fk']:OFF['fk'] + 512, :].rearrange("(h d) s -> h d s", d=128),
                               'fv': ptm[:, 0:512],
                               'fgr': pj[OFF['fg']:OFF['fg'] + 4, :].rearrange("h (j p) -> h j p", p=128),
                               'fgc': pj[OFF['fg']:OFF['fg'] + 4, :].rearrange("h (j p) -> h p j", p=128)}
                        dst = {'foxT': mix[0:512, :]}
                    elif part == 'ssd':
                        xo_ = OFF['xbc']

                        def rows(c, j=j, xo_=xo_):
                            r0 = (xo_ + 512 * j + 128 * c) if c < 4 else (xo_ + 1024 + 128 * j if c == 4 else xo_ + 1280 + 128 * j)
                            return pj[r0:r0 + 128, :]
                        src = {'ssd_rows': rows, 's_xbcT': pj[0:768, :], 's_z': ptm[:, 512 + 512 * j:1024 + 512 * j],
                               's_dt': ptm[:, 1536 + 8 * j:1544 + 8 * j]}
                        dst = {'ssmo': mix[512 + 512 * j:1024 + 512 * j, :]}
                    else:
                        src = {'d_cqT': pj[OFF['cq']:OFF['cq'] + 512, :], 'd_kT': pj[OFF['dk']:OFF['dk'] + 128, :],
                               'd_v': ptm[:, 1552:1680], 'd_kiT2': pj[OFF['dki']:OFF['dki'] + 64, :], 'd_wi': ptm[:, 1680:1688]}
                        dst = {'dsaT': mix[1536:2048, :]}
                    C = m_setup(nc, P, st, 3 if part == 'dsa' else 4, src, dst, (l, j), nt512=(1 if part == 'ssd' else (7 if part == 'fox' else 5)))
                    m_consts2(C)
                    C.pT = Pool(P, st, nc, 'pT_', 8 if part == 'fox' else 5, [128, 512], BF16)
                    if part == 'fox':
                        m_fox(C, nh=4)
                    elif part == 'ssd':
                        m_ssd(C)
                    else:
                        C.psT = Pool(P, st, nc, 'psT', 1, [128, 512], BF16, psum=True)
                        m_dsa(C)
                    P.finish(C.outs)
                    P.GS = GS
                    P.emit(st)
                    names += C.in_names
        _SFX[0] = f'_C{l}'
        with ExitStack() as st:
            P = Prog(nc)
            C = t_setup(nc, P, st)
            gt = st.enter_context(_sbt(nc, 'gains', [128, 48], F32))
            gb = P.buf('gains')
            g2 = di(f'g2_l{l}', [128, 16])
            w_out = di(f'w_out_l{l}', [D_MODEL, D_MODEL])
            wg2, wu2, wd2 = di(f'wg2_l{l}', [D_MODEL, D_FF]), di(f'wu2_l{l}', [D_MODEL, D_FF]), di(f'wd2_l{l}', [D_FF, D_MODEL])
            P.dma('sync', gt[:, 32:48], g2, writes=[gb])
            xab = [P.buf() for _ in range(16)]
            x2b = [P.buf() for _ in range(16)]
            x3b = [P.buf() for _ in range(16)]
            for t0 in range(0, S, TT):
                t_loadT(C, mix, t0)
                t_mm_resid(C, w_out, 16, C.hT, lambda k, t: C.hT_b[k], xa, xab, x2, x2b, t0, 1.0)
                t_norm(C, x2, x2b, t0, gt[:, 32:48], gb)
                t_ffn_gu(C, wg2, wu2)
                t_mm_resid(C, wd2, 32, C.actT, lambda k, t: C.act_b[(k, t)], x2, x2b, xdst, x3b, t0, 0.5)
            P.finish(x3b)
            P.GS = GS
            P.emit(st)
    gst.close()
    return nc, sorted(set(names))


def kernel_fused(w, depth=DEPTH):
    x = np.asarray(w['x'], np.float32)
    w = {k: np.asarray(v, np.float32) for k, v in w.items()}
    key = ('fused', depth)
    if key not in _PROGS:
        _PROGS[key] = build_fused(depth)
    nc, names = _PROGS[key]
    base = dict(m_constants())
    dummy = np.zeros((S, 1), np.float32)
    for l in range(depth):
        base.update({f'g1_l{l}': _gT(w['ffn1_norm'][l]), f'gm_l{l}': _gT(w['mix_norm'][l]), f'g2_l{l}': _gT(w['ffn2_norm'][l]),
                     f'wg1_l{l}': w['ffn1_w_gate'][l], f'wu1_l{l}': w['ffn1_w_up'][l], f'wd1_l{l}': w['ffn1_w_down'][l],
                     f'w_in_l{l}': w['w_in'][l], f'w_out_l{l}': w['w_out'][l],
                     f'wg2_l{l}': w['ffn2_w_gate'][l], f'wu2_l{l}': w['ffn2_w_up'][l], f'wd2_l{l}': w['ffn2_w_down'][l]})
        for j in range(2):
            h8 = slice(8 * j, 8 * j + 8)
            hs = [2 * j, 2 * j + 1]
            sel = np.concatenate([np.arange(512 * j, 512 * j + 512), 1024 + np.arange(128 * j, 128 * j + 128),
                                  1280 + np.arange(128 * j, 128 * j + 128)])
            sfx = f'_l{l}_j{j}'
            base['fnb' + sfx] = _bc(w['fox_fgate_b'][l])
            base['fgains' + sfx] = np.stack([w['fox_q_norm'][l], w['fox_k_norm'][l]], 1)
            base['s_convw' + sfx] = w['ssm_conv_w'][l][:, sel].T.reshape(6, 128, 4).transpose(1, 0, 2)
            base['s_convb' + sfx] = w['ssm_conv_b'][l][sel].reshape(6, 128).T
            base['s_dtb' + sfx] = np.broadcast_to(w['ssm_dt_bias'][l][h8][None, None, :], (128, 32, 8)).reshape(128, 256)
            base['s_alog' + sfx] = np.broadcast_to(w['ssm_a_log'][l][h8][None, None, :], (128, 32, 8)).reshape(128, 256)
            base['s_dsk' + sfx] = _bc(np.repeat(w['ssm_d'][l][h8], 64))
            base['s_ng' + sfx] = _bc(w['ssm_norm'][l][512 * j:512 * j + 512])
            z = np.zeros((S, 1), np.float32)
            prep, _ = dsa_prep(j, np.zeros((S, 512), np.float32), np.zeros((S, 128), np.float32), np.zeros((S, 128), np.float32),
                               np.zeros((S, 64), np.float32), np.zeros((S, 8), np.float32),
                               w['dsa_w_uq'][l], w['dsa_w_uq_idx'][l], w['dsa_cq_norm'][l], w['dsa_q_norm'][l],
                               w['dsa_k_norm'][l], w['dsa_kidx_norm'][l])
            for k in ('d_wuq', 'd_wuqi', 'd_gcq', 'd_gmisc'):
                base[k + sfx] = prep[k]
            for k in PER_J:
                base[k + f'_j{j}'] = prep[k]
            for k in GLOBAL_IN:
                base[k] = prep[k]
    in_maps = []
    for b in range(4):
        m = {k: np.ascontiguousarray(v, dtype=np.float32) for k, v in base.items() if k in names or not (k.startswith('c_') or k.startswith('d_') or k.startswith('s_') or k.startswith('f'))}
        m['xT'] = np.ascontiguousarray(x[b].T)
        in_maps.append(m)
    res = run_bass_kernel_spmd(nc, in_maps, core_ids=list(range(4))).results
    out = np.empty_like(x)
    for b in range(4):
        out[b] = res[b]['outT'].T
    return out


def kernel(**w):
    return kernel_fused(w, DEPTH)
```

```python
import numpy as np
from contextlib import ExitStack
import concourse.bass as bass
import concourse.mybir as mybir
from concourse.bass_utils import run_bass_kernel_spmd

F32 = mybir.dt.float32
BF16 = mybir.dt.bfloat16
AF = mybir.ActivationFunctionType
ALU = mybir.AluOpType
AX = mybir.AxisListType

ENGINES = ['tensor', 'vector', 'scalar', 'gpsimd', 'sync']
_SFX = ['']
_DIN = {}


def _sbt(nc, name, shape, dt):
    return nc.sbuf_tensor(name + _SFX[0], shape, dt)


def _pst(nc, name, shape, dt):
    return nc.psum_tensor(name + _SFX[0], shape, dt)


def _din(nc, name, shape):
    if name not in _DIN:
        _DIN[name] = nc.dram_tensor(name, shape, F32, kind="ExternalInput").ap()
    return _DIN[name]
INORDER_SAFE = {'tensor'}
SEM_LIMIT = 30000
NDMASEM = 6

D_MODEL = 2048
D_FF = 4096
N_IN = 4956
N_INP = 4992
NCORE = 8
TOK = 2048
TT = 1024


class Buf:
    __slots__ = ('name', 'w', 'r', 'rd')

    def __init__(self, name=''):
        self.name = name
        self.w = None
        self.r = {}
        self.rd = []


class Op:
    __slots__ = ('eng', 'fn', 'deps', 'dma', 'ms', 'ev')

    def __init__(self, eng, fn, dma):
        self.eng = eng
        self.fn = fn
        self.dma = dma
        self.deps = []
        self.ms = False
        self.ev = None


class Prog:
    def __init__(self, nc):
        self.nc = nc
        self.ops = {e: [] for e in ENGINES}
        self.dmaops = {e: [] for e in ENGINES}
        self.nbuf = 0
        self.clear_sems = False
        self.GS = None

    def buf(self, name=''):
        self.nbuf += 1
        return Buf(name or f'b{self.nbuf}')

    def add(self, eng, fn, reads=(), writes=(), dma=False):
        op = Op(eng, fn, dma)
        deps = {}

        def need(d, war=False):
            if d is None or d is op:
                return
            if (not d.dma) and (not dma) and d.eng == eng:
                if eng in INORDER_SAFE:
                    return
            deps[id(d)] = d

        for b in reads:
            need(b.w)
        for b in writes:
            need(b.w)
            for r in b.r.values():
                need(r, war=True)
            for r in b.rd:
                need(r)
        if dma:
            lst = self.dmaops[eng]
            if len(lst) >= NDMASEM:
                need(lst[len(lst) - NDMASEM])
            lst.append(op)
        op.deps = list(deps.values())
        for b in reads:
            if dma:
                b.rd.append(op)
            else:
                b.r[eng] = op
        for b in writes:
            b.w = op
            b.r = {}
            b.rd = []
        self.ops[eng].append(op)
        return op

    def finish(self, reads=()):
        op = self.add('sync', None, reads=reads)
        have = {id(d) for d in op.deps}
        for e in ENGINES:
            for d in self.dmaops[e][-NDMASEM:]:
                if id(d) not in have:
                    op.deps.append(d)
            if e != 'sync':
                last = [o for o in self.ops[e] if not o.dma and o.fn is not None]
                if last and id(last[-1]) not in have:
                    op.deps.append(last[-1])
        return op

    def mm(self, out, lhsT, rhs, start=True, stop=True, reads=(), writes=(), **kw):
        return self.add('tensor', lambda e: e.matmul(out, lhsT, rhs, start=start, stop=stop, **kw),
                        reads, writes)

    def dma(self, q, out, in_, reads=(), writes=(), **kw):
        return self.add(q, lambda e: e.dma_start(out=out, in_=in_, **kw), reads, writes, dma=True)

    def V(self, fn, reads=(), writes=()):
        return self.add('vector', fn, reads, writes)

    def S(self, fn, reads=(), writes=()):
        return self.add('scalar', fn, reads, writes)

    def G(self, fn, reads=(), writes=()):
        return self.add('gpsimd', fn, reads, writes)

    def emit(self, stack):
        nc = self.nc
        for e in ENGINES:
            for op in self.ops[e]:
                for d in op.deps:
                    if not d.dma:
                        d.ms = True
        G = self.GS if self.GS is not None else {'sems': {}, 'cnt': {e: 0 for e in ENGINES}, 'dcnt': {e: 0 for e in ENGINES}, 'stack': stack}
        semcache = G['sems']

        def getsem(name):
            if name not in semcache:
                semcache[name] = G['stack'].enter_context(nc.semaphore(name))
            return semcache[name]

        DLIM = SEM_LIMIT // 16
        for e in ENGINES:
            cnt = G['cnt'][e]
            for op in self.ops[e]:
                if op.dma:
                    continue
                if op.ms:
                    ep, v = divmod(cnt, SEM_LIMIT)
                    op.ev = (getsem(f'p_{e}_{ep}'), v + 1)
                    cnt += 1
            G['cnt'][e] = cnt
            base = G['dcnt'][e]
            for i, op in enumerate(self.dmaops[e]):
                gi = base + i
                ep, v = divmod(gi // NDMASEM, DLIM)
                op.ev = (getsem(f'd_{e}_{gi % NDMASEM}_{ep}'), 16 * (v + 1))
            nd = base + len(self.dmaops[e])
            G['dcnt'][e] = nd
        block = stack.enter_context(nc.Block())
        prog = self

        def run(e, eng):
            known = {}
            for op in prog.ops[e]:
                w = {}
                for d in op.deps:
                    s, v = d.ev
                    k = id(s)
                    if known.get(k, 0) >= v:
                        continue
                    if k not in w or w[k][1] < v:
                        w[k] = (s, v)
                for k, (s, v) in w.items():
                    eng.wait_ge(s, v)
                    known[k] = v
                if op.fn is None:
                    continue
                inst = op.fn(eng)
                if op.dma:
                    inst.then_inc(op.ev[0], 16)
                elif op.ms:
                    inst.then_inc(op.ev[0], 1)

        if self.ops['sync']:
            @block.sync
            def _(eng):
                run('sync', eng)
        if self.ops['tensor']:
            @block.tensor
            def _(eng):
                run('tensor', eng)
        if self.ops['vector']:
            @block.vector
            def _(eng):
                run('vector', eng)
        if self.ops['scalar']:
            @block.scalar
            def _(eng):
                run('scalar', eng)
        if self.ops['gpsimd']:
            @block.gpsimd
            def _(eng):
                run('gpsimd', eng)


class Pool:
    def __init__(self, P, st, nc, name, n, shape, dtype, psum=False):
        self.tiles = []
        self.bufs = []
        self.i = 0
        for i in range(n):
            if psum:
                t = st.enter_context(_pst(nc, f'{name}{i}', shape, dtype))
            else:
                t = st.enter_context(_sbt(nc, f'{name}{i}', shape, dtype))
            self.tiles.append(t)
            self.bufs.append(P.buf(f'{name}{i}'))

    def next(self):
        i = self.i % len(self.tiles)
        self.i += 1
        return self.tiles[i], self.bufs[i]


class TCtx:
    pass


def t_setup(nc, P, st):
    C = TCtx()
    C.nc, C.P = nc, P
    C.hT = st.enter_context(_sbt(nc, 'hT', [128, 16, TT], BF16))
    C.hT_b = [P.buf(f'hT{k}') for k in range(16)]
    C.actT = st.enter_context(_sbt(nc, 'actT', [128, 32, TT], BF16))
    C.act_b = {(f, t): P.buf(f'act{f}_{t}') for f in range(32) for t in range(TT // 512)}
    C.wslots = Pool(P, st, nc, 'wsl', 3, [128, 8192], BF16)
    C.xs = Pool(P, st, nc, 'xs', 3, [128, TT], F32)
    C.sq = Pool(P, st, nc, 'sq', 2, [128, TT], F32)
    C.xo = Pool(P, st, nc, 'xo', 3, [128, 512], F32)
    C.xr = Pool(P, st, nc, 'xr', 3, [128, 512], F32)
    C.sg = Pool(P, st, nc, 'sg', 3, [128, 512], F32)
    C.rstd = st.enter_context(_sbt(nc, 'rstd', [128, TT], F32))
    C.rstd_b = P.buf('rstd')
    C.ones = st.enter_context(_sbt(nc, 'onesf', [128, 128], F32))
    C.ones_b = P.buf('ones')
    C.ps = Pool(P, st, nc, 'ps', 8, [128, 512], F32, psum=True)
    P.V(lambda e: e.memset(C.ones[:], 1.0 / D_MODEL), writes=[C.ones_b])
    C.eps_t = st.enter_context(_sbt(nc, 'eps_t', [128, 1], F32))
    P.V(lambda e: e.memset(C.eps_t[:], 1e-6), writes=[C.ones_b])
    C.cp = 0
    return C


def t_norm(C, x_dram, xbufs, t0, g_tile, g_buf, eps=1e-6):
    P = C.P
    nt = TT // 512
    pss = [C.ps.next() for _ in range(nt)]
    for c in range(16):
        xs, xsb = C.xs.next()
        P.dma('sync', xs[:], x_dram[c * 128:(c + 1) * 128, t0:t0 + TT], reads=[xbufs[c]], writes=[xsb])
        sq, sqb = C.sq.next()
        P.S(lambda e, sq=sq, xs=xs: e.activation(sq[:], xs[:], AF.Square), reads=[xsb], writes=[sqb])
        for t in range(nt):
            ps, psb = pss[t]
            P.mm(ps[:], C.ones[:], sq[:, t * 512:(t + 1) * 512], start=(c == 0), stop=(c == 15),
                 reads=[C.ones_b, sqb], writes=[psb])
    for t in range(nt):
        ps, psb = pss[t]
        sg, sgb = C.sg.next()
        P.S(lambda e, ps=ps, sg=sg: e.activation(sg[:], ps[:], AF.Ln, bias=C.eps_t[:, 0:1]), reads=[psb, C.ones_b], writes=[sgb])
        P.S(lambda e, sg=sg, t=t: e.activation(C.rstd[:, t * 512:(t + 1) * 512], sg[:], AF.Exp, scale=-0.5),
            reads=[sgb], writes=[C.rstd_b])
    for c in range(16):
        xs, xsb = C.xs.next()
        P.dma('sync', xs[:], x_dram[c * 128:(c + 1) * 128, t0:t0 + TT], reads=[xbufs[c]], writes=[xsb])
        P.V(lambda e, xs=xs, c=c: e.scalar_tensor_tensor(C.hT[:, c, :], xs[:], g_tile[:, c:c + 1], C.rstd[:],
                                                         ALU.mult, ALU.mult),
            reads=[xsb, C.rstd_b, g_buf], writes=[C.hT_b[c]])


def t_loadT(C, src_dram, t0):
    P = C.P
    for c in range(16):
        P.dma('gpsimd', C.hT[:, c, :], src_dram[c * 128:(c + 1) * 128, t0:t0 + TT], writes=[C.hT_b[c]])


def t_wjob(C, Ws, nk, c0, cw, CB):
    P = C.P
    slot, _ = C.wslots.next()
    idx = (C.wslots.i - 1) % len(C.wslots.tiles)
    if not hasattr(C, 'wpb'):
        C.wpb = {}
    RG = 1024

    def reg(col):
        key = (idx, col // RG)
        if key not in C.wpb:
            C.wpb[key] = P.buf(f'w{key}')
        return C.wpb[key]
    views = []
    KP = 4
    for wi, W in enumerate(Ws):
        base = wi * nk * CB
        v = slot[:, base:base + nk * CB].rearrange("p (k c) -> p k c", c=CB)
        views.append(v)
        for k0 in range(0, nk, KP):
            a, b = base + k0 * CB, base + (k0 + KP) * CB
            wr = [reg(c) for c in range(a, b, RG)]
            src = W[k0 * 128:(k0 + KP) * 128, c0:c0 + cw].rearrange("(k p) c -> p k c", p=128)
            P.dma('gpsimd', v[:, k0:k0 + KP, :cw], src, writes=wr)

    class PB:
        def __init__(self, wi):
            self.wi = wi

        def __getitem__(self, kk):
            return None
    rb = lambda wi, k: reg(wi * nk * CB + k * CB)
    return views, rb, KP


def t_ffn_gu(C, wg, wu):
    P = C.P
    nt = TT // 512
    CB = 256
    for cb in range(D_FF // CB):
        views, pbufs, KP = t_wjob(C, [wg, wu], 16, cb * CB, CB, CB)
        for m in range(CB // 128):
            f = cb * (CB // 128) + m
            for t in range(nt):
                pg, pgb = C.ps.next()
                pu, pub = C.ps.next()
                for wi, (ps, psb) in enumerate([(pg, pgb), (pu, pub)]):
                    for k in range(16):
                        P.mm(ps[:], views[wi][:, k, m * 128:(m + 1) * 128], C.hT[:, k, t * 512:(t + 1) * 512],
                             start=(k == 0), stop=(k == 15),
                             reads=[pbufs(wi, k), C.hT_b[k]], writes=[psb])
                sg, sgb = C.sg.next()
                P.S(lambda e, sg=sg, pg=pg: e.activation(sg[:], pg[:], AF.Silu), reads=[pgb], writes=[sgb])
                P.V(lambda e, sg=sg, pu=pu, f=f, t=t: e.tensor_tensor(
                    C.actT[:, f, t * 512:(t + 1) * 512], sg[:], pu[:], ALU.mult),
                    reads=[sgb, pub], writes=[C.act_b[(f, t)]])


def t_mm_resid(C, W, nk, rhs, rhs_bufs, x_dram, xbufs, o_dram, obufs, t0, fac):
    P = C.P
    nt = TT // 512
    CB = 8192 // nk
    for cb in range(D_MODEL // CB):
        views, pbufs, KP = t_wjob(C, [W], nk, cb * CB, CB, CB)
        for m in range(CB // 128):
            c = cb * (CB // 128) + m
            for t in range(nt):
                ps, psb = C.ps.next()
                for k in range(nk):
                    P.mm(ps[:], views[0][:, k, m * 128:(m + 1) * 128], rhs[:, k, t * 512:(t + 1) * 512],
                         start=(k == 0), stop=(k == nk - 1),
                         reads=[pbufs(0, k), rhs_bufs(k, t)], writes=[psb])
                xr, xrb = C.xr.next()
                sl = slice(t0 + t * 512, t0 + (t + 1) * 512)
                P.dma('sync', xr[:], x_dram[c * 128:(c + 1) * 128, sl], reads=[xbufs[c]], writes=[xrb])
                xo, xob = C.xo.next()
                P.V(lambda e, xo=xo, ps=ps, xr=xr: e.scalar_tensor_tensor(xo[:], ps[:], fac, xr[:], ALU.mult, ALU.add),
                    reads=[psb, xrb], writes=[xob])
                P.dma('sync', o_dram[c * 128:(c + 1) * 128, sl], xo[:], reads=[xob], writes=[obufs[c]])


def t_proj(C, W, ncols, o_dram, obufs, t0, c_lo=0, c_hi=None):
    P = C.P
    nt = TT // 512
    CB = 512
    c_hi = ncols if c_hi is None else c_hi
    for col0 in range(c_lo, c_hi, CB):
        cw = min(CB, c_hi - col0)
        views, pbufs, KP = t_wjob(C, [W], 16, col0, cw, CB)
        for m in range((cw + 127) // 128):
            msz = min(128, cw - m * 128)
            r0 = col0 + m * 128
            for t in range(nt):
                ps, psb = C.ps.next()
                for k in range(16):
                    P.mm(ps[:msz, :], views[0][:, k, m * 128:m * 128 + msz], C.hT[:, k, t * 512:(t + 1) * 512],
                         start=(k == 0), stop=(k == 15),
                         reads=[pbufs(0, k), C.hT_b[k]], writes=[psb])
                xo, xob = C.xo.next()
                if C.cp % 2 == 0:
                    P.S(lambda e, xo=xo, ps=ps, msz=msz: e.copy(xo[:msz, :], ps[:msz, :]), reads=[psb], writes=[xob])
                else:
                    P.V(lambda e, xo=xo, ps=ps, msz=msz: e.tensor_copy(xo[:msz, :], ps[:msz, :]), reads=[psb], writes=[xob])
                C.cp += 1
                sl = slice(t0 + t * 512, t0 + (t + 1) * 512)
                ob = obufs.setdefault(r0, P.buf())
                P.dma('sync', o_dram[r0:r0 + msz, sl], xo[:msz, :], reads=[xob], writes=[ob])


def t_proj_tm(C, W, c_lo, c_hi, tm_off, o_dram, obufs, t0):
    P = C.P
    CB = 512
    for col0 in range(c_lo, c_hi, CB):
        cw = min(CB, c_hi - col0)
        views, pbufs, KP = t_wjob(C, [W], 16, col0, cw, CB)
        for tc in range(TT // 128):
            ps, psb = C.ps.next()
            for k in range(16):
                P.mm(ps[:, :cw], C.hT[:, k, tc * 128:(tc + 1) * 128], views[0][:, k, :cw],
                     start=(k == 0), stop=(k == 15), reads=[pbufs(0, k), C.hT_b[k]], writes=[psb])
            xo, xob = C.xo.next()
            if C.cp % 2 == 0:
                P.S(lambda e, xo=xo, ps=ps, cw=cw: e.copy(xo[:, :cw], ps[:, :cw]), reads=[psb], writes=[xob])
            else:
                P.V(lambda e, xo=xo, ps=ps, cw=cw: e.tensor_copy(xo[:, :cw], ps[:, :cw]), reads=[psb], writes=[xob])
            C.cp += 1
            ob = obufs.setdefault((col0, tc), P.buf())
            o0 = tm_off + (col0 - c_lo)
            P.dma('sync', o_dram[t0 + tc * 128:t0 + (tc + 1) * 128, o0:o0 + cw], xo[:, :cw], reads=[xob], writes=[ob])


def build_T(mode):
    _SFX[0] = ''
    nc = bass.Bass("TRN2", target_bir_lowering=False)
    di = lambda n, s: nc.dram_tensor(n, s, F32, kind="ExternalInput").ap()
    do = lambda n, s: nc.dram_tensor(n, s, F32, kind="ExternalOutput").ap()
    with ExitStack() as st:
        P = Prog(nc)
        C = t_setup(nc, P, st)
        gt = st.enter_context(_sbt(nc, 'gains', [128, 48], F32))
        gb = P.buf('gains')
        hb = lambda n: [P.buf(f'{n}{c}') for c in range(16)]
        obs = []
        if 'C' in mode:
            x1 = di('x1T', [D_MODEL, TOK])
            mix = di('mixT', [D_MODEL, TOK])
            w_out = di('w_out', [D_MODEL, D_MODEL])
            g2 = di('g2', [128, 16])
            wg2, wu2, wd2 = di('wg2', [D_MODEL, D_FF]), di('wu2', [D_MODEL, D_FF]), di('wd2', [D_FF, D_MODEL])
            x2 = do('x2T', [D_MODEL, TOK])
            x3 = do('x3T', [D_MODEL, TOK])
            P.dma('sync', gt[:, 32:48], g2, writes=[gb])
            x1b, x2b, x3b = hb('x1'), hb('x2'), hb('x3')
            for t0 in range(0, TOK, TT):
                t_loadT(C, mix, t0)
                t_mm_resid(C, w_out, 16, C.hT, lambda k, t: C.hT_b[k], x1, x1b, x2, x2b, t0, 1.0)
                t_norm(C, x2, x2b, t0, gt[:, 32:48], gb)
                t_ffn_gu(C, wg2, wu2)
                t_mm_resid(C, wd2, 32, C.actT, lambda k, t: C.act_b[(k, t)], x2, x2b, x3, x3b, t0, 0.5)
            xin, xinb = x3, x3b
            obs += x3b
        if 'A' in mode:
            if 'C' not in mode:
                xin = di('xT', [D_MODEL, TOK])
                xinb = hb('xin')
            g1 = di('g1', [128, 16])
            gm = di('gm', [128, 16])
            wg1, wu1, wd1 = di('wg1', [D_MODEL, D_FF]), di('wu1', [D_MODEL, D_FF]), di('wd1', [D_FF, D_MODEL])
            w_in = di('w_in', [D_MODEL, N_IN])
            xo1 = do('x1oT', [D_MODEL, TOK])
            proj = do('projT', [N_INP, TOK])
            P.dma('sync', gt[:, 0:16], g1, writes=[gb])
            P.dma('sync', gt[:, 16:32], gm, writes=[gb])
            xo1b = hb('xo1')
            pjb = {}
            for t0 in range(0, TOK, TT):
                t_norm(C, xin, xinb, t0, gt[:, 0:16], gb)
                t_ffn_gu(C, wg1, wu1)
                t_mm_resid(C, wd1, 32, C.actT, lambda k, t: C.act_b[(k, t)], xin, xinb, xo1, xo1b, t0, 0.5)
                t_norm(C, xo1, xo1b, t0, gt[:, 16:32], gb)
                t_proj(C, w_in, N_IN, proj, pjb, t0)
            obs += xo1b + list(pjb.values())
        P.finish(obs)
        P.emit(st)
    return nc


S = 4096
NCH = S // 128
SSD_STOP = 0
NEG = -30000.0


class MCtx:
    pass


PER_J = {'d_cosaq', 'd_sinaq', 'd_cosiq', 'd_siniq', 'd_maskT', 'd_maskQ'}
GLOBAL_IN = {'d_cosak', 'd_sinak', 'd_cosik', 'd_sinik'}


def m_setup(nc, P, st, npsA=4, src=None, dst=None, lj=None, nt512=5):
    C = MCtx()
    C.nc, C.P, C.st = nc, P, st
    C.n = 0
    C.src, C.dst, C.lj = src or {}, dst or {}, lj
    C.fused = lj is not None

    def sb(name, shape, dt=F32):
        t = st.enter_context(_sbt(nc, name, shape, dt))
        return t, P.buf(name)
    C.sb = sb
    C.in_names = []

    def di(n, s):
        if n in C.src:
            return C.src[n]
        if C.fused and not (n.startswith('c_') or n in GLOBAL_IN):
            n = n + (f'_j{lj[1]}' if n in PER_J else f'_l{lj[0]}_j{lj[1]}')
        C.in_names.append(n)
        return _din(nc, n, s)
    C.di = di

    def do(n, s):
        if n in C.dst:
            return C.dst[n]
        return nc.dram_tensor(n, s, F32, kind="ExternalOutput").ap()
    C.do = do
    C.ident, C.ident_b = sb('ident', [128, 128])
    C.tri, C.tri_b = sb('tri', [128, 128])
    C.maskT, C.maskT_b = sb('maskT', [128, 128], BF16)
    C.identb, C.identb_b = sb('identb', [128, 128], BF16)
    C.su32, C.su32_b = sb('su32', [32, 32])
    C.ones, C.ones_b = sb('ones', [128, 128])
    C.onesb, C.onesb_b = sb('onesb', [128, 128], BF16)
    C.one_c, C.one_cb = sb('one_c', [128, 4])
    c_ident, c_tri, c_mask, c_su = di('c_ident', [128, 128]), di('c_tri', [128, 128]), di('c_maskT', [128, 128]), di('c_su32', [32, 32])
    P.dma('sync', C.ident[:], c_ident, writes=[C.ident_b])
    P.dma('sync', C.tri[:], c_tri, writes=[C.tri_b])
    P.dma('sync', C.su32[:], c_su, writes=[C.su32_b])
    P.dma('gpsimd', C.maskT[:], c_mask, writes=[C.maskT_b])
    P.dma('gpsimd', C.identb[:], c_ident, writes=[C.identb_b])
    P.V(lambda e: e.memset(C.ones[:], 1.0), writes=[C.ones_b])
    P.V(lambda e: e.memset(C.onesb[:], 1.0), writes=[C.onesb_b])
    P.V(lambda e: e.memset(C.one_c[:], 1.0), writes=[C.one_cb])
    C.psA = Pool(P, st, nc, 'psA', npsA, [128, 512], F32, psum=True)
    C.psO = Pool(P, st, nc, 'psO', 2, [128, 512], F32, psum=True)
    C.psD = Pool(P, st, nc, 'psD', 2, [128, 512], F32, psum=True)
    C.t512 = Pool(P, st, nc, 't512_', nt512, [128, 512], F32)
    C.u512 = Pool(P, st, nc, 'u512_', 4, [128, 512], F32)
    C.outs = []
    return C


def fm_rmsnorm(C, src, srcb, npart, n, gcol, gb, inv_d, eps, out, outb, ones=None):
    P = C.P
    sq, sqb = C.t512.next()
    P.S(lambda e: e.activation(sq[:npart, :n], src, AF.Square), reads=[srcb], writes=[sqb])
    ps, psb = C.psA.next()
    om, omb = ones if ones is not None else (C.ones, C.ones_b)
    P.mm(ps[:npart, :n], om[:npart, :npart], sq[:npart, :n], reads=[omb, sqb], writes=[psb])
    sr, srb = C.t512.next()
    P.S(lambda e: e.activation(sr[:npart, :n], ps[:npart, :n], AF.Ln, bias=C.epsc[:npart, 0:1] if eps == 1e-6 else C.epsc[:npart, 1:2],
                               scale=inv_d), reads=[psb, C.epsc_b], writes=[srb])
    rs, rsb = C.t512.next()
    P.S(lambda e: e.activation(rs[:npart, :n], sr[:npart, :n], AF.Exp, scale=-0.5), reads=[srb], writes=[rsb])
    P.V(lambda e: e.scalar_tensor_tensor(out, src, gcol, rs[:npart, :n], ALU.mult, ALU.mult),
        reads=[srcb, rsb, gb], writes=[outb])


def m_consts2(C):
    C.epsc, C.epsc_b = C.sb('epsc', [128, 2])
    C.P.V(lambda e: e.memset(C.epsc[:, 0:1], 1e-6), writes=[C.epsc_b])
    C.P.V(lambda e: e.memset(C.epsc[:, 1:2], 1e-5), writes=[C.epsc_b])
    C.k255, C.k255_b = C.sb('k255', [128, 1])
    C.P.V(lambda e: e.memset(C.k255[:], TOPK - 0.5), writes=[C.k255_b])


def m_fox(C, nh=2):
    from itertools import zip_longest
    P, nc, sb, di = C.P, C.nc, C.sb, C.di
    fq, fk = di('fq', [nh, 128, S]), di('fk', [nh, 128, S])
    fv = di('fv', [S, nh * 128])
    fgc, fgr = di('fgc', [nh, 128, 32]), di('fgr', [nh, 32, 128])
    fnb = di('fnb', [128, nh])
    fg = di('fgains', [128, 2])
    foxT = C.do('foxT', [nh * 128, S])
    v, vb = sb('fvb', [128, NCH, nh * 128], BF16)
    P.dma('gpsimd', v[:], fv.rearrange("(j p) d -> p j d", p=128), writes=[vb])
    gn, gnb = sb('fgn', [128, 2 + nh])
    P.dma('sync', gn[:, 0:2], fg, writes=[gnb])
    P.dma('sync', gn[:, 2:2 + nh], fnb, writes=[gnb])
    gs, gsb = sb('fgs', [128, 2 + nh])
    P.V(lambda e: e.tensor_scalar(gs[:, 0:1], gn[:, 0:1], 128 ** -0.5, None, ALU.mult), reads=[gnb], writes=[gsb])
    P.V(lambda e: e.tensor_scalar(gs[:, 2:2 + nh], gn[:, 2:2 + nh], -1.0, None, ALU.mult), reads=[gnb], writes=[gsb])
    qn, kn = sb('fqn', [128, nh, S], BF16)[0], sb('fkn', [128, nh, S], BF16)[0]
    qnb = {(h, T): P.buf() for h in range(nh) for T in range(8)}
    knb = {(h, T): P.buf() for h in range(nh) for T in range(8)}
    negC, negCb = sb('negC', [128, nh, 32])
    obufs = []
    for h in range(nh):
        xc, xcb = sb(f'fxc{h}', [128, 32])
        xr, xrb = sb(f'fxr{h}', [32, 128])
        P.dma('sync', xc[:], fgc[h], writes=[xcb], allow_slow_non_contiguous=True)
        P.dma('sync', xr[:], fgr[h], writes=[xrb])
        P.S(lambda e, xc=xc, h=h: e.activation(xc[:], xc[:], AF.Exp, bias=gs[:, 2 + h:3 + h], scale=-1.0), reads=[xcb, gsb], writes=[xcb])
        P.S(lambda e, xc=xc: e.activation(xc[:], xc[:], AF.Ln, bias=C.one_c[:, 0:1], scale=1.0), reads=[xcb, C.one_cb], writes=[xcb])
        P.S(lambda e, xr=xr, h=h: e.activation(xr[:], xr[:], AF.Exp, bias=gs[:32, 2 + h:3 + h], scale=-1.0), reads=[xrb, gsb], writes=[xrb])
        P.S(lambda e, xr=xr: e.activation(xr[:], xr[:], AF.Ln, bias=C.one_c[:32, 0:1], scale=1.0), reads=[xrb, C.one_cb], writes=[xrb])
        rs, rsb = sb(f'frs{h}', [32, 1])
        P.V(lambda e, rs=rs, xr=xr: e.reduce_sum(rs[:], xr[:], axis=AX.X), reads=[xrb], writes=[rsb])
        rm, rmb = sb(f'frm{h}', [32, 128])
        P.V(lambda e, rm=rm, rs=rs: e.tensor_scalar(rm[:], C.ones[:32, :], rs[:, 0:1], None, ALU.mult), reads=[rsb, C.ones_b], writes=[rmb])
        ps, psb = C.psA.next()
        P.mm(ps[:, :32], C.tri[:], xc[:], start=True, stop=False, reads=[C.tri_b, xcb], writes=[psb])
        P.mm(ps[:, :32], rm[:], C.su32[:], start=False, stop=True, reads=[rmb, C.su32_b], writes=[psb])
        P.V(lambda e, ps=ps, h=h: e.tensor_copy(negC[:, h, :], ps[:, :32]), reads=[psb], writes=[negCb])
        for T in range(8):
            for (src, dst, dstb, gcol) in ((fq, qn, qnb, gs[:, 0:1]), (fk, kn, knb, gn[:, 1:2])):
                raw, rawb = C.u512.next()
                P.dma('sync', raw[:], src[h, :, T * 512:(T + 1) * 512], writes=[rawb])
                fm_rmsnorm(C, raw[:], rawb, 128, 512, gcol, gsb if src is fq else gnb, 1.0 / 128, 1e-6,
                           dst[:, h, T * 512:(T + 1) * 512], dstb[(h, T)])
    def att(h):
        for T in range(8):
            dg, dgb = C.t512.next()
            for r in range(4):
                P.V(lambda e, dg=dg, r=r, h=h, T=T: e.tensor_scalar(dg[:, r * 128:(r + 1) * 128], C.ident[:],
                                                                   negC[:, h, 4 * T + r:4 * T + r + 1], None, ALU.mult),
                    reads=[C.ident_b, negCb], writes=[dgb])
            pcb, pcbb = C.psA.next()
            P.mm(pcb[:], C.ones[:], dg[:], reads=[C.ones_b, dgb], writes=[pcbb])
            ncb, ncbb = C.u512.next()
            P.S(lambda e, ncb=ncb, pcb=pcb: e.copy(ncb[:], pcb[:]), reads=[pcbb], writes=[ncbb])
            po, pob = C.psO.next()
            pd, pdb = C.psD.next()
            nj = 4 * T + 4
            LA = 2
            staged = {}

            def stage1(j, T=T, h=h, ncb=ncb, ncbb=ncbb):
                r = j - 4 * T
                c0 = 0 if r < 0 else r * 128
                n = 512 - c0
                tsl = slice(T * 512 + c0, (T + 1) * 512)
                pS, pSb = C.psA.next()
                P.mm(pS[:, :n], kn[:, h, j * 128:(j + 1) * 128], qn[:, h, tsl], start=True, stop=(r < 0),
                     reads=[knb[(h, j // 4)], qnb[(h, T)]], writes=[pSb])
                if r >= 0:
                    P.mm(pS[:, :128], C.identb[:], C.maskT[:], start=False, stop=True,
                         reads=[C.identb_b, C.maskT_b], writes=[pSb])
                L, Lb = C.t512.next()
                P.V(lambda e, L=L, pS=pS, n=n, c0=c0: e.tensor_tensor(L[:, :n], pS[:, :n], ncb[:, c0:], ALU.subtract),
                    reads=[pSb, ncbb], writes=[Lb])
                pT, pTb = C.pT.next()
                P.S(lambda e, pT=pT, L=L, n=n, j=j: e.activation(pT[:, :n], L[:, :n], AF.Exp, bias=negC[:, h, j:j + 1], scale=1.0),
                    reads=[Lb, negCb], writes=[pTb])
                staged[j] = (pT, pTb, c0, n)

            for j in range(min(LA, nj)):
                stage1(j)
                yield
            for j in range(nj):
                if j + LA < nj:
                    stage1(j + LA)
                pT, pTb, c0, n = staged.pop(j)
                P.mm(po[:, c0:], v[:, j, h * 128:(h + 1) * 128], pT[:, :n], start=(j == 0), stop=(j == nj - 1),
                     reads=[vb, pTb], writes=[pob])
                P.mm(pd[:, c0:], C.onesb[:], pT[:, :n], start=(j == 0), stop=(j == nj - 1),
                     reads=[C.onesb_b, pTb], writes=[pdb])
                yield
            rc, rcb = C.t512.next()
            P.S(lambda e, rc=rc, pd=pd: e.activation(rc[:], pd[:], AF.Ln), reads=[pdb], writes=[rcb])
            P.S(lambda e, rc=rc: e.activation(rc[:], rc[:], AF.Exp, scale=-1.0), reads=[rcb], writes=[rcb])
            o, ob = C.u512.next()
            P.V(lambda e, o=o, po=po, rc=rc: e.tensor_tensor(o[:], po[:], rc[:], ALU.mult), reads=[pob, rcb], writes=[ob])
            hb_ = P.buf()
            P.dma('sync', foxT[h * 128:(h + 1) * 128, T * 512:(T + 1) * 512], o[:], reads=[ob], writes=[hb_])
            obufs.append(hb_)
            yield

    for h0 in range(0, nh, 2):
        for _ in zip_longest(att(h0), att(h0 + 1)):
            pass
    C.outs += obufs


def build_M(parts):
    nc = bass.Bass("TRN2", target_bir_lowering=False)
    _DIN.clear()
    _SFX[0] = ''
    with ExitStack() as st:
        P = Prog(nc)
        C = m_setup(nc, P, st, 3 if 'dsa' in parts else 4, nt512=(1 if parts == ['ssd'] else (7 if parts == ['fox'] else 5)))
        m_consts2(C)
        C.pT = Pool(P, st, nc, 'pT_', 8 if parts == ['fox'] else 5, [128, 512], BF16)
        if 'dsa' in parts:
            C.psT = Pool(P, st, nc, 'psT', 1, [128, 512], BF16, psum=True)
        if 'fox' in parts:
            m_fox(C)
        if 'ssd' in parts:
            m_ssd(C)
        if 'dsa' in parts:
            m_dsa(C)
        P.finish(C.outs)
        P.emit(st)
    return nc, list(C.in_names)


def _rope_mat(n, half, bases):
    m = np.zeros((n, n), np.float32)
    for b in bases:
        for d in range(half):
            m[b + d + half, b + d] = -1.0
            m[b + d, b + d + half] = 1.0
    return m


def rope_tables(rot, nrow, reps):
    half = rot // 2
    inv = (500000.0 ** (-np.arange(0, rot, 2, dtype=np.float32) / rot)).astype(np.float32)
    ang = np.arange(S, dtype=np.float32)[:, None] * inv[None, :]
    cos, sin = np.cos(ang).astype(np.float32), np.sin(ang).astype(np.float32)
    cf = np.ones((nrow, S), np.float32)
    sf = np.zeros((nrow, S), np.float32)
    cf[:half] = cos.T
    cf[half:rot] = cos.T
    sf[:half] = sin.T
    sf[half:rot] = sin.T
    return np.tile(cf, (reps, 1)), np.tile(sf, (reps, 1))


def dsa_prep(j, d_cq, d_k, d_v, d_ki, d_wi, w_uq, w_uqi, g_cq, g_q, g_k, g_ki):
    tiles = dsa_slot_tiles(j)
    tok = np.concatenate([np.arange(t * 128, (t + 1) * 128) for t in tiles])
    ca, sa = rope_tables(32, 128, 1)
    ci, si = rope_tables(16, 64, 2)
    i = np.arange(128)
    tri = np.where(i[:, None] <= i[None, :], 0.0, NEG).astype(np.float32)
    full = np.full((128, 128), NEG, np.float32)
    zero = np.zeros((128, 128), np.float32)
    mT, mQ = [], []
    for par in range(2):
        smaller = (j == 0 and par == 0) or (j == 1 and par == 1)
        A, B = (tri, full) if smaller else (zero, tri)
        for m in (A, B):
            mT.append(np.tile(m, (1, 4)))
            mQ.append(np.ascontiguousarray(m.T))
    return {
        'd_cqT': np.ascontiguousarray(d_cq[tok].T), 'd_kT': np.ascontiguousarray(d_k.T), 'd_v': np.ascontiguousarray(d_v),
        'd_kiT2': np.ascontiguousarray(np.concatenate([d_ki.T, d_ki.T], 0)),
        'd_wi': np.ascontiguousarray(d_wi[tok].reshape(NSLOT, 128, 8).transpose(1, 0, 2).reshape(128, NSLOT * 8)),
        'd_wuq': np.ascontiguousarray(w_uq), 'd_wuqi': np.ascontiguousarray(w_uqi),
        'd_gcq': np.ascontiguousarray(g_cq.reshape(4, 128).T),
        'd_gmisc': np.ascontiguousarray(np.stack([g_q, g_k, np.concatenate([g_ki, g_ki])], 1)),
        'd_cosaq': np.ascontiguousarray(ca[:, tok]), 'd_sinaq': np.ascontiguousarray(sa[:, tok]), 'd_cosak': ca, 'd_sinak': sa,
        'd_cosiq': np.ascontiguousarray(ci[:, tok]), 'd_siniq': np.ascontiguousarray(si[:, tok]), 'd_cosik': ci, 'd_sinik': si,
        'd_maskT': np.stack(mT), 'd_maskQ': np.stack(mQ),
    }, tok


def m_constants():
    i = np.arange(128)
    return {
        'c_ident': np.eye(128, dtype=np.float32),
        'c_tri': (i[:, None] <= i[None, :]).astype(np.float32),
        'c_maskT': np.where(i[:, None] <= i[None, :], 0.0, NEG).astype(np.float32),
        'c_su32': (np.arange(32)[:, None] < np.arange(32)[None, :]).astype(np.float32),
        'c_negones': -np.ones((128, 128), np.float32),
        'c_rma': _rope_mat(128, 16, [0]),
        'c_rmi': _rope_mat(128, 8, [0, 64]),
        'c_blk64': np.kron(np.eye(2, dtype=np.float32), np.ones((64, 64), np.float32)),
        'c_pow2': np.ascontiguousarray(np.broadcast_to((0.5 ** np.arange(1, NBIS + 1)).astype(np.float32)[None, :], (128, NBIS))),
        'c_mask4': np.tile(np.where(i[:, None] <= i[None, :], 0.0, NEG).astype(np.float32), (1, 4)),
    }


def m_ssd(C):
    P, nc, sb, di = C.P, C.nc, C.sb, C.di
    xbc = di('s_xbcT', [768, S])
    cwt = di('s_convw', [128, 6, 4])
    cbt_ = di('s_convb', [128, 6])
    zt = di('s_z', [S, 512])
    dtr = di('s_dt', [128, 256])
    dtb = di('s_dtb', [128, 256])
    alg = di('s_alog', [128, 256])
    dsk = di('s_dsk', [128, 512])
    ngn = di('s_ng', [128, 512])
    c_neg1 = di('c_negones', [128, 128])
    c_mask4 = di('c_mask4', [128, 512])
    ssmo = C.do('ssmo', [S, 512])
    cw, cwb = sb('s_cw', [128, 6, 4])
    cb, cbb = sb('s_cb', [128, 6])
    P.dma('sync', cw[:], cwt, writes=[cwb])
    P.dma('sync', cb[:], cbt_, writes=[cbb])
    negones, negb = sb('s_negones', [128, 128])
    mask4, mask4b = sb('s_mask4', [128, 512])
    P.dma('sync', negones[:], c_neg1, writes=[negb])
    P.dma('sync', mask4[:], c_mask4, writes=[mask4b])
    dskt, dskb = sb('s_dskt', [128, 512])
    ngt, ngb = sb('s_ngt', [128, 512])
    P.dma('sync', dskt[:], dsk, writes=[dskb])
    P.dma('sync', ngt[:], ngn, writes=[ngb])
    if SSD_STOP == -1:
        return
    dt, dtB = sb('s_dtv', [128, 256])
    ea_, eab_ = sb('s_ealog', [128, 256])
    tb_, tbb_ = sb('s_dtbv', [128, 256])
    if C.fused:
        P.dma('sync', dt[:].rearrange("p (j h) -> p j h", h=8), dtr.rearrange("(j p) h -> p j h", p=128), writes=[dtB],
              allow_slow_non_contiguous=True)
    else:
        P.dma('sync', dt[:], dtr, writes=[dtB])
    P.dma('sync', tb_[:], dtb, writes=[tbb_])
    P.dma('sync', ea_[:], alg, writes=[eab_])
    P.V(lambda e: e.tensor_tensor(dt[:], dt[:], tb_[:], ALU.add), reads=[dtB, tbb_], writes=[dtB])
    P.S(lambda e: e.activation(dt[:], dt[:], AF.Exp), reads=[dtB], writes=[dtB])
    P.S(lambda e: e.activation(dt[:], dt[:], AF.Ln, bias=C.one_c[:, 0:1], scale=1.0), reads=[dtB, C.one_cb], writes=[dtB])
    P.S(lambda e: e.activation(ea_[:], ea_[:], AF.Exp), reads=[eab_], writes=[eab_])
    if SSD_STOP == -2:
        return
    dA, dAb = sb('s_dA', [128, 256])
    P.V(lambda e: e.scalar_tensor_tensor(dA[:], dt[:], -1.0, ea_[:], ALU.mult, ALU.mult), reads=[dtB, eab_], writes=[dAb])
    pcs, pcsb = C.psA.next()
    ptot, ptotb = C.psA.next()
    P.mm(pcs[:, :256], C.tri[:], dA[:], reads=[C.tri_b, dAb], writes=[pcsb])
    P.mm(ptot[:, :256], C.ones[:], dA[:], reads=[C.ones_b, dAb], writes=[ptotb])
    if SSD_STOP == -3:
        return
    acs, acsb = sb('s_acs', [128, 256])
    P.V(lambda e: e.tensor_copy(acs[:], pcs[:, :256]), reads=[pcsb], writes=[acsb])
    if SSD_STOP == -4:
        return
    eacs, eacsb = sb('s_eacs', [128, 256])
    P.S(lambda e: e.activation(eacs[:], acs[:], AF.Exp), reads=[acsb], writes=[eacsb])
    if SSD_STOP == -5:
        return
    cd, cdb = sb('s_cd', [128, 256])
    tots, totsb = sb('s_tots', [128, 256])
    P.V(lambda e: e.tensor_copy(tots[:], ptot[:, :256]), reads=[ptotb], writes=[totsb])
    P.S(lambda e: e.activation(cd[:], tots[:], AF.Exp), reads=[totsb], writes=[cdb])
    if SSD_STOP == -6:
        return
    dec, decb = sb('s_dec', [128, 256])
    P.V(lambda e: e.tensor_tensor(dec[:], tots[:], acs[:], ALU.subtract), reads=[totsb, acsb], writes=[decb])
    P.S(lambda e: e.activation(dec[:], dec[:], AF.Exp), reads=[decb], writes=[decb])
    if SSD_STOP == 1:
        return
    x_tm, x_tmb = sb('s_xtm', [128, NCH, 512])
    xtb = [P.buf() for _ in range(NCH)]
    B_tm, _ = sb('s_Btm', [128, NCH, 128], BF16)
    Btb = [P.buf() for _ in range(NCH)]
    BT, BTb = sb('s_BT', [128, S], BF16)
    CT, CTb = sb('s_CT', [128, S], BF16)
    raws = Pool(P, C.st, nc, 's_raw', 1, [128, S + 3], F32)
    accs = Pool(P, C.st, nc, 's_acc', 1, [128, S], F32)
    for i in range(1):
        P.V(lambda e, i=i: e.memset(raws.tiles[i][:, 0:3], 0.0), writes=[raws.bufs[i]])
    for c in range(6):
        raw, rawb = raws.next()
        P.dma('sync', raw[:, 3:], C.src['ssd_rows'](c) if C.fused else xbc[c * 128:(c + 1) * 128, :], writes=[rawb])
        acc, accb = accs.next()
        P.V(lambda e, acc=acc, raw=raw, c=c: e.tensor_scalar(acc[:], raw[:, 0:S], cw[:, c, 0:1], None, ALU.mult),
            reads=[rawb, cwb], writes=[accb])
        for k in range(1, 4):
            P.V(lambda e, acc=acc, raw=raw, c=c, k=k: e.scalar_tensor_tensor(acc[:], raw[:, k:k + S], cw[:, c, k:k + 1], acc[:],
                                                                             ALU.mult, ALU.add),
                reads=[rawb, cwb, accb], writes=[accb])
        if c == 4:
            P.S(lambda e, acc=acc, c=c: e.activation(BT[:], acc[:], AF.Silu, bias=cb[:, c:c + 1], scale=1.0), reads=[accb, cbb], writes=[BTb])
        if c == 5:
            P.S(lambda e, acc=acc, c=c: e.activation(CT[:], acc[:], AF.Silu, bias=cb[:, c:c + 1], scale=1.0), reads=[accb, cbb], writes=[CTb])
            continue
        P.S(lambda e, acc=acc, c=c: e.activation(acc[:], acc[:], AF.Silu, bias=cb[:, c:c + 1], scale=1.0), reads=[accb, cbb], writes=[accb])
        for g in range(NCH // 4):
            pt, ptb = C.psA.next()
            for jj in range(4):
                j = 4 * g + jj
                P.add('tensor', lambda e, pt=pt, acc=acc, jj=jj, j=j: e.transpose(pt[:, jj * 128:(jj + 1) * 128],
                                                                                   acc[:, j * 128:(j + 1) * 128], C.ident[:]),
                      reads=[accb, C.ident_b], writes=[ptb])
            if c < 4:
                dst = x_tm[:, 4 * g:4 * g + 4, c * 128:(c + 1) * 128]
                wb_ = xtb[4 * g:4 * g + 4]
            else:
                dst = B_tm[:, 4 * g:4 * g + 4, :]
                wb_ = Btb[4 * g:4 * g + 4]
            src = pt[:].rearrange("p (a b) -> p a b", a=4)
            if g % 2 == 0:
                P.V(lambda e, dst=dst, src=src: e.tensor_copy(dst, src), reads=[ptb], writes=wb_)
            else:
                P.S(lambda e, dst=dst, src=src: e.copy(dst, src), reads=[ptb], writes=wb_)
    if SSD_STOP == 2:
        return
    xDs = Pool(P, C.st, nc, 's_xD', 2, [128, 512], F32)
    Hf, Hfb = sb('s_Hf', [128, 512])
    Hbs = Pool(P, C.st, nc, 's_Hb', 2, [128, 512], BF16)
    Zs = Pool(P, C.st, nc, 's_Z', 2, [128, 8, 128], F32)
    Es = Pool(P, C.st, nc, 's_E', 2, [128, 1024], F32)
    SCs = Pool(P, C.st, nc, 's_SC', 2, [128, 8, 128], BF16)
    cbts = Pool(P, C.st, nc, 's_cbt', 2, [128, 128], F32)
    xdts = Pool(P, C.st, nc, 's_xdt', 2, [128, 512], BF16)
    xdds = Pool(P, C.st, nc, 's_xdd', 2, [128, 512], BF16)
    zts = Pool(P, C.st, nc, 's_zt', 2, [128, 512], F32)
    sss = Pool(P, C.st, nc, 's_ss', 2, [128, 2], F32)
    y0s = Pool(P, C.st, nc, 's_y0', 2, [128, 512], F32)
    psts = Pool(P, C.st, nc, 's_pst', 2, [128, 512], F32)
    ys = Pool(P, C.st, nc, 's_y', 2, [128, 512], F32)
    psE = [C.psO, C.psD]
    NJ = NCH if SSD_STOP == 0 else SSD_STOP - 2
    st1 = {}
    hb_prev = [None]

    def stage1(j):
        hs = slice(j * 8, (j + 1) * 8)
        xDj, xDjb = xDs.next()
        P.G(lambda e: e.tensor_tensor(xDj[:], x_tm[:, j, :], dskt[:], ALU.mult), reads=[xtb[j], dskb], writes=[xDjb])
        Z, Zb = Zs.next()
        P.V(lambda e: e.tensor_tensor(Z[:], C.tri[:].unsqueeze(1).to_broadcast([128, 8, 128]),
                                      dA[:, hs].unsqueeze(2).to_broadcast([128, 8, 128]), ALU.mult),
            reads=[C.tri_b, dAb], writes=[Zb])
        Zf = Z[:].rearrange("p h l -> p (h l)")
        E, Eb = Es.next()
        for half in range(2):
            pe, peb = psE[half].next()
            P.mm(pe[:], C.ones[:], Zf[:, half * 512:(half + 1) * 512], start=True, stop=False, reads=[C.ones_b, Zb], writes=[peb])
            for hh in range(4):
                h = half * 4 + hh
                P.mm(pe[:, hh * 128:(hh + 1) * 128], Z[:, h, :], negones[:], start=False, stop=False, reads=[Zb, negb], writes=[peb])
            P.mm(pe[:], C.ident[:], mask4[:], start=False, stop=True, reads=[C.ident_b, mask4b], writes=[peb])
            P.S(lambda e, pe=pe, half=half: e.activation(E[:, half * 512:(half + 1) * 512], pe[:], AF.Exp), reads=[peb], writes=[Eb])
        pcb_, pcbb_ = C.psA.next()
        P.mm(pcb_[:, :128], BT[:, j * 128:(j + 1) * 128], CT[:, j * 128:(j + 1) * 128], reads=[BTb, CTb], writes=[pcbb_])
        cbt, cbtb = cbts.next()
        P.V(lambda e: e.tensor_copy(cbt[:], pcb_[:, :128]), reads=[pcbb_], writes=[cbtb])
        SC, SCb = SCs.next()
        P.V(lambda e: e.tensor_tensor(SC[:], E[:].rearrange("p (h l) -> p h l", h=8),
                                      cbt[:].unsqueeze(1).to_broadcast([128, 8, 128]), ALU.mult),
            reads=[Eb, cbtb], writes=[SCb])
        xdt, xdtb = xdts.next()
        xdd, xddb = xdds.next()
        x3 = x_tm[:, j, :].rearrange("p (h d) -> p h d", h=8)
        P.V(lambda e: e.tensor_tensor(xdt[:].rearrange("p (h d) -> p h d", h=8), x3,
                                      dt[:, hs].unsqueeze(2).to_broadcast([128, 8, 64]), ALU.mult),
            reads=[xtb[j], dtB], writes=[xdtb])
        P.V(lambda e: e.tensor_tensor(xdd[:].rearrange("p (h d) -> p h d", h=8),
                                      xdt[:].rearrange("p (h d) -> p h d", h=8),
                                      dec[:, hs].unsqueeze(2).to_broadcast([128, 8, 64]), ALU.mult),
            reads=[xdtb, decb], writes=[xddb])
        py, pyb = C.psA.next()
        for h in range(8):
            P.mm(py[:, h * 64:(h + 1) * 64], SC[:, h, :], xdt[:, h * 64:(h + 1) * 64], reads=[SCb, xdtb], writes=[pyb])
        y0, y0b = y0s.next()
        P.V(lambda e: e.tensor_tensor(y0[:], py[:], xDj[:], ALU.add), reads=[pyb, xDjb], writes=[y0b])
        pstt = None
        if j < NCH - 1:
            pst, pstb = C.psA.next()
            P.mm(pst[:], B_tm[:, j, :], xdd[:], reads=[Btb[j], xddb], writes=[pstb])
            pss_, pssb = psts.next()
            P.S(lambda e: e.copy(pss_[:], pst[:]), reads=[pstb], writes=[pssb])
            pstt = (pss_, pssb)
        z, zb = zts.next()
        P.dma('sync', z[:], zt[j * 128:(j + 1) * 128, :], writes=[zb])
        P.S(lambda e: e.activation(z[:], z[:], AF.Silu), reads=[zb], writes=[zb])
        st1[j] = (y0, y0b, pstt, z, zb)

    def stage2(j):
        hs = slice(j * 8, (j + 1) * 8)
        y0, y0b, pstt, z, zb = st1.pop(j)
        y, yb = ys.next()
        if j > 0:
            Hbp, Hbpb = hb_prev[0]
            pyo, pyob = C.psA.next()
            P.mm(pyo[:], CT[:, j * 128:(j + 1) * 128], Hbp[:], reads=[CTb, Hbpb], writes=[pyob])
        if j < NCH - 1:
            pss_, pssb = pstt
            if j == 0:
                P.V(lambda e: e.tensor_copy(Hf[:], pss_[:]), reads=[pssb], writes=[Hfb])
            else:
                P.V(lambda e: e.tensor_tensor(Hf[:].rearrange("p (h d) -> p h d", h=8), Hf[:].rearrange("p (h d) -> p h d", h=8),
                                              cd[:, hs].unsqueeze(2).to_broadcast([128, 8, 64]), ALU.mult),
                    reads=[Hfb, cdb], writes=[Hfb])
                P.V(lambda e: e.tensor_tensor(Hf[:], Hf[:], pss_[:], ALU.add), reads=[Hfb, pssb], writes=[Hfb])
            Hbn, Hbnb = Hbs.next()
            P.S(lambda e: e.copy(Hbn[:], Hf[:]), reads=[Hfb], writes=[Hbnb])
            hb_prev[0] = (Hbn, Hbnb)
        if j > 0:
            P.V(lambda e: e.tensor_tensor(y[:].rearrange("p (h d) -> p h d", h=8),
                                          pyo[:].rearrange("p (h d) -> p h d", h=8),
                                          eacs[:, hs].unsqueeze(2).to_broadcast([128, 8, 64]), ALU.mult),
                reads=[pyob, eacsb], writes=[yb])
            P.V(lambda e: e.tensor_tensor(y[:], y[:], y0[:], ALU.add), reads=[yb, y0b], writes=[yb])
            ysrc, ysrcb = y, yb
        else:
            ysrc, ysrcb = y0, y0b
        P.V(lambda e: e.tensor_tensor(y[:], ysrc[:], z[:], ALU.mult), reads=[ysrcb, zb], writes=[yb])
        ss, ssb = sss.next()
        P.V(lambda e: e.memset(ss[:], 0.0), writes=[ssb])
        P.S(lambda e: e.activation(z[:], y[:], AF.Square, accum_out=ss[:, 0:1]), reads=[yb, zb, ssb], writes=[zb, ssb])
        P.S(lambda e: e.activation(ss[:, 1:2], ss[:, 0:1], AF.Ln, bias=C.epsc[:, 1:2], scale=1.0 / 512),
            reads=[ssb, C.epsc_b], writes=[ssb])
        P.S(lambda e: e.activation(ss[:, 1:2], ss[:, 1:2], AF.Exp, scale=-0.5), reads=[ssb], writes=[ssb])
        o, ob = C.u512.next()
        P.V(lambda e: e.scalar_tensor_tensor(o[:], y[:], ss[:, 1:2], ngt[:], ALU.mult, ALU.mult),
            reads=[yb, ssb, ngb], writes=[ob])
        hb_ = P.buf()
        if C.fused:
            ptr, ptrb = C.psA.next()
            for a in range(4):
                P.add('tensor', lambda e, a=a: e.transpose(ptr[:, a * 128:(a + 1) * 128], o[:, a * 128:(a + 1) * 128], C.ident[:]),
                      reads=[ob, C.ident_b], writes=[ptrb])
            oT, oTb = C.u512.next()
            P.S(lambda e: e.copy(oT[:], ptr[:]), reads=[ptrb], writes=[oTb])
            P.dma('sync', ssmo[:, j * 128:(j + 1) * 128].rearrange("(a p) t -> p a t", p=128),
                  oT[:].rearrange("p (a t) -> p a t", a=4), reads=[oTb], writes=[hb_])
        else:
            P.dma('sync', ssmo[j * 128:(j + 1) * 128, :], o[:], reads=[ob], writes=[hb_])
        C.outs.append(hb_)

    if NJ > 0:
        stage1(0)
    for j in range(NJ):
        if j + 1 < NJ:
            stage1(j + 1)
        stage2(j)


NSLOT = 16
NBIS = 16
TOPK = 256


def dsa_slot_tiles(j):
    out = []
    for m in range(8):
        out += [4 * m, 4 * m + 3] if j == 0 else [4 * m + 1, 4 * m + 2]
    return out


DSA_NCH = [(4 * (k // 2) + 2) if k % 2 == 0 else (4 * (k // 2) + 4) for k in range(NSLOT)]


def fm_rope(C, x, xb, cosd, sind, c0, n, rm, rmb, out, outb, out3=False):
    P = C.P
    ct, ctb = C.u512.next()
    st_, stb = C.u512.next()
    P.dma('sync', ct[:, :n], cosd[:, c0:c0 + n], writes=[ctb])
    P.dma('sync', st_[:, :n], sind[:, c0:c0 + n], writes=[stb])
    pr, prb = C.psA.next()
    P.mm(pr[:, :n], rm[:], x, reads=[rmb, xb], writes=[prb])
    P.V(lambda e: e.tensor_tensor(st_[:, :n], pr[:, :n], st_[:, :n], ALU.mult), reads=[prb, stb], writes=[stb])
    P.G(lambda e: e.tensor_tensor(ct[:, :n], x, ct[:, :n], ALU.mult), reads=[xb, ctb], writes=[ctb])
    if out3:
        P.V(lambda e: e.tensor_tensor(out, ct[:, :n].rearrange("p (a b) -> p a b", b=128), st_[:, :n].rearrange("p (a b) -> p a b", b=128), ALU.add),
            reads=[ctb, stb], writes=[outb])
    else:
        P.V(lambda e: e.tensor_tensor(out, ct[:, :n], st_[:, :n], ALU.add), reads=[ctb, stb], writes=[outb])


def m_dsa(C):
    P, nc, sb, di = C.P, C.nc, C.sb, C.di
    NQ = NSLOT * 128
    cq = di('d_cqT', [512, NQ])
    dk = di('d_kT', [128, S])
    dvt = di('d_v', [S, 128])
    kid = di('d_kiT2', [128, S])
    wid = di('d_wi', [128, NSLOT * 8])
    wuq = di('d_wuq', [512, 512])
    wuqi = di('d_wuqi', [512, 512])
    gcq = di('d_gcq', [128, 4])
    gmisc = di('d_gmisc', [128, 3])
    cosaq, sinaq = di('d_cosaq', [128, NQ]), di('d_sinaq', [128, NQ])
    cosak, sinak = di('d_cosak', [128, S]), di('d_sinak', [128, S])
    cosiq, siniq = di('d_cosiq', [128, NQ]), di('d_siniq', [128, NQ])
    cosik, sinik = di('d_cosik', [128, S]), di('d_sinik', [128, S])
    rma_d, rmi_d = di('c_rma', [128, 128]), di('c_rmi', [128, 128])
    blk_d = di('c_blk64', [128, 128])
    pow2_d = di('c_pow2', [128, NBIS])
    mT_d = di('d_maskT', [4, 128, 512])
    mQ_d = di('d_maskQ', [4, 128, 128])
    dsaT = C.do('dsaT', [512, NQ])
    rma, rmab = sb('d_rma', [128, 128])
    rmi, rmib = sb('d_rmi', [128, 128])
    blk, blkb = sb('d_blk', [128, 128])
    pow2, pow2b = sb('d_pow2', [128, NBIS])
    P.dma('sync', rma[:], rma_d, writes=[rmab])
    P.dma('sync', rmi[:], rmi_d, writes=[rmib])
    P.dma('sync', blk[:], blk_d, writes=[blkb])
    P.dma('sync', pow2[:], pow2_d, writes=[pow2b])
    mT, mTb = sb('d_mT', [128, 4, 512], BF16)
    mQ, mQb = sb('d_mQ', [128, 4, 128])
    for i in range(4):
        P.dma('gpsimd', mT[:, i, :], mT_d[i], writes=[mTb])
        P.dma('sync', mQ[:, i, :], mQ_d[i], writes=[mQb])
    gq, gqb = sb('d_gq', [128, 8])
    P.dma('sync', gq[:, 0:4], gcq, writes=[gqb])
    P.dma('sync', gq[:, 4:7], gmisc, writes=[gqb])
    gqs, gqsb = sb('d_gqs', [128, 1])
    P.V(lambda e: e.tensor_scalar(gqs[:], gq[:, 4:5], 128 ** -0.5, None, ALU.mult), reads=[gqb], writes=[gqsb])
    wi, wib = sb('d_wis', [128, NSLOT * 8])
    tiles_ = dsa_slot_tiles(C.lj[1]) if C.fused else None
    if C.fused:
        for k_ in range(NSLOT):
            P.dma('sync', wi[:, k_ * 8:(k_ + 1) * 8], wid[tiles_[k_] * 128:(tiles_[k_] + 1) * 128, :], writes=[wib])
    else:
        P.dma('sync', wi[:], wid, writes=[wib])
    P.V(lambda e: e.tensor_scalar(wi[:], wi[:], (8 ** -0.5) * (64 ** -0.5), None, ALU.mult), reads=[wib], writes=[wib])
    wq, wqb = sb('d_wq', [128, 4, 512], BF16)
    wqi, wqib = sb('d_wqi', [128, 4, 512], BF16)
    P.dma('gpsimd', wq[:], wuq.rearrange("(k p) c -> p k c", p=128), writes=[wqb])
    P.dma('gpsimd', wqi[:], wuqi.rearrange("(k p) c -> p k c", p=128), writes=[wqib])
    dv, dvb = sb('d_dv', [128, NCH, 128], BF16)
    P.dma('gpsimd', dv[:], dvt.rearrange("(j p) d -> p j d", p=128), writes=[dvb])
    dkT, _ = sb('d_dkT', [128, S], BF16)
    kiT, _ = sb('d_kiT', [128, S], BF16)
    dkb = [P.buf() for _ in range(8)]
    kib = [P.buf() for _ in range(8)]
    for T in range(8):
        sl = slice(T * 512, (T + 1) * 512)
        raw, rawb = C.u512.next()
        P.dma('sync', raw[:], dk[:, sl], writes=[rawb])
        nr, nrb = C.t512.next()
        fm_rmsnorm(C, raw[:], rawb, 128, 512, gq[:, 5:6], gqb, 1.0 / 128, 1e-6, nr[:], nrb)
        fm_rope(C, nr[:], nrb, cosak, sinak, T * 512, 512, rma, rmab, dkT[:, sl], dkb[T])
        raw, rawb = C.u512.next()
        if C.fused:
            P.dma('sync', raw[0:64, :], kid[:, sl], writes=[rawb])
            P.dma('sync', raw[64:128, :], kid[:, sl], writes=[rawb])
        else:
            P.dma('sync', raw[:], kid[:, sl], writes=[rawb])
        nr, nrb = C.t512.next()
        fm_rmsnorm(C, raw[:], rawb, 128, 512, gq[:, 6:7], gqb, 1.0 / 64, 1e-6, nr[:], nrb, ones=(blk, blkb))
        fm_rope(C, nr[:], nrb, cosik, sinik, T * 512, 512, rmi, rmib, kiT[:, sl], kib[T])
    cqn, _ = sb('d_cqn', [128, 4, NQ], BF16)
    cqnb = [P.buf() for _ in range(NQ // 512)]
    dqT, _ = sb('d_dqT', [128, NSLOT, 4, 128], BF16)
    dqb = [P.buf() for _ in range(NQ // 512)]
    qiT, _ = sb('d_qiT', [128, 4, NQ], BF16)
    qib = [P.buf() for _ in range(NQ // 512)]
    cqr = Pool(P, C.st, nc, 'd_cqr', 1, [128, 4, 512], F32)
    for T in range(NQ // 512):
        sl = slice(T * 512, (T + 1) * 512)
        raw, rawb = cqr.next()
        if C.fused:
            for q_ in range(4):
                tl = tiles_[4 * T + q_]
                P.dma('sync', raw[:, :, q_ * 128:(q_ + 1) * 128], cq[:, tl * 128:(tl + 1) * 128].rearrange("(k p) t -> p k t", p=128), writes=[rawb])
        else:
            P.dma('sync', raw[:], cq[:, sl].rearrange("(k p) t -> p k t", p=128), writes=[rawb])
        ps, psb = C.psA.next()
        for k in range(4):
            sq, sqb = C.t512.next()
            P.S(lambda e, sq=sq, raw=raw, k=k: e.activation(sq[:], raw[:, k, :], AF.Square), reads=[rawb], writes=[sqb])
            P.mm(ps[:], C.ones[:], sq[:], start=(k == 0), stop=(k == 3), reads=[C.ones_b, sqb], writes=[psb])
        sr, srb = C.t512.next()
        P.S(lambda e, sr=sr, ps=ps: e.activation(sr[:], ps[:], AF.Ln, bias=C.epsc[:, 0:1], scale=1.0 / 512), reads=[psb, C.epsc_b], writes=[srb])
        rs, rsb = C.t512.next()
        P.S(lambda e, rs=rs, sr=sr: e.activation(rs[:], sr[:], AF.Exp, scale=-0.5), reads=[srb], writes=[rsb])
        for k in range(4):
            P.V(lambda e, raw=raw, rs=rs, k=k, sl=sl: e.scalar_tensor_tensor(cqn[:, k, sl], raw[:, k, :], gq[:, k:k + 1], rs[:], ALU.mult, ALU.mult),
                reads=[rawb, rsb, gqb], writes=[cqnb[T]])
        for h in range(4):
            pq, pqb = C.psA.next()
            for k in range(4):
                P.mm(pq[:], wq[:, k, h * 128:(h + 1) * 128], cqn[:, k, sl], start=(k == 0), stop=(k == 3),
                     reads=[wqb, cqnb[T]], writes=[pqb])
            qf, qfb = C.u512.next()
            P.S(lambda e, qf=qf, pq=pq: e.copy(qf[:], pq[:]), reads=[pqb], writes=[qfb])
            nr, nrb = C.t512.next()
            fm_rmsnorm(C, qf[:], qfb, 128, 512, gqs[:, 0:1], gqsb, 1.0 / 128, 1e-6, nr[:], nrb)
            dst = dqT[:, 4 * T:4 * T + 4, h, :]
            fm_rope(C, nr[:], nrb, cosaq, sinaq, T * 512, 512, rma, rmab, dst, dqb[T], out3=True)
        for c in range(4):
            pq, pqb = C.psA.next()
            for k in range(4):
                P.mm(pq[:], wqi[:, k, c * 128:(c + 1) * 128], cqn[:, k, sl], start=(k == 0), stop=(k == 3),
                     reads=[wqib, cqnb[T]], writes=[pqb])
            qf, qfb = C.t512.next()
            P.S(lambda e, qf=qf, pq=pq: e.copy(qf[:], pq[:]), reads=[pqb], writes=[qfb])
            fm_rope(C, qf[:], qfb, cosiq, siniq, T * 512, 512, rmi, rmib, qiT[:, c, sl], qib[T])
    idxs = Pool(P, C.st, nc, 'd_idx', 2, [128, S], F32)
    sels = Pool(P, C.st, nc, 'd_sel', 2, [128, S], BF16)
    selTs = Pool(P, C.st, nc, 'd_selT', 2, [128, NCH, 128], BF16)
    rls = Pool(P, C.st, nc, 'd_rl', 3, [128, 512], F32)
    bst = Pool(P, C.st, nc, 'd_bs', 2, [128, 8], F32)
    Wt = Pool(P, C.st, nc, 'd_W', 2, [128, NBIS], F32)
    junks = Pool(P, C.st, nc, 'd_junk2_', 2, [128, S], BF16)

    def idx_phase(k):
        nch = DSA_NCH[k]
        n = nch * 128
        par = k % 2
        idx, idxb = idxs.next()
        for s0 in range(0, n, 512):
            sn = min(512, n - s0)
            for h in range(8):
                c, half = h // 2, h % 2
                pi, pib = C.psA.next()
                P.mm(pi[:, :sn], qiT[half * 64:(half + 1) * 64, c, k * 128:(k + 1) * 128],
                     kiT[half * 64:(half + 1) * 64, s0:s0 + sn], reads=[qib[k // 4], kib[s0 // 512]], writes=[pib])
                rl, rlb = rls.next()
                P.S(lambda e, rl=rl, pi=pi, sn=sn: e.activation(rl[:, :sn], pi[:, :sn], AF.Relu), reads=[pib], writes=[rlb])
                if h == 0:
                    P.V(lambda e, rl=rl, idx=idx, s0=s0, sn=sn, k=k, h=h: e.tensor_scalar(
                        idx[:, s0:s0 + sn], rl[:, :sn], wi[:, k * 8 + h:k * 8 + h + 1], None, ALU.mult),
                        reads=[rlb, wib], writes=[idxb])
                else:
                    P.V(lambda e, rl=rl, idx=idx, s0=s0, sn=sn, k=k, h=h: e.scalar_tensor_tensor(
                        idx[:, s0:s0 + sn], rl[:, :sn], wi[:, k * 8 + h:k * 8 + h + 1], idx[:, s0:s0 + sn], ALU.mult, ALU.add),
                        reads=[rlb, wib, idxb], writes=[idxb])
        bs, bsb = bst.next()
        W, Wb = Wt.next()
        jk, jkb = junks.next()
        return dict(k=k, nch=nch, n=n, par=par, idx=idx, idxb=idxb, bs=bs, bsb=bsb, W=W, Wb=Wb, jk=jk, jkb=jkb)

    def prep_ops(q):
        bs, bsb, idx, idxb, W, Wb, n, nch, par = q['bs'], q['bsb'], q['idx'], q['idxb'], q['W'], q['Wb'], q['n'], q['nch'], q['par']
        ops = [
            lambda: P.V(lambda e: e.tensor_reduce(bs[:, 0:1], idx[:, :n], AX.X, ALU.min), reads=[idxb], writes=[bsb]),
            lambda: P.V(lambda e: e.tensor_reduce(bs[:, 1:2], idx[:, :n], AX.X, ALU.max), reads=[idxb, bsb], writes=[bsb]),
            lambda: P.V(lambda e: e.tensor_tensor(bs[:, 1:2], bs[:, 1:2], bs[:, 0:1], ALU.subtract), reads=[bsb], writes=[bsb]),
            lambda: P.V(lambda e: e.tensor_scalar(W[:], pow2[:], bs[:, 1:2], None, ALU.mult), reads=[bsb, pow2b], writes=[Wb]),
        ]
        ops.append(lambda: P.V(lambda e: e.memset(bs[:, 5:6], float(n) - 2.0 * TOPK + 0.5), reads=[bsb], writes=[bsb]))
        for a_ in range(2):
            cc = nch - 2 + a_
            ops.append(lambda cc=cc, a_=a_: P.V(lambda e: e.tensor_tensor(idx[:, cc * 128:(cc + 1) * 128], idx[:, cc * 128:(cc + 1) * 128],
                                                                        mQ[:, par * 2 + a_, :], ALU.add),
                                                reads=[idxb, mQb], writes=[idxb]))
        return ops

    def bis_ops(q, it, on_act=False):
        bs, bsb, idx, idxb, W, Wb, n, jk, jkb = q['bs'], q['bsb'], q['idx'], q['idxb'], q['W'], q['Wb'], q['n'], q['jk'], q['jkb']
        if on_act:
            return [
                lambda: P.V(lambda e: e.tensor_tensor(bs[:, 2:3], bs[:, 0:1], W[:, it:it + 1], ALU.add), reads=[bsb, Wb], writes=[bsb]),
                lambda: P.V(lambda e: e.memset(bs[:, 3:4], 0.0), reads=[bsb], writes=[bsb]),
                lambda: P.S(lambda e: e.activation(jk[:, :n], idx[:, :n], AF.Sign, bias=bs[:, 2:3], scale=-1.0, accum_out=bs[:, 3:4]),
                            reads=[idxb, bsb, jkb], writes=[jkb, bsb]),
                lambda: P.V(lambda e: e.tensor_scalar(bs[:, 4:5], bs[:, 3:4], bs[:, 5:6], W[:, it:it + 1], ALU.is_le, ALU.mult),
                            reads=[bsb, Wb], writes=[bsb]),
                lambda: P.V(lambda e: e.tensor_tensor(bs[:, 0:1], bs[:, 0:1], bs[:, 4:5], ALU.add), reads=[bsb], writes=[bsb]),
            ]
        return [
            lambda: P.V(lambda e: e.tensor_tensor(bs[:, 2:3], bs[:, 0:1], W[:, it:it + 1], ALU.add), reads=[bsb, Wb], writes=[bsb]),
            lambda: P.V(lambda e: e.memset(bs[:, 3:4], 0.0), reads=[bsb], writes=[bsb]),
            lambda: P.V(lambda e: e.tensor_scalar(jk[:, :n], idx[:, :n], bs[:, 2:3], 0.0, ALU.is_ge, ALU.add, accum_out=bs[:, 3:4]),
                        reads=[idxb, bsb, jkb], writes=[jkb, bsb]),
            lambda: P.V(lambda e: e.tensor_scalar(bs[:, 4:5], bs[:, 3:4], C.k255[:, 0:1], W[:, it:it + 1], ALU.is_ge, ALU.mult),
                        reads=[bsb, Wb, C.k255_b], writes=[bsb]),
            lambda: P.V(lambda e: e.tensor_tensor(bs[:, 0:1], bs[:, 0:1], bs[:, 4:5], ALU.add), reads=[bsb], writes=[bsb]),
        ]

    def interleave(la, lb):
        for i in range(max(len(la), len(lb))):
            if i < len(la):
                la[i]()
            if i < len(lb):
                lb[i]()

    def post_phase(q):
        k, nch, n, par, idx, idxb, bs, bsb = q['k'], q['nch'], q['n'], q['par'], q['idx'], q['idxb'], q['bs'], q['bsb']
        sel, selb = sels.next()
        P.V(lambda e, sel=sel, idx=idx, bs=bs, n=n: e.tensor_scalar(sel[:, :n], idx[:, :n], bs[:, 0:1], None, ALU.is_ge),
            reads=[idxb, bsb], writes=[selb])
        selT, selTb = selTs.next()
        for g in range(0, nch, 4):
            gn = min(4, nch - g)
            pt, ptb = C.psT.next()
            for cc in range(gn):
                P.add('tensor', lambda e, pt=pt, sel=sel, cc=cc, g=g: e.transpose(pt[:, cc * 128:(cc + 1) * 128],
                                                                                 sel[:, (g + cc) * 128:(g + cc + 1) * 128], C.identb[:]),
                      reads=[selb, C.identb_b], writes=[ptb])
            src = pt[:, :gn * 128].rearrange("p (a b) -> p a b", a=gn)
            P.S(lambda e, selT=selT, src=src, g=g, gn=gn: e.copy(selT[:, g:g + gn, :], src), reads=[ptb], writes=[selTb])
        po, pob = C.psO.next()
        pd, pdb = C.psD.next()
        LA = 2
        staged = {}

        def stage1(c, k=k, nch=nch, par=par, selT=selT, selTb=selTb):
            pS, pSb = C.psA.next()
            last2 = c >= nch - 2
            P.mm(pS[:], dkT[:, c * 128:(c + 1) * 128], dqT[:, k, :, :].rearrange("p h t -> p (h t)"), start=True, stop=not last2,
                 reads=[dkb[c // 4], dqb[k // 4]], writes=[pSb])
            if last2:
                P.mm(pS[:], C.identb[:], mT[:, par * 2 + (c - (nch - 2)), :], start=False, stop=True,
                     reads=[C.identb_b, mTb], writes=[pSb])
            pT, pTb = C.pT.next()
            P.S(lambda e, pT=pT, pS=pS: e.activation(pT[:], pS[:], AF.Exp), reads=[pSb], writes=[pTb])
            P.V(lambda e, pT=pT, c=c: e.tensor_tensor(pT[:].rearrange("p (h t) -> p h t", h=4), pT[:].rearrange("p (h t) -> p h t", h=4),
                                                      selT[:, c, :].unsqueeze(1).to_broadcast([128, 4, 128]), ALU.mult),
                reads=[pTb, selTb], writes=[pTb])
            staged[c] = (pT, pTb)

        for c in range(min(LA, nch)):
            stage1(c)
        for c in range(nch):
            if c + LA < nch:
                stage1(c + LA)
            pT, pTb = staged.pop(c)
            P.mm(po[:], dv[:, c, :], pT[:], start=(c == 0), stop=(c == nch - 1), reads=[dvb, pTb], writes=[pob])
            P.mm(pd[:], C.onesb[:], pT[:], start=(c == 0), stop=(c == nch - 1), reads=[C.onesb_b, pTb], writes=[pdb])
        rc, rcb = C.t512.next()
        P.S(lambda e, rc=rc, pd=pd: e.activation(rc[:], pd[:], AF.Ln), reads=[pdb], writes=[rcb])
        P.S(lambda e, rc=rc: e.activation(rc[:], rc[:], AF.Exp, scale=-1.0), reads=[rcb], writes=[rcb])
        o, ob = C.u512.next()
        P.V(lambda e, o=o, po=po, rc=rc: e.tensor_tensor(o[:], po[:], rc[:], ALU.mult), reads=[pob, rcb], writes=[ob])
        hb_ = P.buf()
        kc = tiles_[k] if C.fused else k
        P.dma('sync', dsaT.rearrange("(h d) t -> d h t", d=128)[:, :, kc * 128:(kc + 1) * 128],
              o[:].rearrange("p (h t) -> p h t", h=4), reads=[ob], writes=[hb_])
        C.outs.append(hb_)

    for k0 in range(0, NSLOT, 2):
        qa, qb = idx_phase(k0), idx_phase(k0 + 1)
        interleave(prep_ops(qa), prep_ops(qb))
        for it in range(NBIS):
            interleave(bis_ops(qa, it), bis_ops(qb, it, on_act=True))
        for q in (qa, qb):
            post_phase(q)


_PROGS = {}
DEPTH = 4
OFF = dict(fq=0, fk=512, fv=1024, fg=1536, z=1540, xbc=2564, dt=4100, cq=4116, dk=4628, dv=4756, dki=4884, dwi=4948)


def _prog(key):
    if key not in _PROGS:
        if key in ('A', 'C', 'CA'):
            _PROGS[key] = (build_T(key), None)
        else:
            _PROGS[key] = build_M([key])
    return _PROGS[key]


def _run(key, in_maps):
    nc, names = _prog(key)
    if names is not None:
        in_maps = [{k: v for k, v in m.items() if k in names} for m in in_maps]
    in_maps = [{k: np.ascontiguousarray(v, dtype=np.float32) for k, v in m.items()} for m in in_maps]
    return run_bass_kernel_spmd(nc, in_maps, core_ids=list(range(NCORE))).results


def _gT(g):
    return np.ascontiguousarray(g.reshape(16, 128).T)


def _bc(v, n=128):
    return np.ascontiguousarray(np.broadcast_to(np.asarray(v, np.float32)[None, :], (n, len(v))))


def _mixer_inputs(PT, j, l, w):
    hs = [2 * j, 2 * j + 1]
    rows = lambda k, n: PT[OFF[k]:OFF[k] + n]
    fox = {}
    fox['fq'] = rows('fq', 512).reshape(4, 128, S)[hs]
    fox['fk'] = rows('fk', 512).reshape(4, 128, S)[hs]
    fox['fv'] = rows('fv', 512).reshape(4, 128, S)[hs].transpose(2, 0, 1).reshape(S, 256)
    fgl = rows('fg', 4)[hs]
    fox['fgc'] = fgl.reshape(2, 32, 128).transpose(0, 2, 1)
    fox['fgr'] = fgl.reshape(2, 32, 128)
    fox['fnb'] = _bc(w['fox_fgate_b'][l][hs])
    fox['fgains'] = np.stack([w['fox_q_norm'][l], w['fox_k_norm'][l]], 1)
    sel = np.concatenate([np.arange(512 * j, 512 * j + 512), 1024 + np.arange(128 * j, 128 * j + 128),
                          1280 + np.arange(128 * j, 128 * j + 128)])
    ssd = {}
    ssd['s_xbcT'] = rows('xbc', 1536)[sel]
    ssd['s_convw'] = w['ssm_conv_w'][l][:, sel].T.reshape(6, 128, 4).transpose(1, 0, 2)
    ssd['s_convb'] = w['ssm_conv_b'][l][sel].reshape(6, 128).T
    ssd['s_z'] = rows('z', 1024)[512 * j:512 * j + 512].T
    dtr = rows('dt', 16)[8 * j:8 * j + 8].T
    ssd['s_dt'] = dtr.reshape(32, 128, 8).transpose(1, 0, 2).reshape(128, 256)
    h8 = slice(8 * j, 8 * j + 8)
    ssd['s_dtb'] = np.broadcast_to(w['ssm_dt_bias'][l][h8][None, None, :], (128, 32, 8)).reshape(128, 256)
    ssd['s_alog'] = np.broadcast_to(w['ssm_a_log'][l][h8][None, None, :], (128, 32, 8)).reshape(128, 256)
    ssd['s_dsk'] = _bc(np.repeat(w['ssm_d'][l][h8], 64))
    ssd['s_ng'] = _bc(w['ssm_norm'][l][512 * j:512 * j + 512])
    dsa, tok = dsa_prep(j, rows('cq', 512).T, rows('dk', 128).T, rows('dv', 128).T, rows('dki', 64).T, rows('dwi', 8).T,
                        w['dsa_w_uq'][l], w['dsa_w_uq_idx'][l], w['dsa_cq_norm'][l], w['dsa_q_norm'][l],
                        w['dsa_k_norm'][l], w['dsa_kidx_norm'][l])
    return fox, ssd, dsa, tok


def kernel_unfused(**w):
    x = np.asarray(w['x'], np.float32)
    w = {k: np.asarray(v, np.float32) for k, v in w.items()}
    consts = m_constants()
    cores = [(b, h) for b in range(4) for h in range(2)]
    xT = [np.ascontiguousarray(x[b, h * TOK:(h + 1) * TOK].T) for b, h in cores]

    def a_inputs(l):
        return {'g1': _gT(w['ffn1_norm'][l]), 'gm': _gT(w['mix_norm'][l]), 'wg1': w['ffn1_w_gate'][l],
                'wu1': w['ffn1_w_up'][l], 'wd1': w['ffn1_w_down'][l], 'w_in': w['w_in'][l]}

    def c_inputs(l):
        return {'g2': _gT(w['ffn2_norm'][l]), 'wg2': w['ffn2_w_gate'][l], 'wu2': w['ffn2_w_up'][l],
                'wd2': w['ffn2_w_down'][l], 'w_out': w['w_out'][l]}

    res = _run('A', [dict(a_inputs(0), xT=xT[c]) for c in range(NCORE)])
    x1T = [r['x1oT'] for r in res]
    projT = [r['projT'] for r in res]
    out = None
    for l in range(DEPTH):
        fox_in, ssd_in, dsa_in, toks = [], [], [], []
        for b in range(4):
            PT = np.concatenate([projT[2 * b][:N_IN], projT[2 * b + 1][:N_IN]], axis=1)
            for j in range(2):
                f, s_, d, tok = _mixer_inputs(PT, j, l, w)
                fox_in.append(dict(consts, **f))
                ssd_in.append(dict(consts, **s_))
                dsa_in.append(dict(consts, **d))
                toks.append(tok)
        rf = _run('fox', fox_in)
        rs = _run('ssd', ssd_in)
        rd = _run('dsa', dsa_in)
        mixT = []
        for b in range(4):
            M = np.empty((D_MODEL, S), np.float32)
            for j in range(2):
                c = 2 * b + j
                M[256 * j:256 * j + 256] = rf[c]['foxT']
                M[512 + 512 * j:1024 + 512 * j] = rs[c]['ssmo'].T
                M[1536:2048, toks[c]] = rd[c]['dsaT']
            mixT += [np.ascontiguousarray(M[:, :TOK]), np.ascontiguousarray(M[:, TOK:])]
        if l < DEPTH - 1:
            res = _run('CA', [dict(c_inputs(l), **a_inputs(l + 1), x1T=x1T[c], mixT=mixT[c]) for c in range(NCORE)])
            x1T = [r['x1oT'] for r in res]
            projT = [r['projT'] for r in res]
        else:
            res = _run('C', [dict(c_inputs(l), x1T=x1T[c], mixT=mixT[c]) for c in range(NCORE)])
            out = np.empty_like(x)
            for c, (b, h) in enumerate(cores):
                out[b, h * TOK:(h + 1) * TOK] = res[c]['x3T'].T
    return out


FM_SEGS = [(0, 1024), (1536, 1540), (2564, 4100), (4116, 4756), (4884, 4948)]
TM_SEGS = [(1024, 1536, 0), (1540, 2564, 512), (4100, 4116, 1536), (4756, 4884, 1552), (4948, 4956, 1680)]
TMC = 1688


def build_fused(depth):
    nc = bass.Bass("TRN2", target_bir_lowering=False)
    _DIN.clear()
    di = lambda n, s: _din(nc, n, s)
    dint = lambda n, s: nc.dram_tensor(n, s, F32, kind="Internal").ap()
    xin = di('xT', [D_MODEL, S])
    outT = nc.dram_tensor('outT', [D_MODEL, S], F32, kind="ExternalOutput").ap()
    xa, x2, xb = dint('i_xa', [D_MODEL, S]), dint('i_x2', [D_MODEL, S]), dint('i_xb', [D_MODEL, S])
    pj, ptm, mix = dint('i_pj', [N_INP, S]), dint('i_ptm', [S, TMC]), dint('i_mix', [D_MODEL, S])
    names = []
    gst = ExitStack()
    GS = {'sems': {}, 'cnt': {e: 0 for e in ENGINES}, 'dcnt': {e: 0 for e in ENGINES}, 'stack': gst}
    for l in range(depth):
        xsrc = xin if l == 0 else xb
        xdst = outT if l == depth - 1 else xb
        _SFX[0] = f'_A{l}'
        with ExitStack() as st:
            P = Prog(nc)
            C = t_setup(nc, P, st)
            gt = st.enter_context(_sbt(nc, 'gains', [128, 48], F32))
            gb = P.buf('gains')
            g1, gm = di(f'g1_l{l}', [128, 16]), di(f'gm_l{l}', [128, 16])
            wg1, wu1, wd1 = di(f'wg1_l{l}', [D_MODEL, D_FF]), di(f'wu1_l{l}', [D_MODEL, D_FF]), di(f'wd1_l{l}', [D_FF, D_MODEL])
            w_in = di(f'w_in_l{l}', [D_MODEL, N_IN])
            P.dma('sync', gt[:, 0:16], g1, writes=[gb])
            P.dma('sync', gt[:, 16:32], gm, writes=[gb])
            xinb = [P.buf() for _ in range(16)]
            xab = [P.buf() for _ in range(16)]
            pjb, ptb = {}, {}
            for t0 in range(0, S, TT):
                t_norm(C, xsrc, xinb, t0, gt[:, 0:16], gb)
                t_ffn_gu(C, wg1, wu1)
                t_mm_resid(C, wd1, 32, C.actT, lambda k, t: C.act_b[(k, t)], xsrc, xinb, xa, xab, t0, 0.5)
                t_norm(C, xa, xab, t0, gt[:, 16:32], gb)
                for lo, hi in FM_SEGS:
                    t_proj(C, w_in, N_IN, pj, pjb, t0, lo, hi)
                for lo, hi, off in TM_SEGS:
                    t_proj_tm(C, w_in, lo, hi, off, ptm, ptb, t0)
            P.finish(xab + list(pjb.values()) + list(ptb.values()))
            P.GS = GS
            P.emit(st)
        for part in ('fox', 'ssd', 'dsa'):
            for j in range(2):
                _SFX[0] = f'_{part}{l}{j}'
                with ExitStack() as st:
                    P = Prog(nc)
                    if part == 'fox':
                        src = {'fq': pj[OFF['fq'] + 256 * j:OFF['fq'] + 256 * j + 256, :].rearrange("(h d) s -> h d s", d=128),
                               'fk': pj[OFF['fk'] + 256 * j:OFF['fk'] + 256 * j + 256, :].rearrange("(h d) s -> h d s", d=128),
                               'fv': ptm[:, 256 * j:256 * j + 256],
                               'fgr': pj[OFF['fg'] + 2 * j:OFF['fg'] + 2 * j + 2, :].rearrange("h (j p) -> h j p", p=128),
                               'fgc': pj[OFF['fg'] + 2 * j:OFF['fg'] + 2 * j + 2, :].rearrange("h (j p) -> h p j", p=128)}
                        dst = {'foxT': mix[256 * j:256 * j + 256, :]}
                    elif part == 'ssd':
                        xo_ = OFF['xbc']

                        def rows(c, j=j, xo_=xo_):
                            r0 = (xo_ + 512 * j + 128 * c) if c < 4 else (xo_ + 1024 + 128 * j if c == 4 else xo_ + 1280 + 128 * j)
                            return pj[r0:r0 + 128, :]
                        src = {'ssd_rows': rows, 's_xbcT': pj[0:768, :], 's_z': ptm[:, 512 + 512 * j:1024 + 512 * j],
                               's_dt': ptm[:, 1536 + 8 * j:1544 + 8 * j]}
                        dst = {'ssmo': mix[512 + 512 * j:1024 + 512 * j, :]}
                    else:
                        src = {'d_cqT': pj[OFF['cq']:OFF['cq'] + 512, :], 'd_kT': pj[OFF['dk']:OFF['dk'] + 128, :],
                               'd_v': ptm[:, 1552:1680], 'd_kiT2': pj[OFF['dki']:OFF['dki'] + 64, :], 'd_wi': ptm[:, 1680:1688]}
                        dst = {'dsaT': mix[1536:2048, :]}
                    C = m_setup(nc, P, st, 3 if part == 'dsa' else 4, src, dst, (l, j), nt512=(1 if part == 'ssd' else (7 if part == 'fox' else 5)))
                    m_consts2(C)
                    C.pT = Pool(P, st, nc, 'pT_', 8 if part == 'fox' else 5, [128, 512], BF16)
                    if part == 'fox':
                        m_fox(C)
                    elif part == 'ssd':
                        m_ssd(C)
                    else:
                        C.psT = Pool(P, st, nc, 'psT', 1, [128, 512], BF16, psum=True)
                        m_dsa(C)
                    P.finish(C.outs)
                    P.GS = GS
                    P.emit(st)
                    names += C.in_names
        _SFX[0] = f'_C{l}'
        with ExitStack() as st:
            P = Prog(nc)
            C = t_setup(nc, P, st)
            gt = st.enter_context(_sbt(nc, 'gains', [128, 48], F32))
            gb = P.buf('gains')
            g2 = di(f'g2_l{l}', [128, 16])
            w_out = di(f'w_out_l{l}', [D_MODEL, D_MODEL])
            wg2, wu2, wd2 = di(f'wg2_l{l}', [D_MODEL, D_FF]), di(f'wu2_l{l}', [D_MODEL, D_FF]), di(f'wd2_l{l}', [D_FF, D_MODEL])
            P.dma('sync', gt[:, 32:48], g2, writes=[gb])
            xab = [P.buf() for _ in range(16)]
            x2b = [P.buf() for _ in range(16)]
            x3b = [P.buf() for _ in range(16)]
            for t0 in range(0, S, TT):
                t_loadT(C, mix, t0)
                t_mm_resid(C, w_out, 16, C.hT, lambda k, t: C.hT_b[k], xa, xab, x2, x2b, t0, 1.0)
                t_norm(C, x2, x2b, t0, gt[:, 32:48], gb)
                t_ffn_gu(C, wg2, wu2)
                t_mm_resid(C, wd2, 32, C.actT, lambda k, t: C.act_b[(k, t)], x2, x2b, xdst, x3b, t0, 0.5)
            P.finish(x3b)
            P.GS = GS
            P.emit(st)
    gst.close()
    return nc, sorted(set(names))


def kernel_fused(w, depth=DEPTH):
    x = np.asarray(w['x'], np.float32)
    w = {k: np.asarray(v, np.float32) for k, v in w.items()}
    key = ('fused', depth)
    if key not in _PROGS:
        _PROGS[key] = build_fused(depth)
    nc, names = _PROGS[key]
    base = dict(m_constants())
    dummy = np.zeros((S, 1), np.float32)
    for l in range(depth):
        base.update({f'g1_l{l}': _gT(w['ffn1_norm'][l]), f'gm_l{l}': _gT(w['mix_norm'][l]), f'g2_l{l}': _gT(w['ffn2_norm'][l]),
                     f'wg1_l{l}': w['ffn1_w_gate'][l], f'wu1_l{l}': w['ffn1_w_up'][l], f'wd1_l{l}': w['ffn1_w_down'][l],
                     f'w_in_l{l}': w['w_in'][l], f'w_out_l{l}': w['w_out'][l],
                     f'wg2_l{l}': w['ffn2_w_gate'][l], f'wu2_l{l}': w['ffn2_w_up'][l], f'wd2_l{l}': w['ffn2_w_down'][l]})
        for j in range(2):
            h8 = slice(8 * j, 8 * j + 8)
            hs = [2 * j, 2 * j + 1]
            sel = np.concatenate([np.arange(512 * j, 512 * j + 512), 1024 + np.arange(128 * j, 128 * j + 128),
                                  1280 + np.arange(128 * j, 128 * j + 128)])
            sfx = f'_l{l}_j{j}'
            base['fnb' + sfx] = _bc(w['fox_fgate_b'][l][hs])
            base['fgains' + sfx] = np.stack([w['fox_q_norm'][l], w['fox_k_norm'][l]], 1)
            base['s_convw' + sfx] = w['ssm_conv_w'][l][:, sel].T.reshape(6, 128, 4).transpose(1, 0, 2)
            base['s_convb' + sfx] = w['ssm_conv_b'][l][sel].reshape(6, 128).T
            base['s_dtb' + sfx] = np.broadcast_to(w['ssm_dt_bias'][l][h8][None, None, :], (128, 32, 8)).reshape(128, 256)
            base['s_alog' + sfx] = np.broadcast_to(w['ssm_a_log'][l][h8][None, None, :], (128, 32, 8)).reshape(128, 256)
            base['s_dsk' + sfx] = _bc(np.repeat(w['ssm_d'][l][h8], 64))
            base['s_ng' + sfx] = _bc(w['ssm_norm'][l][512 * j:512 * j + 512])
            z = np.zeros((S, 1), np.float32)
            prep, _ = dsa_prep(j, np.zeros((S, 512), np.float32), np.zeros((S, 128), np.float32), np.zeros((S, 128), np.float32),
                               np.zeros((S, 64), np.float32), np.zeros((S, 8), np.float32),
                               w['dsa_w_uq'][l], w['dsa_w_uq_idx'][l], w['dsa_cq_norm'][l], w['dsa_q_norm'][l],
                               w['dsa_k_norm'][l], w['dsa_kidx_norm'][l])
            for k in ('d_wuq', 'd_wuqi', 'd_gcq', 'd_gmisc'):
                base[k + sfx] = prep[k]
            for k in PER_J:
                base[k + f'_j{j}'] = prep[k]
            for k in GLOBAL_IN:
                base[k] = prep[k]
    in_maps = []
    for b in range(4):
        m = {k: np.ascontiguousarray(v, dtype=np.float32) for k, v in base.items() if k in names or not (k.startswith('c_') or k.startswith('d_') or k.startswith('s_') or k.startswith('f'))}
        m['xT'] = np.ascontiguousarray(x[b].T)
        in_maps.append(m)
    res = run_bass_kernel_spmd(nc, in_maps, core_ids=list(range(4))).results
    out = np.empty_like(x)
    for b in range(4):
        out[b] = res[b]['outT'].T
    return out


def kernel(**w):
    return kernel_fused(w, DEPTH)
```

```python
import numpy as np
from contextlib import ExitStack
import concourse.bass as bass
import concourse.mybir as mybir
from concourse.bass_utils import run_bass_kernel_spmd

F32 = mybir.dt.float32
BF16 = mybir.dt.bfloat16
AF = mybir.ActivationFunctionType
ALU = mybir.AluOpType
AX = mybir.AxisListType

ENGINES = ['tensor', 'vector', 'scalar', 'gpsimd', 'sync']
_SFX = ['']
_DIN = {}


def _sbt(nc, name, shape, dt):
    return nc.sbuf_tensor(name + _SFX[0], shape, dt)


def _pst(nc, name, shape, dt):
    return nc.psum_tensor(name + _SFX[0], shape, dt)


def _din(nc, name, shape):
    if name not in _DIN:
        _DIN[name] = nc.dram_tensor(name, shape, F32, kind="ExternalInput").ap()
    return _DIN[name]
INORDER_SAFE = {'tensor'}
SEM_LIMIT = 30000
NDMASEM = 6

D_MODEL = 2048
D_FF = 4096
N_IN = 4956
N_INP = 4992
NCORE = 8
TOK = 2048
TT = 1024


class Buf:
    __slots__ = ('name', 'w', 'r', 'rd')

    def __init__(self, name=''):
        self.name = name
        self.w = None
        self.r = {}
        self.rd = []


class Op:
    __slots__ = ('eng', 'fn', 'deps', 'dma', 'ms', 'ev')

    def __init__(self, eng, fn, dma):
        self.eng = eng
        self.fn = fn
        self.dma = dma
        self.deps = []
        self.ms = False
        self.ev = None


class Prog:
    def __init__(self, nc):
        self.nc = nc
        self.ops = {e: [] for e in ENGINES}
        self.dmaops = {e: [] for e in ENGINES}
        self.nbuf = 0
        self.clear_sems = False
        self.GS = None

    def buf(self, name=''):
        self.nbuf += 1
        return Buf(name or f'b{self.nbuf}')

    def add(self, eng, fn, reads=(), writes=(), dma=False):
        op = Op(eng, fn, dma)
        deps = {}

        def need(d, war=False):
            if d is None or d is op:
                return
            if (not d.dma) and (not dma) and d.eng == eng:
                if eng in INORDER_SAFE:
                    return
            deps[id(d)] = d

        for b in reads:
            need(b.w)
        for b in writes:
            need(b.w)
            for r in b.r.values():
                need(r, war=True)
            for r in b.rd:
                need(r)
        if dma:
            lst = self.dmaops[eng]
            if len(lst) >= NDMASEM:
                need(lst[len(lst) - NDMASEM])
            lst.append(op)
        op.deps = list(deps.values())
        for b in reads:
            if dma:
                b.rd.append(op)
            else:
                b.r[eng] = op
        for b in writes:
            b.w = op
            b.r = {}
            b.rd = []
        self.ops[eng].append(op)
        return op

    def finish(self, reads=()):
        op = self.add('sync', None, reads=reads)
        have = {id(d) for d in op.deps}
        for e in ENGINES:
            for d in self.dmaops[e][-NDMASEM:]:
                if id(d) not in have:
                    op.deps.append(d)
            if e != 'sync':
                last = [o for o in self.ops[e] if not o.dma and o.fn is not None]
                if last and id(last[-1]) not in have:
                    op.deps.append(last[-1])
        return op

    def mm(self, out, lhsT, rhs, start=True, stop=True, reads=(), writes=(), **kw):
        return self.add('tensor', lambda e: e.matmul(out, lhsT, rhs, start=start, stop=stop, **kw),
                        reads, writes)

    def dma(self, q, out, in_, reads=(), writes=(), **kw):
        return self.add(q, lambda e: e.dma_start(out=out, in_=in_, **kw), reads, writes, dma=True)

    def V(self, fn, reads=(), writes=()):
        return self.add('vector', fn, reads, writes)

    def S(self, fn, reads=(), writes=()):
        return self.add('scalar', fn, reads, writes)

    def G(self, fn, reads=(), writes=()):
        return self.add('gpsimd', fn, reads, writes)

    def emit(self, stack):
        nc = self.nc
        for e in ENGINES:
            for op in self.ops[e]:
                for d in op.deps:
                    if not d.dma:
                        d.ms = True
        G = self.GS if self.GS is not None else {'sems': {}, 'cnt': {e: 0 for e in ENGINES}, 'dcnt': {e: 0 for e in ENGINES}, 'stack': stack}
        semcache = G['sems']

        def getsem(name):
            if name not in semcache:
                semcache[name] = G['stack'].enter_context(nc.semaphore(name))
            return semcache[name]

        DLIM = SEM_LIMIT // 16
        for e in ENGINES:
            cnt = G['cnt'][e]
            for op in self.ops[e]:
                if op.dma:
                    continue
                if op.ms:
                    ep, v = divmod(cnt, SEM_LIMIT)
                    op.ev = (getsem(f'p_{e}_{ep}'), v + 1)
                    cnt += 1
            G['cnt'][e] = cnt
            base = G['dcnt'][e]
            for i, op in enumerate(self.dmaops[e]):
                gi = base + i
                ep, v = divmod(gi // NDMASEM, DLIM)
                op.ev = (getsem(f'd_{e}_{gi % NDMASEM}_{ep}'), 16 * (v + 1))
            nd = base + len(self.dmaops[e])
            G['dcnt'][e] = nd
        block = stack.enter_context(nc.Block())
        prog = self

        def run(e, eng):
            known = {}
            for op in prog.ops[e]:
                w = {}
                for d in op.deps:
                    s, v = d.ev
                    k = id(s)
                    if known.get(k, 0) >= v:
                        continue
                    if k not in w or w[k][1] < v:
                        w[k] = (s, v)
                for k, (s, v) in w.items():
                    eng.wait_ge(s, v)
                    known[k] = v
                if op.fn is None:
                    continue
                inst = op.fn(eng)
                if op.dma:
                    inst.then_inc(op.ev[0], 16)
                elif op.ms:
                    inst.then_inc(op.ev[0], 1)

        if self.ops['sync']:
            @block.sync
            def _(eng):
                run('sync', eng)
        if self.ops['tensor']:
            @block.tensor
            def _(eng):
                run('tensor', eng)
        if self.ops['vector']:
            @block.vector
            def _(eng):
                run('vector', eng)
        if self.ops['scalar']:
            @block.scalar
            def _(eng):
                run('scalar', eng)
        if self.ops['gpsimd']:
            @block.gpsimd
            def _(eng):
                run('gpsimd', eng)


class Pool:
    def __init__(self, P, st, nc, name, n, shape, dtype, psum=False):
        self.tiles = []
        self.bufs = []
        self.i = 0
        for i in range(n):
            if psum:
                t = st.enter_context(_pst(nc, f'{name}{i}', shape, dtype))
            else:
                t = st.enter_context(_sbt(nc, f'{name}{i}', shape, dtype))
            self.tiles.append(t)
            self.bufs.append(P.buf(f'{name}{i}'))

    def next(self):
        i = self.i % len(self.tiles)
        self.i += 1
        return self.tiles[i], self.bufs[i]


class TCtx:
    pass


def t_setup(nc, P, st):
    C = TCtx()
    C.nc, C.P = nc, P
    C.hT = st.enter_context(_sbt(nc, 'hT', [128, 16, TT], BF16))
    C.hT_b = [P.buf(f'hT{k}') for k in range(16)]
    C.actT = st.enter_context(_sbt(nc, 'actT', [128, 32, TT], BF16))
    C.act_b = {(f, t): P.buf(f'act{f}_{t}') for f in range(32) for t in range(TT // 512)}
    C.wslots = Pool(P, st, nc, 'wsl', 3, [128, 8192], BF16)
    C.xs = Pool(P, st, nc, 'xs', 3, [128, TT], F32)
    C.sq = Pool(P, st, nc, 'sq', 2, [128, TT], F32)
    C.xo = Pool(P, st, nc, 'xo', 3, [128, 512], F32)
    C.xr = Pool(P, st, nc, 'xr', 3, [128, 512], F32)
    C.sg = Pool(P, st, nc, 'sg', 3, [128, 512], F32)
    C.rstd = st.enter_context(_sbt(nc, 'rstd', [128, TT], F32))
    C.rstd_b = P.buf('rstd')
    C.ones = st.enter_context(_sbt(nc, 'onesf', [128, 128], F32))
    C.ones_b = P.buf('ones')
    C.ps = Pool(P, st, nc, 'ps', 8, [128, 512], F32, psum=True)
    P.V(lambda e: e.memset(C.ones[:], 1.0 / D_MODEL), writes=[C.ones_b])
    C.eps_t = st.enter_context(_sbt(nc, 'eps_t', [128, 1], F32))
    P.V(lambda e: e.memset(C.eps_t[:], 1e-6), writes=[C.ones_b])
    C.cp = 0
    return C


def t_norm(C, x_dram, xbufs, t0, g_tile, g_buf, eps=1e-6):
    P = C.P
    nt = TT // 512
    pss = [C.ps.next() for _ in range(nt)]
    for c in range(16):
        xs, xsb = C.xs.next()
        P.dma('sync', xs[:], x_dram[c * 128:(c + 1) * 128, t0:t0 + TT], reads=[xbufs[c]], writes=[xsb])
        sq, sqb = C.sq.next()
        P.S(lambda e, sq=sq, xs=xs: e.activation(sq[:], xs[:], AF.Square), reads=[xsb], writes=[sqb])
        for t in range(nt):
            ps, psb = pss[t]
            P.mm(ps[:], C.ones[:], sq[:, t * 512:(t + 1) * 512], start=(c == 0), stop=(c == 15),
                 reads=[C.ones_b, sqb], writes=[psb])
    for t in range(nt):
        ps, psb = pss[t]
        sg, sgb = C.sg.next()
        P.S(lambda e, ps=ps, sg=sg: e.activation(sg[:], ps[:], AF.Ln, bias=C.eps_t[:, 0:1]), reads=[psb, C.ones_b], writes=[sgb])
        P.S(lambda e, sg=sg, t=t: e.activation(C.rstd[:, t * 512:(t + 1) * 512], sg[:], AF.Exp, scale=-0.5),
            reads=[sgb], writes=[C.rstd_b])
    for c in range(16):
        xs, xsb = C.xs.next()
        P.dma('sync', xs[:], x_dram[c * 128:(c + 1) * 128, t0:t0 + TT], reads=[xbufs[c]], writes=[xsb])
        P.V(lambda e, xs=xs, c=c: e.scalar_tensor_tensor(C.hT[:, c, :], xs[:], g_tile[:, c:c + 1], C.rstd[:],
                                                         ALU.mult, ALU.mult),
            reads=[xsb, C.rstd_b, g_buf], writes=[C.hT_b[c]])


def t_loadT(C, src_dram, t0):
    P = C.P
    for c in range(16):
        P.dma('gpsimd', C.hT[:, c, :], src_dram[c * 128:(c + 1) * 128, t0:t0 + TT], writes=[C.hT_b[c]])


def t_wjob(C, Ws, nk, c0, cw, CB):
    P = C.P
    slot, _ = C.wslots.next()
    idx = (C.wslots.i - 1) % len(C.wslots.tiles)
    if not hasattr(C, 'wpb'):
        C.wpb = {}
    RG = 1024

    def reg(col):
        key = (idx, col // RG)
        if key not in C.wpb:
            C.wpb[key] = P.buf(f'w{key}')
        return C.wpb[key]
    views = []
    KP = 4
    for wi, W in enumerate(Ws):
        base = wi * nk * CB
        v = slot[:, base:base + nk * CB].rearrange("p (k c) -> p k c", c=CB)
        views.append(v)
        for k0 in range(0, nk, KP):
            a, b = base + k0 * CB, base + (k0 + KP) * CB
            wr = [reg(c) for c in range(a, b, RG)]
            src = W[k0 * 128:(k0 + KP) * 128, c0:c0 + cw].rearrange("(k p) c -> p k c", p=128)
            P.dma('gpsimd', v[:, k0:k0 + KP, :cw], src, writes=wr)

    class PB:
        def __init__(self, wi):
            self.wi = wi

        def __getitem__(self, kk):
            return None
    rb = lambda wi, k: reg(wi * nk * CB + k * CB)
    return views, rb, KP


def t_ffn_gu(C, wg, wu):
    P = C.P
    nt = TT // 512
    CB = 256
    for cb in range(D_FF // CB):
        views, pbufs, KP = t_wjob(C, [wg, wu], 16, cb * CB, CB, CB)
        for m in range(CB // 128):
            f = cb * (CB // 128) + m
            for t in range(nt):
                pg, pgb = C.ps.next()
                pu, pub = C.ps.next()
                for wi, (ps, psb) in enumerate([(pg, pgb), (pu, pub)]):
                    for k in range(16):
                        P.mm(ps[:], views[wi][:, k, m * 128:(m + 1) * 128], C.hT[:, k, t * 512:(t + 1) * 512],
                             start=(k == 0), stop=(k == 15),
                             reads=[pbufs(wi, k), C.hT_b[k]], writes=[psb])
                sg, sgb = C.sg.next()
                P.S(lambda e, sg=sg, pg=pg: e.activation(sg[:], pg[:], AF.Silu), reads=[pgb], writes=[sgb])
                P.V(lambda e, sg=sg, pu=pu, f=f, t=t: e.tensor_tensor(
                    C.actT[:, f, t * 512:(t + 1) * 512], sg[:], pu[:], ALU.mult),
                    reads=[sgb, pub], writes=[C.act_b[(f, t)]])


def t_mm_resid(C, W, nk, rhs, rhs_bufs, x_dram, xbufs, o_dram, obufs, t0, fac):
    P = C.P
    nt = TT // 512
    CB = 8192 // nk
    for cb in range(D_MODEL // CB):
        views, pbufs, KP = t_wjob(C, [W], nk, cb * CB, CB, CB)
        for m in range(CB // 128):
            c = cb * (CB // 128) + m
            for t in range(nt):
                ps, psb = C.ps.next()
                for k in range(nk):
                    P.mm(ps[:], views[0][:, k, m * 128:(m + 1) * 128], rhs[:, k, t * 512:(t + 1) * 512],
                         start=(k == 0), stop=(k == nk - 1),
                         reads=[pbufs(0, k), rhs_bufs(k, t)], writes=[psb])
                xr, xrb = C.xr.next()
                sl = slice(t0 + t * 512, t0 + (t + 1) * 512)
                P.dma('sync', xr[:], x_dram[c * 128:(c + 1) * 128, sl], reads=[xbufs[c]], writes=[xrb])
                xo, xob = C.xo.next()
                P.V(lambda e, xo=xo, ps=ps, xr=xr: e.scalar_tensor_tensor(xo[:], ps[:], fac, xr[:], ALU.mult, ALU.add),
                    reads=[psb, xrb], writes=[xob])
                P.dma('sync', o_dram[c * 128:(c + 1) * 128, sl], xo[:], reads=[xob], writes=[obufs[c]])


def t_proj(C, W, ncols, o_dram, obufs, t0, c_lo=0, c_hi=None):
    P = C.P
    nt = TT // 512
    CB = 512
    c_hi = ncols if c_hi is None else c_hi
    for col0 in range(c_lo, c_hi, CB):
        cw = min(CB, c_hi - col0)
        views, pbufs, KP = t_wjob(C, [W], 16, col0, cw, CB)
        for m in range((cw + 127) // 128):
            msz = min(128, cw - m * 128)
            r0 = col0 + m * 128
            for t in range(nt):
                ps, psb = C.ps.next()
                for k in range(16):
                    P.mm(ps[:msz, :], views[0][:, k, m * 128:m * 128 + msz], C.hT[:, k, t * 512:(t + 1) * 512],
                         start=(k == 0), stop=(k == 15),
                         reads=[pbufs(0, k), C.hT_b[k]], writes=[psb])
                xo, xob = C.xo.next()
                if C.cp % 2 == 0:
                    P.S(lambda e, xo=xo, ps=ps, msz=msz: e.copy(xo[:msz, :], ps[:msz, :]), reads=[psb], writes=[xob])
                else:
                    P.V(lambda e, xo=xo, ps=ps, msz=msz: e.tensor_copy(xo[:msz, :], ps[:msz, :]), reads=[psb], writes=[xob])
                C.cp += 1
                sl = slice(t0 + t * 512, t0 + (t + 1) * 512)
                ob = obufs.setdefault(r0, P.buf())
                P.dma('sync', o_dram[r0:r0 + msz, sl], xo[:msz, :], reads=[xob], writes=[ob])


def t_proj_tm(C, W, c_lo, c_hi, tm_off, o_dram, obufs, t0):
    P = C.P
    CB = 512
    for col0 in range(c_lo, c_hi, CB):
        cw = min(CB, c_hi - col0)
        views, pbufs, KP = t_wjob(C, [W], 16, col0, cw, CB)
        for tc in range(TT // 128):
            ps, psb = C.ps.next()
            for k in range(16):
                P.mm(ps[:, :cw], C.hT[:, k, tc * 128:(tc + 1) * 128], views[0][:, k, :cw],
                     start=(k == 0), stop=(k == 15), reads=[pbufs(0, k), C.hT_b[k]], writes=[psb])
            xo, xob = C.xo.next()
            if C.cp % 2 == 0:
                P.S(lambda e, xo=xo, ps=ps, cw=cw: e.copy(xo[:, :cw], ps[:, :cw]), reads=[psb], writes=[xob])
            else:
                P.V(lambda e, xo=xo, ps=ps, cw=cw: e.tensor_copy(xo[:, :cw], ps[:, :cw]), reads=[psb], writes=[xob])
            C.cp += 1
            ob = obufs.setdefault((col0, tc), P.buf())
            o0 = tm_off + (col0 - c_lo)
            P.dma('sync', o_dram[t0 + tc * 128:t0 + (tc + 1) * 128, o0:o0 + cw], xo[:, :cw], reads=[xob], writes=[ob])


def build_T(mode):
    _SFX[0] = ''
    nc = bass.Bass("TRN2", target_bir_lowering=False)
    di = lambda n, s: nc.dram_tensor(n, s, F32, kind="ExternalInput").ap()
    do = lambda n, s: nc.dram_tensor(n, s, F32, kind="ExternalOutput").ap()
    with ExitStack() as st:
        P = Prog(nc)
        C = t_setup(nc, P, st)
        gt = st.enter_context(_sbt(nc, 'gains', [128, 48], F32))
        gb = P.buf('gains')
        hb = lambda n: [P.buf(f'{n}{c}') for c in range(16)]
        obs = []
        if 'C' in mode:
            x1 = di('x1T', [D_MODEL, TOK])
            mix = di('mixT', [D_MODEL, TOK])
            w_out = di('w_out', [D_MODEL, D_MODEL])
            g2 = di('g2', [128, 16])
            wg2, wu2, wd2 = di('wg2', [D_MODEL, D_FF]), di('wu2', [D_MODEL, D_FF]), di('wd2', [D_FF, D_MODEL])
            x2 = do('x2T', [D_MODEL, TOK])
            x3 = do('x3T', [D_MODEL, TOK])
            P.dma('sync', gt[:, 32:48], g2, writes=[gb])
            x1b, x2b, x3b = hb('x1'), hb('x2'), hb('x3')
            for t0 in range(0, TOK, TT):
                t_loadT(C, mix, t0)
                t_mm_resid(C, w_out, 16, C.hT, lambda k, t: C.hT_b[k], x1, x1b, x2, x2b, t0, 1.0)
                t_norm(C, x2, x2b, t0, gt[:, 32:48], gb)
                t_ffn_gu(C, wg2, wu2)
                t_mm_resid(C, wd2, 32, C.actT, lambda k, t: C.act_b[(k, t)], x2, x2b, x3, x3b, t0, 0.5)
            xin, xinb = x3, x3b
            obs += x3b
        if 'A' in mode:
            if 'C' not in mode:
                xin = di('xT', [D_MODEL, TOK])
                xinb = hb('xin')
            g1 = di('g1', [128, 16])
            gm = di('gm', [128, 16])
            wg1, wu1, wd1 = di('wg1', [D_MODEL, D_FF]), di('wu1', [D_MODEL, D_FF]), di('wd1', [D_FF, D_MODEL])
            w_in = di('w_in', [D_MODEL, N_IN])
            xo1 = do('x1oT', [D_MODEL, TOK])
            proj = do('projT', [N_INP, TOK])
            P.dma('sync', gt[:, 0:16], g1, writes=[gb])
            P.dma('sync', gt[:, 16:32], gm, writes=[gb])
            xo1b = hb('xo1')
            pjb = {}
            for t0 in range(0, TOK, TT):
                t_norm(C, xin, xinb, t0, gt[:, 0:16], gb)
                t_ffn_gu(C, wg1, wu1)
                t_mm_resid(C, wd1, 32, C.actT, lambda k, t: C.act_b[(k, t)], xin, xinb, xo1, xo1b, t0, 0.5)
                t_norm(C, xo1, xo1b, t0, gt[:, 16:32], gb)
                t_proj(C, w_in, N_IN, proj, pjb, t0)
            obs += xo1b + list(pjb.values())
        P.finish(obs)
        P.emit(st)
    return nc


S = 4096
NCH = S // 128
SSD_STOP = 0
NEG = -30000.0


class MCtx:
    pass


PER_J = {'d_cosaq', 'd_sinaq', 'd_cosiq', 'd_siniq', 'd_maskT', 'd_maskQ'}
GLOBAL_IN = {'d_cosak', 'd_sinak', 'd_cosik', 'd_sinik'}


def m_setup(nc, P, st, npsA=4, src=None, dst=None, lj=None, nt512=5):
    C = MCtx()
    C.nc, C.P, C.st = nc, P, st
    C.n = 0
    C.src, C.dst, C.lj = src or {}, dst or {}, lj
    C.fused = lj is not None

    def sb(name, shape, dt=F32):
        t = st.enter_context(_sbt(nc, name, shape, dt))
        return t, P.buf(name)
    C.sb = sb
    C.in_names = []

    def di(n, s):
        if n in C.src:
            return C.src[n]
        if C.fused and not (n.startswith('c_') or n in GLOBAL_IN):
            n = n + (f'_j{lj[1]}' if n in PER_J else f'_l{lj[0]}_j{lj[1]}')
        C.in_names.append(n)
        return _din(nc, n, s)
    C.di = di

    def do(n, s):
        if n in C.dst:
            return C.dst[n]
        return nc.dram_tensor(n, s, F32, kind="ExternalOutput").ap()
    C.do = do
    C.ident, C.ident_b = sb('ident', [128, 128])
    C.tri, C.tri_b = sb('tri', [128, 128])
    C.maskT, C.maskT_b = sb('maskT', [128, 128], BF16)
    C.identb, C.identb_b = sb('identb', [128, 128], BF16)
    C.su32, C.su32_b = sb('su32', [32, 32])
    C.ones, C.ones_b = sb('ones', [128, 128])
    C.onesb, C.onesb_b = sb('onesb', [128, 128], BF16)
    C.one_c, C.one_cb = sb('one_c', [128, 4])
    c_ident, c_tri, c_mask, c_su = di('c_ident', [128, 128]), di('c_tri', [128, 128]), di('c_maskT', [128, 128]), di('c_su32', [32, 32])
    P.dma('sync', C.ident[:], c_ident, writes=[C.ident_b])
    P.dma('sync', C.tri[:], c_tri, writes=[C.tri_b])
    P.dma('sync', C.su32[:], c_su, writes=[C.su32_b])
    P.dma('gpsimd', C.maskT[:], c_mask, writes=[C.maskT_b])
    P.dma('gpsimd', C.identb[:], c_ident, writes=[C.identb_b])
    P.V(lambda e: e.memset(C.ones[:], 1.0), writes=[C.ones_b])
    P.V(lambda e: e.memset(C.onesb[:], 1.0), writes=[C.onesb_b])
    P.V(lambda e: e.memset(C.one_c[:], 1.0), writes=[C.one_cb])
    C.psA = Pool(P, st, nc, 'psA', npsA, [128, 512], F32, psum=True)
    C.psO = Pool(P, st, nc, 'psO', 2, [128, 512], F32, psum=True)
    C.psD = Pool(P, st, nc, 'psD', 2, [128, 512], F32, psum=True)
    C.t512 = Pool(P, st, nc, 't512_', nt512, [128, 512], F32)
    C.u512 = Pool(P, st, nc, 'u512_', 4, [128, 512], F32)
    C.outs = []
    return C


def fm_rmsnorm(C, src, srcb, npart, n, gcol, gb, inv_d, eps, out, outb, ones=None):
    P = C.P
    sq, sqb = C.t512.next()
    P.S(lambda e: e.activation(sq[:npart, :n], src, AF.Square), reads=[srcb], writes=[sqb])
    ps, psb = C.psA.next()
    om, omb = ones if ones is not None else (C.ones, C.ones_b)
    P.mm(ps[:npart, :n], om[:npart, :npart], sq[:npart, :n], reads=[omb, sqb], writes=[psb])
    sr, srb = C.t512.next()
    P.S(lambda e: e.activation(sr[:npart, :n], ps[:npart, :n], AF.Ln, bias=C.epsc[:npart, 0:1] if eps == 1e-6 else C.epsc[:npart, 1:2],
                               scale=inv_d), reads=[psb, C.epsc_b], writes=[srb])
    rs, rsb = C.t512.next()
    P.S(lambda e: e.activation(rs[:npart, :n], sr[:npart, :n], AF.Exp, scale=-0.5), reads=[srb], writes=[rsb])
    P.V(lambda e: e.scalar_tensor_tensor(out, src, gcol, rs[:npart, :n], ALU.mult, ALU.mult),
        reads=[srcb, rsb, gb], writes=[outb])


def m_consts2(C):
    C.epsc, C.epsc_b = C.sb('epsc', [128, 2])
    C.P.V(lambda e: e.memset(C.epsc[:, 0:1], 1e-6), writes=[C.epsc_b])
    C.P.V(lambda e: e.memset(C.epsc[:, 1:2], 1e-5), writes=[C.epsc_b])
    C.k255, C.k255_b = C.sb('k255', [128, 1])
    C.P.V(lambda e: e.memset(C.k255[:], TOPK - 0.5), writes=[C.k255_b])


def m_fox(C):
    P, nc, sb, di = C.P, C.nc, C.sb, C.di
    fq, fk = di('fq', [2, 128, S]), di('fk', [2, 128, S])
    fv = di('fv', [S, 256])
    fgc, fgr = di('fgc', [2, 128, 32]), di('fgr', [2, 32, 128])
    fnb = di('fnb', [128, 2])
    fg = di('fgains', [128, 2])
    foxT = C.do('foxT', [256, S])
    v, vb = sb('fvb', [128, NCH, 256], BF16)
    P.dma('gpsimd', v[:], fv.rearrange("(j p) d -> p j d", p=128), writes=[vb])
    gn, gnb = sb('fgn', [128, 4])
    P.dma('sync', gn[:, 0:2], fg, writes=[gnb])
    P.dma('sync', gn[:, 2:4], fnb, writes=[gnb])
    gs, gsb = sb('fgs', [128, 4])
    P.V(lambda e: e.tensor_scalar(gs[:, 0:1], gn[:, 0:1], 128 ** -0.5, None, ALU.mult), reads=[gnb], writes=[gsb])
    P.V(lambda e: e.tensor_scalar(gs[:, 2:4], gn[:, 2:4], -1.0, None, ALU.mult), reads=[gnb], writes=[gsb])
    qn, kn = sb('fqn', [128, 2, S], BF16)[0], sb('fkn', [128, 2, S], BF16)[0]
    qnb = {(h, T): P.buf() for h in range(2) for T in range(8)}
    knb = {(h, T): P.buf() for h in range(2) for T in range(8)}
    negC, negCb = sb('negC', [128, 2, 32])
    obufs = []
    for h in range(2):
        xc, xcb = sb(f'fxc{h}', [128, 32])
        xr, xrb = sb(f'fxr{h}', [32, 128])
        P.dma('sync', xc[:], fgc[h], writes=[xcb], allow_slow_non_contiguous=True)
        P.dma('sync', xr[:], fgr[h], writes=[xrb])
        P.S(lambda e, xc=xc, h=h: e.activation(xc[:], xc[:], AF.Exp, bias=gs[:, 2 + h:3 + h], scale=-1.0), reads=[xcb, gsb], writes=[xcb])
        P.S(lambda e, xc=xc: e.activation(xc[:], xc[:], AF.Ln, bias=C.one_c[:, 0:1], scale=1.0), reads=[xcb, C.one_cb], writes=[xcb])
        P.S(lambda e, xr=xr, h=h: e.activation(xr[:], xr[:], AF.Exp, bias=gs[:32, 2 + h:3 + h], scale=-1.0), reads=[xrb, gsb], writes=[xrb])
        P.S(lambda e, xr=xr: e.activation(xr[:], xr[:], AF.Ln, bias=C.one_c[:32, 0:1], scale=1.0), reads=[xrb, C.one_cb], writes=[xrb])
        rs, rsb = sb(f'frs{h}', [32, 1])
        P.V(lambda e, rs=rs, xr=xr: e.reduce_sum(rs[:], xr[:], axis=AX.X), reads=[xrb], writes=[rsb])
        rm, rmb = sb(f'frm{h}', [32, 128])
        P.V(lambda e, rm=rm, rs=rs: e.tensor_scalar(rm[:], C.ones[:32, :], rs[:, 0:1], None, ALU.mult), reads=[rsb, C.ones_b], writes=[rmb])
        ps, psb = C.psA.next()
        P.mm(ps[:, :32], C.tri[:], xc[:], start=True, stop=False, reads=[C.tri_b, xcb], writes=[psb])
        P.mm(ps[:, :32], rm[:], C.su32[:], start=False, stop=True, reads=[rmb, C.su32_b], writes=[psb])
        P.V(lambda e, ps=ps, h=h: e.tensor_copy(negC[:, h, :], ps[:, :32]), reads=[psb], writes=[negCb])
        for T in range(8):
            for (src, dst, dstb, gcol) in ((fq, qn, qnb, gs[:, 0:1]), (fk, kn, knb, gn[:, 1:2])):
                raw, rawb = C.u512.next()
                P.dma('sync', raw[:], src[h, :, T * 512:(T + 1) * 512], writes=[rawb])
                fm_rmsnorm(C, raw[:], rawb, 128, 512, gcol, gsb if src is fq else gnb, 1.0 / 128, 1e-6,
                           dst[:, h, T * 512:(T + 1) * 512], dstb[(h, T)])
        for T in range(8):
            dg, dgb = C.t512.next()
            for r in range(4):
                P.V(lambda e, dg=dg, r=r, h=h, T=T: e.tensor_scalar(dg[:, r * 128:(r + 1) * 128], C.ident[:],
                                                                   negC[:, h, 4 * T + r:4 * T + r + 1], None, ALU.mult),
                    reads=[C.ident_b, negCb], writes=[dgb])
            pcb, pcbb = C.psA.next()
            P.mm(pcb[:], C.ones[:], dg[:], reads=[C.ones_b, dgb], writes=[pcbb])
            ncb, ncbb = C.u512.next()
            P.S(lambda e, ncb=ncb, pcb=pcb: e.copy(ncb[:], pcb[:]), reads=[pcbb], writes=[ncbb])
            po, pob = C.psO.next()
            pd, pdb = C.psD.next()
            nj = 4 * T + 4
            LA = 2
            staged = {}

            def stage1(j, T=T, h=h, ncb=ncb, ncbb=ncbb):
                r = j - 4 * T
                c0 = 0 if r < 0 else r * 128
                n = 512 - c0
                tsl = slice(T * 512 + c0, (T + 1) * 512)
                pS, pSb = C.psA.next()
                P.mm(pS[:, :n], kn[:, h, j * 128:(j + 1) * 128], qn[:, h, tsl], start=True, stop=(r < 0),
                     reads=[knb[(h, j // 4)], qnb[(h, T)]], writes=[pSb])
                if r >= 0:
                    P.mm(pS[:, :128], C.identb[:], C.maskT[:], start=False, stop=True,
                         reads=[C.identb_b, C.maskT_b], writes=[pSb])
                L, Lb = C.t512.next()
                P.V(lambda e, L=L, pS=pS, n=n, c0=c0: e.tensor_tensor(L[:, :n], pS[:, :n], ncb[:, c0:], ALU.subtract),
                    reads=[pSb, ncbb], writes=[Lb])
                pT, pTb = C.pT.next()
                P.S(lambda e, pT=pT, L=L, n=n, j=j: e.activation(pT[:, :n], L[:, :n], AF.Exp, bias=negC[:, h, j:j + 1], scale=1.0),
                    reads=[Lb, negCb], writes=[pTb])
                staged[j] = (pT, pTb, c0, n)

            for j in range(min(LA, nj)):
                stage1(j)
            for j in range(nj):
                if j + LA < nj:
                    stage1(j + LA)
                pT, pTb, c0, n = staged.pop(j)
                P.mm(po[:, c0:], v[:, j, h * 128:(h + 1) * 128], pT[:, :n], start=(j == 0), stop=(j == nj - 1),
                     reads=[vb, pTb], writes=[pob])
                P.mm(pd[:, c0:], C.onesb[:], pT[:, :n], start=(j == 0), stop=(j == nj - 1),
                     reads=[C.onesb_b, pTb], writes=[pdb])
            rc, rcb = C.t512.next()
            P.S(lambda e, rc=rc, pd=pd: e.activation(rc[:], pd[:], AF.Ln), reads=[pdb], writes=[rcb])
            P.S(lambda e, rc=rc: e.activation(rc[:], rc[:], AF.Exp, scale=-1.0), reads=[rcb], writes=[rcb])
            o, ob = C.u512.next()
            P.V(lambda e, o=o, po=po, rc=rc: e.tensor_tensor(o[:], po[:], rc[:], ALU.mult), reads=[pob, rcb], writes=[ob])
            hb_ = P.buf()
            P.dma('sync', foxT[h * 128:(h + 1) * 128, T * 512:(T + 1) * 512], o[:], reads=[ob], writes=[hb_])
            obufs.append(hb_)
    C.outs += obufs


def build_M(parts):
    nc = bass.Bass("TRN2", target_bir_lowering=False)
    _DIN.clear()
    _SFX[0] = ''
    with ExitStack() as st:
        P = Prog(nc)
        C = m_setup(nc, P, st, 3 if 'dsa' in parts else 4, nt512=(1 if parts == ['ssd'] else 5))
        m_consts2(C)
        C.pT = Pool(P, st, nc, 'pT_', 5, [128, 512], BF16)
        if 'dsa' in parts:
            C.psT = Pool(P, st, nc, 'psT', 1, [128, 512], BF16, psum=True)
        if 'fox' in parts:
            m_fox(C)
        if 'ssd' in parts:
            m_ssd(C)
        if 'dsa' in parts:
            m_dsa(C)
        P.finish(C.outs)
        P.emit(st)
    return nc, list(C.in_names)


def _rope_mat(n, half, bases):
    m = np.zeros((n, n), np.float32)
    for b in bases:
        for d in range(half):
            m[b + d + half, b + d] = -1.0
            m[b + d, b + d + half] = 1.0
    return m


def rope_tables(rot, nrow, reps):
    half = rot // 2
    inv = (500000.0 ** (-np.arange(0, rot, 2, dtype=np.float32) / rot)).astype(np.float32)
    ang = np.arange(S, dtype=np.float32)[:, None] * inv[None, :]
    cos, sin = np.cos(ang).astype(np.float32), np.sin(ang).astype(np.float32)
    cf = np.ones((nrow, S), np.float32)
    sf = np.zeros((nrow, S), np.float32)
    cf[:half] = cos.T
    cf[half:rot] = cos.T
    sf[:half] = sin.T
    sf[half:rot] = sin.T
    return np.tile(cf, (reps, 1)), np.tile(sf, (reps, 1))


def dsa_prep(j, d_cq, d_k, d_v, d_ki, d_wi, w_uq, w_uqi, g_cq, g_q, g_k, g_ki):
    tiles = dsa_slot_tiles(j)
    tok = np.concatenate([np.arange(t * 128, (t + 1) * 128) for t in tiles])
    ca, sa = rope_tables(32, 128, 1)
    ci, si = rope_tables(16, 64, 2)
    i = np.arange(128)
    tri = np.where(i[:, None] <= i[None, :], 0.0, NEG).astype(np.float32)
    full = np.full((128, 128), NEG, np.float32)
    zero = np.zeros((128, 128), np.float32)
    mT, mQ = [], []
    for par in range(2):
        smaller = (j == 0 and par == 0) or (j == 1 and par == 1)
        A, B = (tri, full) if smaller else (zero, tri)
        for m in (A, B):
            mT.append(np.tile(m, (1, 4)))
            mQ.append(np.ascontiguousarray(m.T))
    return {
        'd_cqT': np.ascontiguousarray(d_cq[tok].T), 'd_kT': np.ascontiguousarray(d_k.T), 'd_v': np.ascontiguousarray(d_v),
        'd_kiT2': np.ascontiguousarray(np.concatenate([d_ki.T, d_ki.T], 0)),
        'd_wi': np.ascontiguousarray(d_wi[tok].reshape(NSLOT, 128, 8).transpose(1, 0, 2).reshape(128, NSLOT * 8)),
        'd_wuq': np.ascontiguousarray(w_uq), 'd_wuqi': np.ascontiguousarray(w_uqi),
        'd_gcq': np.ascontiguousarray(g_cq.reshape(4, 128).T),
        'd_gmisc': np.ascontiguousarray(np.stack([g_q, g_k, np.concatenate([g_ki, g_ki])], 1)),
        'd_cosaq': np.ascontiguousarray(ca[:, tok]), 'd_sinaq': np.ascontiguousarray(sa[:, tok]), 'd_cosak': ca, 'd_sinak': sa,
        'd_cosiq': np.ascontiguousarray(ci[:, tok]), 'd_siniq': np.ascontiguousarray(si[:, tok]), 'd_cosik': ci, 'd_sinik': si,
        'd_maskT': np.stack(mT), 'd_maskQ': np.stack(mQ),
    }, tok


def m_constants():
    i = np.arange(128)
    return {
        'c_ident': np.eye(128, dtype=np.float32),
        'c_tri': (i[:, None] <= i[None, :]).astype(np.float32),
        'c_maskT': np.where(i[:, None] <= i[None, :], 0.0, NEG).astype(np.float32),
        'c_su32': (np.arange(32)[:, None] < np.arange(32)[None, :]).astype(np.float32),
        'c_negones': -np.ones((128, 128), np.float32),
        'c_rma': _rope_mat(128, 16, [0]),
        'c_rmi': _rope_mat(128, 8, [0, 64]),
        'c_blk64': np.kron(np.eye(2, dtype=np.float32), np.ones((64, 64), np.float32)),
        'c_pow2': np.ascontiguousarray(np.broadcast_to((0.5 ** np.arange(1, NBIS + 1)).astype(np.float32)[None, :], (128, NBIS))),
        'c_mask4': np.tile(np.where(i[:, None] <= i[None, :], 0.0, NEG).astype(np.float32), (1, 4)),
    }


def m_ssd(C):
    P, nc, sb, di = C.P, C.nc, C.sb, C.di
    xbc = di('s_xbcT', [768, S])
    cwt = di('s_convw', [128, 6, 4])
    cbt_ = di('s_convb', [128, 6])
    zt = di('s_z', [S, 512])
    dtr = di('s_dt', [128, 256])
    dtb = di('s_dtb', [128, 256])
    alg = di('s_alog', [128, 256])
    dsk = di('s_dsk', [128, 512])
    ngn = di('s_ng', [128, 512])
    c_neg1 = di('c_negones', [128, 128])
    c_mask4 = di('c_mask4', [128, 512])
    ssmo = C.do('ssmo', [S, 512])
    cw, cwb = sb('s_cw', [128, 6, 4])
    cb, cbb = sb('s_cb', [128, 6])
    P.dma('sync', cw[:], cwt, writes=[cwb])
    P.dma('sync', cb[:], cbt_, writes=[cbb])
    negones, negb = sb('s_negones', [128, 128])
    mask4, mask4b = sb('s_mask4', [128, 512])
    P.dma('sync', negones[:], c_neg1, writes=[negb])
    P.dma('sync', mask4[:], c_mask4, writes=[mask4b])
    dskt, dskb = sb('s_dskt', [128, 512])
    ngt, ngb = sb('s_ngt', [128, 512])
    P.dma('sync', dskt[:], dsk, writes=[dskb])
    P.dma('sync', ngt[:], ngn, writes=[ngb])
    if SSD_STOP == -1:
        return
    dt, dtB = sb('s_dtv', [128, 256])
    ea_, eab_ = sb('s_ealog', [128, 256])
    tb_, tbb_ = sb('s_dtbv', [128, 256])
    if C.fused:
        P.dma('sync', dt[:].rearrange("p (j h) -> p j h", h=8), dtr.rearrange("(j p) h -> p j h", p=128), writes=[dtB],
              allow_slow_non_contiguous=True)
    else:
        P.dma('sync', dt[:], dtr, writes=[dtB])
    P.dma('sync', tb_[:], dtb, writes=[tbb_])
    P.dma('sync', ea_[:], alg, writes=[eab_])
    P.V(lambda e: e.tensor_tensor(dt[:], dt[:], tb_[:], ALU.add), reads=[dtB, tbb_], writes=[dtB])
    P.S(lambda e: e.activation(dt[:], dt[:], AF.Exp), reads=[dtB], writes=[dtB])
    P.S(lambda e: e.activation(dt[:], dt[:], AF.Ln, bias=C.one_c[:, 0:1], scale=1.0), reads=[dtB, C.one_cb], writes=[dtB])
    P.S(lambda e: e.activation(ea_[:], ea_[:], AF.Exp), reads=[eab_], writes=[eab_])
    if SSD_STOP == -2:
        return
    dA, dAb = sb('s_dA', [128, 256])
    P.V(lambda e: e.scalar_tensor_tensor(dA[:], dt[:], -1.0, ea_[:], ALU.mult, ALU.mult), reads=[dtB, eab_], writes=[dAb])
    pcs, pcsb = C.psA.next()
    ptot, ptotb = C.psA.next()
    P.mm(pcs[:, :256], C.tri[:], dA[:], reads=[C.tri_b, dAb], writes=[pcsb])
    P.mm(ptot[:, :256], C.ones[:], dA[:], reads=[C.ones_b, dAb], writes=[ptotb])
    if SSD_STOP == -3:
        return
    acs, acsb = sb('s_acs', [128, 256])
    P.V(lambda e: e.tensor_copy(acs[:], pcs[:, :256]), reads=[pcsb], writes=[acsb])
    if SSD_STOP == -4:
        return
    eacs, eacsb = sb('s_eacs', [128, 256])
    P.S(lambda e: e.activation(eacs[:], acs[:], AF.Exp), reads=[acsb], writes=[eacsb])
    if SSD_STOP == -5:
        return
    cd, cdb = sb('s_cd', [128, 256])
    tots, totsb = sb('s_tots', [128, 256])
    P.V(lambda e: e.tensor_copy(tots[:], ptot[:, :256]), reads=[ptotb], writes=[totsb])
    P.S(lambda e: e.activation(cd[:], tots[:], AF.Exp), reads=[totsb], writes=[cdb])
    if SSD_STOP == -6:
        return
    dec, decb = sb('s_dec', [128, 256])
    P.V(lambda e: e.tensor_tensor(dec[:], tots[:], acs[:], ALU.subtract), reads=[totsb, acsb], writes=[decb])
    P.S(lambda e: e.activation(dec[:], dec[:], AF.Exp), reads=[decb], writes=[decb])
    if SSD_STOP == 1:
        return
    x_tm, x_tmb = sb('s_xtm', [128, NCH, 512])
    xtb = [P.buf() for _ in range(NCH)]
    B_tm, _ = sb('s_Btm', [128, NCH, 128], BF16)
    Btb = [P.buf() for _ in range(NCH)]
    BT, BTb = sb('s_BT', [128, S], BF16)
    CT, CTb = sb('s_CT', [128, S], BF16)
    raws = Pool(P, C.st, nc, 's_raw', 1, [128, S + 3], F32)
    accs = Pool(P, C.st, nc, 's_acc', 1, [128, S], F32)
    for i in range(1):
        P.V(lambda e, i=i: e.memset(raws.tiles[i][:, 0:3], 0.0), writes=[raws.bufs[i]])
    for c in range(6):
        raw, rawb = raws.next()
        P.dma('sync', raw[:, 3:], C.src['ssd_rows'](c) if C.fused else xbc[c * 128:(c + 1) * 128, :], writes=[rawb])
        acc, accb = accs.next()
        P.V(lambda e, acc=acc, raw=raw, c=c: e.tensor_scalar(acc[:], raw[:, 0:S], cw[:, c, 0:1], None, ALU.mult),
            reads=[rawb, cwb], writes=[accb])
        for k in range(1, 4):
            P.V(lambda e, acc=acc, raw=raw, c=c, k=k: e.scalar_tensor_tensor(acc[:], raw[:, k:k + S], cw[:, c, k:k + 1], acc[:],
                                                                             ALU.mult, ALU.add),
                reads=[rawb, cwb, accb], writes=[accb])
        if c == 4:
            P.S(lambda e, acc=acc, c=c: e.activation(BT[:], acc[:], AF.Silu, bias=cb[:, c:c + 1], scale=1.0), reads=[accb, cbb], writes=[BTb])
        if c == 5:
            P.S(lambda e, acc=acc, c=c: e.activation(CT[:], acc[:], AF.Silu, bias=cb[:, c:c + 1], scale=1.0), reads=[accb, cbb], writes=[CTb])
            continue
        P.S(lambda e, acc=acc, c=c: e.activation(acc[:], acc[:], AF.Silu, bias=cb[:, c:c + 1], scale=1.0), reads=[accb, cbb], writes=[accb])
        for g in range(NCH // 4):
            pt, ptb = C.psA.next()
            for jj in range(4):
                j = 4 * g + jj
                P.add('tensor', lambda e, pt=pt, acc=acc, jj=jj, j=j: e.transpose(pt[:, jj * 128:(jj + 1) * 128],
                                                                                   acc[:, j * 128:(j + 1) * 128], C.ident[:]),
                      reads=[accb, C.ident_b], writes=[ptb])
            if c < 4:
                dst = x_tm[:, 4 * g:4 * g + 4, c * 128:(c + 1) * 128]
                wb_ = xtb[4 * g:4 * g + 4]
            else:
                dst = B_tm[:, 4 * g:4 * g + 4, :]
                wb_ = Btb[4 * g:4 * g + 4]
            src = pt[:].rearrange("p (a b) -> p a b", a=4)
            if g % 2 == 0:
                P.V(lambda e, dst=dst, src=src: e.tensor_copy(dst, src), reads=[ptb], writes=wb_)
            else:
                P.S(lambda e, dst=dst, src=src: e.copy(dst, src), reads=[ptb], writes=wb_)
    if SSD_STOP == 2:
        return
    xDs = Pool(P, C.st, nc, 's_xD', 2, [128, 512], F32)
    Hf, Hfb = sb('s_Hf', [128, 512])
    Hbs = Pool(P, C.st, nc, 's_Hb', 2, [128, 512], BF16)
    Zs = Pool(P, C.st, nc, 's_Z', 2, [128, 8, 128], F32)
    Es = Pool(P, C.st, nc, 's_E', 2, [128, 1024], F32)
    SCs = Pool(P, C.st, nc, 's_SC', 2, [128, 8, 128], BF16)
    cbts = Pool(P, C.st, nc, 's_cbt', 2, [128, 128], F32)
    xdts = Pool(P, C.st, nc, 's_xdt', 2, [128, 512], BF16)
    xdds = Pool(P, C.st, nc, 's_xdd', 2, [128, 512], BF16)
    zts = Pool(P, C.st, nc, 's_zt', 2, [128, 512], F32)
    sss = Pool(P, C.st, nc, 's_ss', 2, [128, 2], F32)
    y0s = Pool(P, C.st, nc, 's_y0', 2, [128, 512], F32)
    psts = Pool(P, C.st, nc, 's_pst', 2, [128, 512], F32)
    ys = Pool(P, C.st, nc, 's_y', 2, [128, 512], F32)
    psE = [C.psO, C.psD]
    NJ = NCH if SSD_STOP == 0 else SSD_STOP - 2
    st1 = {}
    hb_prev = [None]

    def stage1(j):
        hs = slice(j * 8, (j + 1) * 8)
        xDj, xDjb = xDs.next()
        P.G(lambda e: e.tensor_tensor(xDj[:], x_tm[:, j, :], dskt[:], ALU.mult), reads=[xtb[j], dskb], writes=[xDjb])
        Z, Zb = Zs.next()
        P.V(lambda e: e.tensor_tensor(Z[:], C.tri[:].unsqueeze(1).to_broadcast([128, 8, 128]),
                                      dA[:, hs].unsqueeze(2).to_broadcast([128, 8, 128]), ALU.mult),
            reads=[C.tri_b, dAb], writes=[Zb])
        Zf = Z[:].rearrange("p h l -> p (h l)")
        E, Eb = Es.next()
        for half in range(2):
            pe, peb = psE[half].next()
            P.mm(pe[:], C.ones[:], Zf[:, half * 512:(half + 1) * 512], start=True, stop=False, reads=[C.ones_b, Zb], writes=[peb])
            for hh in range(4):
                h = half * 4 + hh
                P.mm(pe[:, hh * 128:(hh + 1) * 128], Z[:, h, :], negones[:], start=False, stop=False, reads=[Zb, negb], writes=[peb])
            P.mm(pe[:], C.ident[:], mask4[:], start=False, stop=True, reads=[C.ident_b, mask4b], writes=[peb])
            P.S(lambda e, pe=pe, half=half: e.activation(E[:, half * 512:(half + 1) * 512], pe[:], AF.Exp), reads=[peb], writes=[Eb])
        pcb_, pcbb_ = C.psA.next()
        P.mm(pcb_[:, :128], BT[:, j * 128:(j + 1) * 128], CT[:, j * 128:(j + 1) * 128], reads=[BTb, CTb], writes=[pcbb_])
        cbt, cbtb = cbts.next()
        P.V(lambda e: e.tensor_copy(cbt[:], pcb_[:, :128]), reads=[pcbb_], writes=[cbtb])
        SC, SCb = SCs.next()
        P.V(lambda e: e.tensor_tensor(SC[:], E[:].rearrange("p (h l) -> p h l", h=8),
                                      cbt[:].unsqueeze(1).to_broadcast([128, 8, 128]), ALU.mult),
            reads=[Eb, cbtb], writes=[SCb])
        xdt, xdtb = xdts.next()
        xdd, xddb = xdds.next()
        x3 = x_tm[:, j, :].rearrange("p (h d) -> p h d", h=8)
        P.V(lambda e: e.tensor_tensor(xdt[:].rearrange("p (h d) -> p h d", h=8), x3,
                                      dt[:, hs].unsqueeze(2).to_broadcast([128, 8, 64]), ALU.mult),
            reads=[xtb[j], dtB], writes=[xdtb])
        P.V(lambda e: e.tensor_tensor(xdd[:].rearrange("p (h d) -> p h d", h=8),
                                      xdt[:].rearrange("p (h d) -> p h d", h=8),
                                      dec[:, hs].unsqueeze(2).to_broadcast([128, 8, 64]), ALU.mult),
            reads=[xdtb, decb], writes=[xddb])
        py, pyb = C.psA.next()
        for h in range(8):
            P.mm(py[:, h * 64:(h + 1) * 64], SC[:, h, :], xdt[:, h * 64:(h + 1) * 64], reads=[SCb, xdtb], writes=[pyb])
        y0, y0b = y0s.next()
        P.V(lambda e: e.tensor_tensor(y0[:], py[:], xDj[:], ALU.add), reads=[pyb, xDjb], writes=[y0b])
        pstt = None
        if j < NCH - 1:
            pst, pstb = C.psA.next()
            P.mm(pst[:], B_tm[:, j, :], xdd[:], reads=[Btb[j], xddb], writes=[pstb])
            pss_, pssb = psts.next()
            P.S(lambda e: e.copy(pss_[:], pst[:]), reads=[pstb], writes=[pssb])
            pstt = (pss_, pssb)
        z, zb = zts.next()
        P.dma('sync', z[:], zt[j * 128:(j + 1) * 128, :], writes=[zb])
        P.S(lambda e: e.activation(z[:], z[:], AF.Silu), reads=[zb], writes=[zb])
        st1[j] = (y0, y0b, pstt, z, zb)

    def stage2(j):
        hs = slice(j * 8, (j + 1) * 8)
        y0, y0b, pstt, z, zb = st1.pop(j)
        y, yb = ys.next()
        if j > 0:
            Hbp, Hbpb = hb_prev[0]
            pyo, pyob = C.psA.next()
            P.mm(pyo[:], CT[:, j * 128:(j + 1) * 128], Hbp[:], reads=[CTb, Hbpb], writes=[pyob])
        if j < NCH - 1:
            pss_, pssb = pstt
            if j == 0:
                P.V(lambda e: e.tensor_copy(Hf[:], pss_[:]), reads=[pssb], writes=[Hfb])
            else:
                P.V(lambda e: e.tensor_tensor(Hf[:].rearrange("p (h d) -> p h d", h=8), Hf[:].rearrange("p (h d) -> p h d", h=8),
                                              cd[:, hs].unsqueeze(2).to_broadcast([128, 8, 64]), ALU.mult),
                    reads=[Hfb, cdb], writes=[Hfb])
                P.V(lambda e: e.tensor_tensor(Hf[:], Hf[:], pss_[:], ALU.add), reads=[Hfb, pssb], writes=[Hfb])
            Hbn, Hbnb = Hbs.next()
            P.S(lambda e: e.copy(Hbn[:], Hf[:]), reads=[Hfb], writes=[Hbnb])
            hb_prev[0] = (Hbn, Hbnb)
        if j > 0:
            P.V(lambda e: e.tensor_tensor(y[:].rearrange("p (h d) -> p h d", h=8),
                                          pyo[:].rearrange("p (h d) -> p h d", h=8),
                                          eacs[:, hs].unsqueeze(2).to_broadcast([128, 8, 64]), ALU.mult),
                reads=[pyob, eacsb], writes=[yb])
            P.V(lambda e: e.tensor_tensor(y[:], y[:], y0[:], ALU.add), reads=[yb, y0b], writes=[yb])
            ysrc, ysrcb = y, yb
        else:
            ysrc, ysrcb = y0, y0b
        P.V(lambda e: e.tensor_tensor(y[:], ysrc[:], z[:], ALU.mult), reads=[ysrcb, zb], writes=[yb])
        ss, ssb = sss.next()
        P.V(lambda e: e.memset(ss[:], 0.0), writes=[ssb])
        P.S(lambda e: e.activation(z[:], y[:], AF.Square, accum_out=ss[:, 0:1]), reads=[yb, zb, ssb], writes=[zb, ssb])
        P.S(lambda e: e.activation(ss[:, 1:2], ss[:, 0:1], AF.Ln, bias=C.epsc[:, 1:2], scale=1.0 / 512),
            reads=[ssb, C.epsc_b], writes=[ssb])
        P.S(lambda e: e.activation(ss[:, 1:2], ss[:, 1:2], AF.Exp, scale=-0.5), reads=[ssb], writes=[ssb])
        o, ob = C.u512.next()
        P.V(lambda e: e.scalar_tensor_tensor(o[:], y[:], ss[:, 1:2], ngt[:], ALU.mult, ALU.mult),
            reads=[yb, ssb, ngb], writes=[ob])
        hb_ = P.buf()
        if C.fused:
            ptr, ptrb = C.psA.next()
            for a in range(4):
                P.add('tensor', lambda e, a=a: e.transpose(ptr[:, a * 128:(a + 1) * 128], o[:, a * 128:(a + 1) * 128], C.ident[:]),
                      reads=[ob, C.ident_b], writes=[ptrb])
            oT, oTb = C.u512.next()
            P.S(lambda e: e.copy(oT[:], ptr[:]), reads=[ptrb], writes=[oTb])
            P.dma('sync', ssmo[:, j * 128:(j + 1) * 128].rearrange("(a p) t -> p a t", p=128),
                  oT[:].rearrange("p (a t) -> p a t", a=4), reads=[oTb], writes=[hb_])
        else:
            P.dma('sync', ssmo[j * 128:(j + 1) * 128, :], o[:], reads=[ob], writes=[hb_])
        C.outs.append(hb_)

    if NJ > 0:
        stage1(0)
    for j in range(NJ):
        if j + 1 < NJ:
            stage1(j + 1)
        stage2(j)


NSLOT = 16
NBIS = 14
TOPK = 256


def dsa_slot_tiles(j):
    out = []
    for m in range(8):
        out += [4 * m, 4 * m + 3] if j == 0 else [4 * m + 1, 4 * m + 2]
    return out


DSA_NCH = [(4 * (k // 2) + 2) if k % 2 == 0 else (4 * (k // 2) + 4) for k in range(NSLOT)]


def fm_rope(C, x, xb, cosd, sind, c0, n, rm, rmb, out, outb, out3=False):
    P = C.P
    ct, ctb = C.u512.next()
    st_, stb = C.u512.next()
    P.dma('sync', ct[:, :n], cosd[:, c0:c0 + n], writes=[ctb])
    P.dma('sync', st_[:, :n], sind[:, c0:c0 + n], writes=[stb])
    pr, prb = C.psA.next()
    P.mm(pr[:, :n], rm[:], x, reads=[rmb, xb], writes=[prb])
    P.V(lambda e: e.tensor_tensor(st_[:, :n], pr[:, :n], st_[:, :n], ALU.mult), reads=[prb, stb], writes=[stb])
    P.G(lambda e: e.tensor_tensor(ct[:, :n], x, ct[:, :n], ALU.mult), reads=[xb, ctb], writes=[ctb])
    if out3:
        P.V(lambda e: e.tensor_tensor(out, ct[:, :n].rearrange("p (a b) -> p a b", b=128), st_[:, :n].rearrange("p (a b) -> p a b", b=128), ALU.add),
            reads=[ctb, stb], writes=[outb])
    else:
        P.V(lambda e: e.tensor_tensor(out, ct[:, :n], st_[:, :n], ALU.add), reads=[ctb, stb], writes=[outb])


def m_dsa(C):
    P, nc, sb, di = C.P, C.nc, C.sb, C.di
    NQ = NSLOT * 128
    cq = di('d_cqT', [512, NQ])
    dk = di('d_kT', [128, S])
    dvt = di('d_v', [S, 128])
    kid = di('d_kiT2', [128, S])
    wid = di('d_wi', [128, NSLOT * 8])
    wuq = di('d_wuq', [512, 512])
    wuqi = di('d_wuqi', [512, 512])
    gcq = di('d_gcq', [128, 4])
    gmisc = di('d_gmisc', [128, 3])
    cosaq, sinaq = di('d_cosaq', [128, NQ]), di('d_sinaq', [128, NQ])
    cosak, sinak = di('d_cosak', [128, S]), di('d_sinak', [128, S])
    cosiq, siniq = di('d_cosiq', [128, NQ]), di('d_siniq', [128, NQ])
    cosik, sinik = di('d_cosik', [128, S]), di('d_sinik', [128, S])
    rma_d, rmi_d = di('c_rma', [128, 128]), di('c_rmi', [128, 128])
    blk_d = di('c_blk64', [128, 128])
    pow2_d = di('c_pow2', [128, NBIS])
    mT_d = di('d_maskT', [4, 128, 512])
    mQ_d = di('d_maskQ', [4, 128, 128])
    dsaT = C.do('dsaT', [512, NQ])
    rma, rmab = sb('d_rma', [128, 128])
    rmi, rmib = sb('d_rmi', [128, 128])
    blk, blkb = sb('d_blk', [128, 128])
    pow2, pow2b = sb('d_pow2', [128, NBIS])
    P.dma('sync', rma[:], rma_d, writes=[rmab])
    P.dma('sync', rmi[:], rmi_d, writes=[rmib])
    P.dma('sync', blk[:], blk_d, writes=[blkb])
    P.dma('sync', pow2[:], pow2_d, writes=[pow2b])
    mT, mTb = sb('d_mT', [128, 4, 512], BF16)
    mQ, mQb = sb('d_mQ', [128, 4, 128])
    for i in range(4):
        P.dma('gpsimd', mT[:, i, :], mT_d[i], writes=[mTb])
        P.dma('sync', mQ[:, i, :], mQ_d[i], writes=[mQb])
    gq, gqb = sb('d_gq', [128, 8])
    P.dma('sync', gq[:, 0:4], gcq, writes=[gqb])
    P.dma('sync', gq[:, 4:7], gmisc, writes=[gqb])
    gqs, gqsb = sb('d_gqs', [128, 1])
    P.V(lambda e: e.tensor_scalar(gqs[:], gq[:, 4:5], 128 ** -0.5, None, ALU.mult), reads=[gqb], writes=[gqsb])
    wi, wib = sb('d_wis', [128, NSLOT * 8])
    tiles_ = dsa_slot_tiles(C.lj[1]) if C.fused else None
    if C.fused:
        for k_ in range(NSLOT):
            P.dma('sync', wi[:, k_ * 8:(k_ + 1) * 8], wid[tiles_[k_] * 128:(tiles_[k_] + 1) * 128, :], writes=[wib])
    else:
        P.dma('sync', wi[:], wid, writes=[wib])
    P.V(lambda e: e.tensor_scalar(wi[:], wi[:], (8 ** -0.5) * (64 ** -0.5), None, ALU.mult), reads=[wib], writes=[wib])
    wq, wqb = sb('d_wq', [128, 4, 512], BF16)
    wqi, wqib = sb('d_wqi', [128, 4, 512], BF16)
    P.dma('gpsimd', wq[:], wuq.rearrange("(k p) c -> p k c", p=128), writes=[wqb])
    P.dma('gpsimd', wqi[:], wuqi.rearrange("(k p) c -> p k c", p=128), writes=[wqib])
    dv, dvb = sb('d_dv', [128, NCH, 128], BF16)
    P.dma('gpsimd', dv[:], dvt.rearrange("(j p) d -> p j d", p=128), writes=[dvb])
    dkT, _ = sb('d_dkT', [128, S], BF16)
    kiT, _ = sb('d_kiT', [128, S], BF16)
    dkb = [P.buf() for _ in range(8)]
    kib = [P.buf() for _ in range(8)]
    for T in range(8):
        sl = slice(T * 512, (T + 1) * 512)
        raw, rawb = C.u512.next()
        P.dma('sync', raw[:], dk[:, sl], writes=[rawb])
        nr, nrb = C.t512.next()
        fm_rmsnorm(C, raw[:], rawb, 128, 512, gq[:, 5:6], gqb, 1.0 / 128, 1e-6, nr[:], nrb)
        fm_rope(C, nr[:], nrb, cosak, sinak, T * 512, 512, rma, rmab, dkT[:, sl], dkb[T])
        raw, rawb = C.u512.next()
        if C.fused:
            P.dma('sync', raw[0:64, :], kid[:, sl], writes=[rawb])
            P.dma('sync', raw[64:128, :], kid[:, sl], writes=[rawb])
        else:
            P.dma('sync', raw[:], kid[:, sl], writes=[rawb])
        nr, nrb = C.t512.next()
        fm_rmsnorm(C, raw[:], rawb, 128, 512, gq[:, 6:7], gqb, 1.0 / 64, 1e-6, nr[:], nrb, ones=(blk, blkb))
        fm_rope(C, nr[:], nrb, cosik, sinik, T * 512, 512, rmi, rmib, kiT[:, sl], kib[T])
    cqn, _ = sb('d_cqn', [128, 4, NQ], BF16)
    cqnb = [P.buf() for _ in range(NQ // 512)]
    dqT, _ = sb('d_dqT', [128, NSLOT, 4, 128], BF16)
    dqb = [P.buf() for _ in range(NQ // 512)]
    qiT, _ = sb('d_qiT', [128, 4, NQ], BF16)
    qib = [P.buf() for _ in range(NQ // 512)]
    cqr = Pool(P, C.st, nc, 'd_cqr', 1, [128, 4, 512], F32)
    for T in range(NQ // 512):
        sl = slice(T * 512, (T + 1) * 512)
        raw, rawb = cqr.next()
        if C.fused:
            for q_ in range(4):
                tl = tiles_[4 * T + q_]
                P.dma('sync', raw[:, :, q_ * 128:(q_ + 1) * 128], cq[:, tl * 128:(tl + 1) * 128].rearrange("(k p) t -> p k t", p=128), writes=[rawb])
        else:
            P.dma('sync', raw[:], cq[:, sl].rearrange("(k p) t -> p k t", p=128), writes=[rawb])
        ps, psb = C.psA.next()
        for k in range(4):
            sq, sqb = C.t512.next()
            P.S(lambda e, sq=sq, raw=raw, k=k: e.activation(sq[:], raw[:, k, :], AF.Square), reads=[rawb], writes=[sqb])
            P.mm(ps[:], C.ones[:], sq[:], start=(k == 0), stop=(k == 3), reads=[C.ones_b, sqb], writes=[psb])
        sr, srb = C.t512.next()
        P.S(lambda e, sr=sr, ps=ps: e.activation(sr[:], ps[:], AF.Ln, bias=C.epsc[:, 0:1], scale=1.0 / 512), reads=[psb, C.epsc_b], writes=[srb])
        rs, rsb = C.t512.next()
        P.S(lambda e, rs=rs, sr=sr: e.activation(rs[:], sr[:], AF.Exp, scale=-0.5), reads=[srb], writes=[rsb])
        for k in range(4):
            P.V(lambda e, raw=raw, rs=rs, k=k, sl=sl: e.scalar_tensor_tensor(cqn[:, k, sl], raw[:, k, :], gq[:, k:k + 1], rs[:], ALU.mult, ALU.mult),
                reads=[rawb, rsb, gqb], writes=[cqnb[T]])
        for h in range(4):
            pq, pqb = C.psA.next()
            for k in range(4):
                P.mm(pq[:], wq[:, k, h * 128:(h + 1) * 128], cqn[:, k, sl], start=(k == 0), stop=(k == 3),
                     reads=[wqb, cqnb[T]], writes=[pqb])
            qf, qfb = C.u512.next()
            P.S(lambda e, qf=qf, pq=pq: e.copy(qf[:], pq[:]), reads=[pqb], writes=[qfb])
            nr, nrb = C.t512.next()
            fm_rmsnorm(C, qf[:], qfb, 128, 512, gqs[:, 0:1], gqsb, 1.0 / 128, 1e-6, nr[:], nrb)
            dst = dqT[:, 4 * T:4 * T + 4, h, :]
            fm_rope(C, nr[:], nrb, cosaq, sinaq, T * 512, 512, rma, rmab, dst, dqb[T], out3=True)
        for c in range(4):
            pq, pqb = C.psA.next()
            for k in range(4):
                P.mm(pq[:], wqi[:, k, c * 128:(c + 1) * 128], cqn[:, k, sl], start=(k == 0), stop=(k == 3),
                     reads=[wqib, cqnb[T]], writes=[pqb])
            qf, qfb = C.t512.next()
            P.S(lambda e, qf=qf, pq=pq: e.copy(qf[:], pq[:]), reads=[pqb], writes=[qfb])
            fm_rope(C, qf[:], qfb, cosiq, siniq, T * 512, 512, rmi, rmib, qiT[:, c, sl], qib[T])
    idxs = Pool(P, C.st, nc, 'd_idx', 2, [128, S], F32)
    sels = Pool(P, C.st, nc, 'd_sel', 2, [128, S], BF16)
    selTs = Pool(P, C.st, nc, 'd_selT', 2, [128, NCH, 128], BF16)
    rls = Pool(P, C.st, nc, 'd_rl', 3, [128, 512], F32)
    bst = Pool(P, C.st, nc, 'd_bs', 2, [128, 8], F32)
    Wt = Pool(P, C.st, nc, 'd_W', 2, [128, NBIS], F32)
    junks = Pool(P, C.st, nc, 'd_junk2_', 2, [128, S], BF16)

    def idx_phase(k):
        nch = DSA_NCH[k]
        n = nch * 128
        par = k % 2
        idx, idxb = idxs.next()
        for s0 in range(0, n, 512):
            sn = min(512, n - s0)
            for h in range(8):
                c, half = h // 2, h % 2
                pi, pib = C.psA.next()
                P.mm(pi[:, :sn], qiT[half * 64:(half + 1) * 64, c, k * 128:(k + 1) * 128],
                     kiT[half * 64:(half + 1) * 64, s0:s0 + sn], reads=[qib[k // 4], kib[s0 // 512]], writes=[pib])
                rl, rlb = rls.next()
                P.S(lambda e, rl=rl, pi=pi, sn=sn: e.activation(rl[:, :sn], pi[:, :sn], AF.Relu), reads=[pib], writes=[rlb])
                if h == 0:
                    P.V(lambda e, rl=rl, idx=idx, s0=s0, sn=sn, k=k, h=h: e.tensor_scalar(
                        idx[:, s0:s0 + sn], rl[:, :sn], wi[:, k * 8 + h:k * 8 + h + 1], None, ALU.mult),
                        reads=[rlb, wib], writes=[idxb])
                else:
                    P.V(lambda e, rl=rl, idx=idx, s0=s0, sn=sn, k=k, h=h: e.scalar_tensor_tensor(
                        idx[:, s0:s0 + sn], rl[:, :sn], wi[:, k * 8 + h:k * 8 + h + 1], idx[:, s0:s0 + sn], ALU.mult, ALU.add),
                        reads=[rlb, wib, idxb], writes=[idxb])
        bs, bsb = bst.next()
        W, Wb = Wt.next()
        jk, jkb = junks.next()
        return dict(k=k, nch=nch, n=n, par=par, idx=idx, idxb=idxb, bs=bs, bsb=bsb, W=W, Wb=Wb, jk=jk, jkb=jkb)

    def prep_ops(q):
        bs, bsb, idx, idxb, W, Wb, n, nch, par = q['bs'], q['bsb'], q['idx'], q['idxb'], q['W'], q['Wb'], q['n'], q['nch'], q['par']
        ops = [
            lambda: P.V(lambda e: e.tensor_reduce(bs[:, 0:1], idx[:, :n], AX.X, ALU.min), reads=[idxb], writes=[bsb]),
            lambda: P.V(lambda e: e.tensor_reduce(bs[:, 1:2], idx[:, :n], AX.X, ALU.max), reads=[idxb, bsb], writes=[bsb]),
            lambda: P.V(lambda e: e.tensor_tensor(bs[:, 1:2], bs[:, 1:2], bs[:, 0:1], ALU.subtract), reads=[bsb], writes=[bsb]),
            lambda: P.V(lambda e: e.tensor_scalar(W[:], pow2[:], bs[:, 1:2], None, ALU.mult), reads=[bsb, pow2b], writes=[Wb]),
        ]
        ops.append(lambda: P.V(lambda e: e.memset(bs[:, 5:6], float(n) - 2.0 * TOPK + 0.5), reads=[bsb], writes=[bsb]))
        for a_ in range(2):
            cc = nch - 2 + a_
            ops.append(lambda cc=cc, a_=a_: P.V(lambda e: e.tensor_tensor(idx[:, cc * 128:(cc + 1) * 128], idx[:, cc * 128:(cc + 1) * 128],
                                                                        mQ[:, par * 2 + a_, :], ALU.add),
                                                reads=[idxb, mQb], writes=[idxb]))
        return ops

    def bis_ops(q, it, on_act=False):
        bs, bsb, idx, idxb, W, Wb, n, jk, jkb = q['bs'], q['bsb'], q['idx'], q['idxb'], q['W'], q['Wb'], q['n'], q['jk'], q['jkb']
        if on_act:
            return [
                lambda: P.V(lambda e: e.tensor_tensor(bs[:, 2:3], bs[:, 0:1], W[:, it:it + 1], ALU.add), reads=[bsb, Wb], writes=[bsb]),
                lambda: P.S(lambda e: e.activation(jk[:, :n], idx[:, :n], AF.Sign, bias=bs[:, 2:3], scale=-1.0, accum_out=bs[:, 3:4]),
                            reads=[idxb, bsb, jkb], writes=[jkb, bsb]),
                lambda: P.V(lambda e: e.tensor_scalar(bs[:, 4:5], bs[:, 3:4], bs[:, 5:6], W[:, it:it + 1], ALU.is_le, ALU.mult),
                            reads=[bsb, Wb], writes=[bsb]),
                lambda: P.V(lambda e: e.tensor_tensor(bs[:, 0:1], bs[:, 0:1], bs[:, 4:5], ALU.add), reads=[bsb], writes=[bsb]),
            ]
        return [
            lambda: P.V(lambda e: e.tensor_tensor(bs[:, 2:3], bs[:, 0:1], W[:, it:it + 1], ALU.add), reads=[bsb, Wb], writes=[bsb]),
            lambda: P.V(lambda e: e.tensor_scalar(jk[:, :n], idx[:, :n], bs[:, 2:3], 0.0, ALU.is_ge, ALU.add, accum_out=bs[:, 3:4]),
                        reads=[idxb, bsb, jkb], writes=[jkb, bsb]),
            lambda: P.V(lambda e: e.tensor_scalar(bs[:, 4:5], bs[:, 3:4], C.k255[:, 0:1], W[:, it:it + 1], ALU.is_ge, ALU.mult),
                        reads=[bsb, Wb, C.k255_b], writes=[bsb]),
            lambda: P.V(lambda e: e.tensor_tensor(bs[:, 0:1], bs[:, 0:1], bs[:, 4:5], ALU.add), reads=[bsb], writes=[bsb]),
        ]

    def interleave(la, lb):
        for i in range(max(len(la), len(lb))):
            if i < len(la):
                la[i]()
            if i < len(lb):
                lb[i]()

    def post_phase(q):
        k, nch, n, par, idx, idxb, bs, bsb = q['k'], q['nch'], q['n'], q['par'], q['idx'], q['idxb'], q['bs'], q['bsb']
        sel, selb = sels.next()
        P.V(lambda e, sel=sel, idx=idx, bs=bs, n=n: e.tensor_scalar(sel[:, :n], idx[:, :n], bs[:, 0:1], None, ALU.is_ge),
            reads=[idxb, bsb], writes=[selb])
        selT, selTb = selTs.next()
        for g in range(0, nch, 4):
            gn = min(4, nch - g)
            pt, ptb = C.psT.next()
            for cc in range(gn):
                P.add('tensor', lambda e, pt=pt, sel=sel, cc=cc, g=g: e.transpose(pt[:, cc * 128:(cc + 1) * 128],
                                                                                 sel[:, (g + cc) * 128:(g + cc + 1) * 128], C.identb[:]),
                      reads=[selb, C.identb_b], writes=[ptb])
            src = pt[:, :gn * 128].rearrange("p (a b) -> p a b", a=gn)
            P.S(lambda e, selT=selT, src=src, g=g, gn=gn: e.copy(selT[:, g:g + gn, :], src), reads=[ptb], writes=[selTb])
        po, pob = C.psO.next()
        pd, pdb = C.psD.next()
        LA = 2
        staged = {}

        def stage1(c, k=k, nch=nch, par=par, selT=selT, selTb=selTb):
            pS, pSb = C.psA.next()
            last2 = c >= nch - 2
            P.mm(pS[:], dkT[:, c * 128:(c + 1) * 128], dqT[:, k, :, :].rearrange("p h t -> p (h t)"), start=True, stop=not last2,
                 reads=[dkb[c // 4], dqb[k // 4]], writes=[pSb])
            if last2:
                P.mm(pS[:], C.identb[:], mT[:, par * 2 + (c - (nch - 2)), :], start=False, stop=True,
                     reads=[C.identb_b, mTb], writes=[pSb])
            pT, pTb = C.pT.next()
            P.S(lambda e, pT=pT, pS=pS: e.activation(pT[:], pS[:], AF.Exp), reads=[pSb], writes=[pTb])
            P.V(lambda e, pT=pT, c=c: e.tensor_tensor(pT[:].rearrange("p (h t) -> p h t", h=4), pT[:].rearrange("p (h t) -> p h t", h=4),
                                                      selT[:, c, :].unsqueeze(1).to_broadcast([128, 4, 128]), ALU.mult),
                reads=[pTb, selTb], writes=[pTb])
            staged[c] = (pT, pTb)

        for c in range(min(LA, nch)):
            stage1(c)
        for c in range(nch):
            if c + LA < nch:
                stage1(c + LA)
            pT, pTb = staged.pop(c)
            P.mm(po[:], dv[:, c, :], pT[:], start=(c == 0), stop=(c == nch - 1), reads=[dvb, pTb], writes=[pob])
            P.mm(pd[:], C.onesb[:], pT[:], start=(c == 0), stop=(c == nch - 1), reads=[C.onesb_b, pTb], writes=[pdb])
        rc, rcb = C.t512.next()
        P.S(lambda e, rc=rc, pd=pd: e.activation(rc[:], pd[:], AF.Ln), reads=[pdb], writes=[rcb])
        P.S(lambda e, rc=rc: e.activation(rc[:], rc[:], AF.Exp, scale=-1.0), reads=[rcb], writes=[rcb])
        o, ob = C.u512.next()
        P.V(lambda e, o=o, po=po, rc=rc: e.tensor_tensor(o[:], po[:], rc[:], ALU.mult), reads=[pob, rcb], writes=[ob])
        hb_ = P.buf()
        kc = tiles_[k] if C.fused else k
        P.dma('sync', dsaT.rearrange("(h d) t -> d h t", d=128)[:, :, kc * 128:(kc + 1) * 128],
              o[:].rearrange("p (h t) -> p h t", h=4), reads=[ob], writes=[hb_])
        C.outs.append(hb_)

    for k0 in range(0, NSLOT, 2):
        qa, qb = idx_phase(k0), idx_phase(k0 + 1)
        interleave(prep_ops(qa), prep_ops(qb))
        for it in range(NBIS):
            interleave(bis_ops(qa, it), bis_ops(qb, it, on_act=True))
        for q in (qa, qb):
            post_phase(q)


_PROGS = {}
DEPTH = 4
OFF = dict(fq=0, fk=512, fv=1024, fg=1536, z=1540, xbc=2564, dt=4100, cq=4116, dk=4628, dv=4756, dki=4884, dwi=4948)


def _prog(key):
    if key not in _PROGS:
        if key in ('A', 'C', 'CA'):
            _PROGS[key] = (build_T(key), None)
        else:
            _PROGS[key] = build_M([key])
    return _PROGS[key]


def _run(key, in_maps):
    nc, names = _prog(key)
    if names is not None:
        in_maps = [{k: v for k, v in m.items() if k in names} for m in in_maps]
    in_maps = [{k: np.ascontiguousarray(v, dtype=np.float32) for k, v in m.items()} for m in in_maps]
    return run_bass_kernel_spmd(nc, in_maps, core_ids=list(range(NCORE))).results


def _gT(g):
    return np.ascontiguousarray(g.reshape(16, 128).T)


def _bc(v, n=128):
    return np.ascontiguousarray(np.broadcast_to(np.asarray(v, np.float32)[None, :], (n, len(v))))


def _mixer_inputs(PT, j, l, w):
    hs = [2 * j, 2 * j + 1]
    rows = lambda k, n: PT[OFF[k]:OFF[k] + n]
    fox = {}
    fox['fq'] = rows('fq', 512).reshape(4, 128, S)[hs]
    fox['fk'] = rows('fk', 512).reshape(4, 128, S)[hs]
    fox['fv'] = rows('fv', 512).reshape(4, 128, S)[hs].transpose(2, 0, 1).reshape(S, 256)
    fgl = rows('fg', 4)[hs]
    fox['fgc'] = fgl.reshape(2, 32, 128).transpose(0, 2, 1)
    fox['fgr'] = fgl.reshape(2, 32, 128)
    fox['fnb'] = _bc(w['fox_fgate_b'][l][hs])
    fox['fgains'] = np.stack([w['fox_q_norm'][l], w['fox_k_norm'][l]], 1)
    sel = np.concatenate([np.arange(512 * j, 512 * j + 512), 1024 + np.arange(128 * j, 128 * j + 128),
                          1280 + np.arange(128 * j, 128 * j + 128)])
    ssd = {}
    ssd['s_xbcT'] = rows('xbc', 1536)[sel]
    ssd['s_convw'] = w['ssm_conv_w'][l][:, sel].T.reshape(6, 128, 4).transpose(1, 0, 2)
    ssd['s_convb'] = w['ssm_conv_b'][l][sel].reshape(6, 128).T
    ssd['s_z'] = rows('z', 1024)[512 * j:512 * j + 512].T
    dtr = rows('dt', 16)[8 * j:8 * j + 8].T
    ssd['s_dt'] = dtr.reshape(32, 128, 8).transpose(1, 0, 2).reshape(128, 256)
    h8 = slice(8 * j, 8 * j + 8)
    ssd['s_dtb'] = np.broadcast_to(w['ssm_dt_bias'][l][h8][None, None, :], (128, 32, 8)).reshape(128, 256)
    ssd['s_alog'] = np.broadcast_to(w['ssm_a_log'][l][h8][None, None, :], (128, 32, 8)).reshape(128, 256)
    ssd['s_dsk'] = _bc(np.repeat(w['ssm_d'][l][h8], 64))
    ssd['s_ng'] = _bc(w['ssm_norm'][l][512 * j:512 * j + 512])
    dsa, tok = dsa_prep(j, rows('cq', 512).T, rows('dk', 128).T, rows('dv', 128).T, rows('dki', 64).T, rows('dwi', 8).T,
                        w['dsa_w_uq'][l], w['dsa_w_uq_idx'][l], w['dsa_cq_norm'][l], w['dsa_q_norm'][l],
                        w['dsa_k_norm'][l], w['dsa_kidx_norm'][l])
    return fox, ssd, dsa, tok


def kernel_unfused(**w):
    x = np.asarray(w['x'], np.float32)
    w = {k: np.asarray(v, np.float32) for k, v in w.items()}
    consts = m_constants()
    cores = [(b, h) for b in range(4) for h in range(2)]
    xT = [np.ascontiguousarray(x[b, h * TOK:(h + 1) * TOK].T) for b, h in cores]

    def a_inputs(l):
        return {'g1': _gT(w['ffn1_norm'][l]), 'gm': _gT(w['mix_norm'][l]), 'wg1': w['ffn1_w_gate'][l],
                'wu1': w['ffn1_w_up'][l], 'wd1': w['ffn1_w_down'][l], 'w_in': w['w_in'][l]}

    def c_inputs(l):
        return {'g2': _gT(w['ffn2_norm'][l]), 'wg2': w['ffn2_w_gate'][l], 'wu2': w['ffn2_w_up'][l],
                'wd2': w['ffn2_w_down'][l], 'w_out': w['w_out'][l]}

    res = _run('A', [dict(a_inputs(0), xT=xT[c]) for c in range(NCORE)])
    x1T = [r['x1oT'] for r in res]
    projT = [r['projT'] for r in res]
    out = None
    for l in range(DEPTH):
        fox_in, ssd_in, dsa_in, toks = [], [], [], []
        for b in range(4):
            PT = np.concatenate([projT[2 * b][:N_IN], projT[2 * b + 1][:N_IN]], axis=1)
            for j in range(2):
                f, s_, d, tok = _mixer_inputs(PT, j, l, w)
                fox_in.append(dict(consts, **f))
                ssd_in.append(dict(consts, **s_))
                dsa_in.append(dict(consts, **d))
                toks.append(tok)
        rf = _run('fox', fox_in)
        rs = _run('ssd', ssd_in)
        rd = _run('dsa', dsa_in)
        mixT = []
        for b in range(4):
            M = np.empty((D_MODEL, S), np.float32)
            for j in range(2):
                c = 2 * b + j
                M[256 * j:256 * j + 256] = rf[c]['foxT']
                M[512 + 512 * j:1024 + 512 * j] = rs[c]['ssmo'].T
                M[1536:2048, toks[c]] = rd[c]['dsaT']
            mixT += [np.ascontiguousarray(M[:, :TOK]), np.ascontiguousarray(M[:, TOK:])]
        if l < DEPTH - 1:
            res = _run('CA', [dict(c_inputs(l), **a_inputs(l + 1), x1T=x1T[c], mixT=mixT[c]) for c in range(NCORE)])
            x1T = [r['x1oT'] for r in res]
            projT = [r['projT'] for r in res]
        else:
            res = _run('C', [dict(c_inputs(l), x1T=x1T[c], mixT=mixT[c]) for c in range(NCORE)])
            out = np.empty_like(x)
            for c, (b, h) in enumerate(cores):
                out[b, h * TOK:(h + 1) * TOK] = res[c]['x3T'].T
    return out


FM_SEGS = [(0, 1024), (1536, 1540), (2564, 4100), (4116, 4756), (4884, 4948)]
TM_SEGS = [(1024, 1536, 0), (1540, 2564, 512), (4100, 4116, 1536), (4756, 4884, 1552), (4948, 4956, 1680)]
TMC = 1688


def build_fused(depth):
    nc = bass.Bass("TRN2", target_bir_lowering=False)
    _DIN.clear()
    di = lambda n, s: _din(nc, n, s)
    dint = lambda n, s: nc.dram_tensor(n, s, F32, kind="Internal").ap()
    xin = di('xT', [D_MODEL, S])
    outT = nc.dram_tensor('outT', [D_MODEL, S], F32, kind="ExternalOutput").ap()
    xa, x2, xb = dint('i_xa', [D_MODEL, S]), dint('i_x2', [D_MODEL, S]), dint('i_xb', [D_MODEL, S])
    pj, ptm, mix = dint('i_pj', [N_INP, S]), dint('i_ptm', [S, TMC]), dint('i_mix', [D_MODEL, S])
    names = []
    gst = ExitStack()
    GS = {'sems': {}, 'cnt': {e: 0 for e in ENGINES}, 'dcnt': {e: 0 for e in ENGINES}, 'stack': gst}
    for l in range(depth):
        xsrc = xin if l == 0 else xb
        xdst = outT if l == depth - 1 else xb
        _SFX[0] = f'_A{l}'
        with ExitStack() as st:
            P = Prog(nc)
            C = t_setup(nc, P, st)
            gt = st.enter_context(_sbt(nc, 'gains', [128, 48], F32))
            gb = P.buf('gains')
            g1, gm = di(f'g1_l{l}', [128, 16]), di(f'gm_l{l}', [128, 16])
            wg1, wu1, wd1 = di(f'wg1_l{l}', [D_MODEL, D_FF]), di(f'wu1_l{l}', [D_MODEL, D_FF]), di(f'wd1_l{l}', [D_FF, D_MODEL])
            w_in = di(f'w_in_l{l}', [D_MODEL, N_IN])
            P.dma('sync', gt[:, 0:16], g1, writes=[gb])
            P.dma('sync', gt[:, 16:32], gm, writes=[gb])
            xinb = [P.buf() for _ in range(16)]
            xab = [P.buf() for _ in range(16)]
            pjb, ptb = {}, {}
            for t0 in range(0, S, TT):
                t_norm(C, xsrc, xinb, t0, gt[:, 0:16], gb)
                t_ffn_gu(C, wg1, wu1)
                t_mm_resid(C, wd1, 32, C.actT, lambda k, t: C.act_b[(k, t)], xsrc, xinb, xa, xab, t0, 0.5)
                t_norm(C, xa, xab, t0, gt[:, 16:32], gb)
                for lo, hi in FM_SEGS:
                    t_proj(C, w_in, N_IN, pj, pjb, t0, lo, hi)
                for lo, hi, off in TM_SEGS:
                    t_proj_tm(C, w_in, lo, hi, off, ptm, ptb, t0)
            P.finish(xab + list(pjb.values()) + list(ptb.values()))
            P.GS = GS
            P.emit(st)
        for part in ('fox', 'ssd', 'dsa'):
            for j in range(2):
                _SFX[0] = f'_{part}{l}{j}'
                with ExitStack() as st:
                    P = Prog(nc)
                    if part == 'fox':
                        src = {'fq': pj[OFF['fq'] + 256 * j:OFF['fq'] + 256 * j + 256, :].rearrange("(h d) s -> h d s", d=128),
                               'fk': pj[OFF['fk'] + 256 * j:OFF['fk'] + 256 * j + 256, :].rearrange("(h d) s -> h d s", d=128),
                               'fv': ptm[:, 256 * j:256 * j + 256],
                               'fgr': pj[OFF['fg'] + 2 * j:OFF['fg'] + 2 * j + 2, :].rearrange("h (j p) -> h j p", p=128),
                               'fgc': pj[OFF['fg'] + 2 * j:OFF['fg'] + 2 * j + 2, :].rearrange("h (j p) -> h p j", p=128)}
                        dst = {'foxT': mix[256 * j:256 * j + 256, :]}
                    elif part == 'ssd':
                        xo_ = OFF['xbc']

                        def rows(c, j=j, xo_=xo_):
                            r0 = (xo_ + 512 * j + 128 * c) if c < 4 else (xo_ + 1024 + 128 * j if c == 4 else xo_ + 1280 + 128 * j)
                            return pj[r0:r0 + 128, :]
                        src = {'ssd_rows': rows, 's_xbcT': pj[0:768, :], 's_z': ptm[:, 512 + 512 * j:1024 + 512 * j],
                               's_dt': ptm[:, 1536 + 8 * j:1544 + 8 * j]}
                        dst = {'ssmo': mix[512 + 512 * j:1024 + 512 * j, :]}
                    else:
                        src = {'d_cqT': pj[OFF['cq']:OFF['cq'] + 512, :], 'd_kT': pj[OFF['dk']:OFF['dk'] + 128, :],
                               'd_v': ptm[:, 1552:1680], 'd_kiT2': pj[OFF['dki']:OFF['dki'] + 64, :], 'd_wi': ptm[:, 1680:1688]}
                        dst = {'dsaT': mix[1536:2048, :]}
                    C = m_setup(nc, P, st, 3 if part == 'dsa' else 4, src, dst, (l, j), nt512=(1 if part == 'ssd' else 5))
                    m_consts2(C)
                    C.pT = Pool(P, st, nc, 'pT_', 5, [128, 512], BF16)
                    if part == 'fox':
                        m_fox(C)
                    elif part == 'ssd':
                        m_ssd(C)
                    else:
                        C.psT = Pool(P, st, nc, 'psT', 1, [128, 512], BF16, psum=True)
                        m_dsa(C)
                    P.finish(C.outs)
                    P.GS = GS
                    P.emit(st)
                    names += C.in_names
        _SFX[0] = f'_C{l}'
        with ExitStack() as st:
            P = Prog(nc)
            C = t_setup(nc, P, st)
            gt = st.enter_context(_sbt(nc, 'gains', [128, 48], F32))
            gb = P.buf('gains')
            g2 = di(f'g2_l{l}', [128, 16])
            w_out = di(f'w_out_l{l}', [D_MODEL, D_MODEL])
            wg2, wu2, wd2 = di(f'wg2_l{l}', [D_MODEL, D_FF]), di(f'wu2_l{l}', [D_MODEL, D_FF]), di(f'wd2_l{l}', [D_FF, D_MODEL])
            P.dma('sync', gt[:, 32:48], g2, writes=[gb])
            xab = [P.buf() for _ in range(16)]
            x2b = [P.buf() for _ in range(16)]
            x3b = [P.buf() for _ in range(16)]
            for t0 in range(0, S, TT):
                t_loadT(C, mix, t0)
                t_mm_resid(C, w_out, 16, C.hT, lambda k, t: C.hT_b[k], xa, xab, x2, x2b, t0, 1.0)
                t_norm(C, x2, x2b, t0, gt[:, 32:48], gb)
                t_ffn_gu(C, wg2, wu2)
                t_mm_resid(C, wd2, 32, C.actT, lambda k, t: C.act_b[(k, t)], x2, x2b, xdst, x3b, t0, 0.5)
            P.finish(x3b)
            P.GS = GS
            P.emit(st)
    gst.close()
    return nc, sorted(set(names))


def kernel_fused(w, depth=DEPTH):
    x = np.asarray(w['x'], np.float32)
    w = {k: np.asarray(v, np.float32) for k, v in w.items()}
    key = ('fused', depth)
    if key not in _PROGS:
        _PROGS[key] = build_fused(depth)
    nc, names = _PROGS[key]
    base = dict(m_constants())
    dummy = np.zeros((S, 1), np.float32)
    for l in range(depth):
        base.update({f'g1_l{l}': _gT(w['ffn1_norm'][l]), f'gm_l{l}': _gT(w['mix_norm'][l]), f'g2_l{l}': _gT(w['ffn2_norm'][l]),
                     f'wg1_l{l}': w['ffn1_w_gate'][l], f'wu1_l{l}': w['ffn1_w_up'][l], f'wd1_l{l}': w['ffn1_w_down'][l],
                     f'w_in_l{l}': w['w_in'][l], f'w_out_l{l}': w['w_out'][l],
                     f'wg2_l{l}': w['ffn2_w_gate'][l], f'wu2_l{l}': w['ffn2_w_up'][l], f'wd2_l{l}': w['ffn2_w_down'][l]})
        for j in range(2):
            h8 = slice(8 * j, 8 * j + 8)
            hs = [2 * j, 2 * j + 1]
            sel = np.concatenate([np.arange(512 * j, 512 * j + 512), 1024 + np.arange(128 * j, 128 * j + 128),
                                  1280 + np.arange(128 * j, 128 * j + 128)])
            sfx = f'_l{l}_j{j}'
            base['fnb' + sfx] = _bc(w['fox_fgate_b'][l][hs])
            base['fgains' + sfx] = np.stack([w['fox_q_norm'][l], w['fox_k_norm'][l]], 1)
            base['s_convw' + sfx] = w['ssm_conv_w'][l][:, sel].T.reshape(6, 128, 4).transpose(1, 0, 2)
            base['s_convb' + sfx] = w['ssm_conv_b'][l][sel].reshape(6, 128).T
            base['s_dtb' + sfx] = np.broadcast_to(w['ssm_dt_bias'][l][h8][None, None, :], (128, 32, 8)).reshape(128, 256)
            base['s_alog' + sfx] = np.broadcast_to(w['ssm_a_log'][l][h8][None, None, :], (128, 32, 8)).reshape(128, 256)
            base['s_dsk' + sfx] = _bc(np.repeat(w['ssm_d'][l][h8], 64))
            base['s_ng' + sfx] = _bc(w['ssm_norm'][l][512 * j:512 * j + 512])
            z = np.zeros((S, 1), np.float32)
            prep, _ = dsa_prep(j, np.zeros((S, 512), np.float32), np.zeros((S, 128), np.float32), np.zeros((S, 128), np.float32),
                               np.zeros((S, 64), np.float32), np.zeros((S, 8), np.float32),
                               w['dsa_w_uq'][l], w['dsa_w_uq_idx'][l], w['dsa_cq_norm'][l], w['dsa_q_norm'][l],
                               w['dsa_k_norm'][l], w['dsa_kidx_norm'][l])
            for k in ('d_wuq', 'd_wuqi', 'd_gcq', 'd_gmisc'):
                base[k + sfx] = prep[k]
            for k in PER_J:
                base[k + f'_j{j}'] = prep[k]
            for k in GLOBAL_IN:
                base[k] = prep[k]
    in_maps = []
    for b in range(4):
        m = {k: np.ascontiguousarray(v, dtype=np.float32) for k, v in base.items() if k in names or not (k.startswith('c_') or k.startswith('d_') or k.startswith('s_') or k.startswith('f'))}
        m['xT'] = np.ascontiguousarray(x[b].T)
        in_maps.append(m)
    res = run_bass_kernel_spmd(nc, in_maps, core_ids=list(range(4))).results
    out = np.empty_like(x)
    for b in range(4):
        out[b] = res[b]['outT'].T
    return out


def kernel(**w):
    return kernel_fused(w, DEPTH)
```
